# Optimizing a Trainium2 kernel written in Bass

```python
import math
import jax
import jax.numpy as jnp
from jax import lax
import numpy as np

D_MODEL = 1024
BATCH = 4
SEQ = 4096
DEPTH = 2
DEC_BATCH = 128
DEC_SEQ = 4
PAST_LEN = 2048
PAGE_SIZE = 128

S5_WIDTH = D_MODEL // 2
S5_GROUP = 16
S5_GROUPS = S5_WIDTH // S5_GROUP
S5_STATE = 64
FOX_HEADS = 8
FOX_HD = 64
FOX_WIDTH = FOX_HEADS * FOX_HD
Q_BLOCK = 128
ML_HEADS = 4
ML_HD = 128
ML_WIDTH = ML_HEADS * ML_HD
ML_CHUNK = 64
N_MEM = 256
MEM_HEADS = 4
MEM_HD = 128
MEM_WIDTH = MEM_HEADS * MEM_HD
D_FF = 4 * D_MODEL
N_BRANCH = 3
EPS = 1e-6
SPLITS = (S5_WIDTH, FOX_WIDTH, FOX_WIDTH, FOX_WIDTH, FOX_HEADS, ML_WIDTH, ML_WIDTH, ML_WIDTH, ML_HEADS, ML_HEADS, ML_WIDTH, N_BRANCH * D_MODEL)
D_IN = sum(SPLITS)

kernel_name = 'hybrid_s5_fox_mlstm_decode_step'


def rmsnorm(x, g):
    xf = x.astype(jnp.float32)
    y = xf * lax.rsqrt(jnp.mean(xf * xf, axis=-1, keepdims=True) + EPS)
    return (y * g.astype(jnp.float32)).astype(x.dtype)


def split_columns(a):
    idx = [int(i) for i in np.cumsum(SPLITS)[:-1]]
    return jnp.split(a, idx, axis=-1)


def _linear_combine(e1, e2):
    a1, b1 = e1
    a2, b2 = e2
    return a2 * a1, a2 * b1 + b2


def s5_mixer(u, h0_re, h0_im, a_re, a_im, log_step, b_re, b_im, c_re, c_im, d_skip, w_glu, b_glu):
    f32 = jnp.float32
    bsz, t, _ = u.shape
    lam = lax.complex(a_re.astype(f32), a_im.astype(f32))
    dt = jnp.exp(log_step.astype(f32))[:, None]
    lam_bar = jnp.exp(lam * dt)
    b_c = lax.complex(b_re.astype(f32), b_im.astype(f32))
    b_bar = ((lam_bar - 1.0) / lam)[..., None] * b_c
    uf = u.astype(f32)
    ug = uf.reshape(bsz, t, S5_GROUPS, S5_GROUP).astype(jnp.complex64)
    bu = jnp.einsum('btgc,gpc->btgp', ug, b_bar)
    h0 = lax.complex(h0_re.astype(f32), h0_im.astype(f32))
    bu = bu.at[:, 0].add(lam_bar[None] * h0)
    lam_seq = jnp.broadcast_to(lam_bar, bu.shape)
    _, h = lax.associative_scan(_linear_combine, (lam_seq, bu), axis=1)
    c_c = lax.complex(c_re.astype(f32), c_im.astype(f32))
    y = jnp.real(jnp.einsum('btgp,gcp->btgc', h, c_c)).reshape(bsz, t, S5_WIDTH)
    y = jax.nn.gelu(y + d_skip.astype(f32) * uf)
    y = y * jax.nn.sigmoid(y @ w_glu.astype(f32) + b_glu.astype(f32))
    h_last = h[:, -1]
    return y.astype(u.dtype), jnp.real(h_last), jnp.imag(h_last)


def fox_attend(q, k, v, f_q, f_k, q_pos, k_pos):
    bsz, tq, nh, hd = q.shape
    qb = Q_BLOCK if tq % Q_BLOCK == 0 else tq
    nb = tq // qb
    scale = hd ** -0.5
    f_kT = f_k.transpose(0, 2, 1)[:, :, None, :]

    def block(xs):
        qi, fqi, pi = xs
        s = jnp.einsum('bqhd,bkhd->bhqk', qi, k).astype(jnp.float32) * scale
        s = s + fqi.transpose(0, 2, 1)[..., None] - f_kT
        s = jnp.where((pi[:, None] >= k_pos[None, :])[None, None], s, -jnp.inf)
        p = jax.nn.softmax(s, axis=-1).astype(v.dtype)
        return jnp.einsum('bhqk,bkhd->bqhd', p, v)

    qs = q.reshape(bsz, nb, qb, nh, hd).swapaxes(0, 1)
    fqs = f_q.reshape(bsz, nb, qb, nh).swapaxes(0, 1)
    ps = q_pos.reshape(nb, qb)
    o = lax.map(block, (qs, fqs, ps))
    return o.swapaxes(0, 1).reshape(bsz, tq, nh, hd)


def mlstm_chunkwise(q, k, v, ig, lf, c0, n0, m0):
    f32 = jnp.float32
    bsz, t, nh, dh = q.shape
    L = ML_CHUNK if t % ML_CHUNK == 0 else t
    nc = t // L

    def chunks(a):
        return a.astype(f32).reshape(bsz, nc, L, *a.shape[2:]).swapaxes(0, 1)

    causal = jnp.tril(jnp.ones((L, L), dtype=bool))

    def step(carry, xs):
        c, n, m = carry
        qc, kc, vc, ic, fc = xs
        bT = jnp.cumsum(fc, axis=1).transpose(0, 2, 1)
        iT = ic.transpose(0, 2, 1)
        log_d = bT[..., :, None] - bT[..., None, :] + iT[..., None, :]
        log_d = jnp.where(causal, log_d, -jnp.inf)
        log_inter = bT + m[..., None]
        m_t = jnp.maximum(log_inter, jnp.max(log_d, axis=-1))
        d_w = jnp.exp(log_d - m_t[..., None])
        inter_w = jnp.exp(log_inter - m_t)
        s = jnp.einsum('blhd,bshd->bhls', qc, kc) * d_w
        num = jnp.einsum('bhls,bshe->bhle', s, vc) + inter_w[..., None] * jnp.einsum('bhed,blhd->bhle', c, qc)
        den = jnp.sum(s, axis=-1) + inter_w * jnp.einsum('bhd,blhd->bhl', n, qc)
        h = num / jnp.maximum(jnp.abs(den), jnp.exp(-m_t))[..., None]
        w_end = d_w[:, :, -1, :]
        a_end = inter_w[:, :, -1]
        c_new = a_end[..., None, None] * c + jnp.einsum('bhs,bshe,bshd->bhed', w_end, vc, kc)
        n_new = a_end[..., None] * n + jnp.einsum('bhs,bshd->bhd', w_end, kc)
        return (c_new, n_new, m_t[:, :, -1]), h.transpose(0, 2, 1, 3)

    xs = (chunks(q), chunks(k), chunks(v), chunks(ig), chunks(lf))
    (c, n, m), hs = lax.scan(step, (c0.astype(f32), n0.astype(f32), m0.astype(f32)), xs)
    h = hs.swapaxes(0, 1).reshape(bsz, t, nh, dh)
    return h, c, n, m


def memory_kv(mem, g_mem, w_mk, w_mv, g_k):
    bsz, nm, _ = mem.shape
    mn = rmsnorm(mem, g_mem)
    k = rmsnorm((mn @ w_mk).reshape(bsz, nm, MEM_HEADS, MEM_HD), g_k)
    v = (mn @ w_mv).reshape(bsz, nm, MEM_HEADS, MEM_HD)
    return k, v


def gather_pages(pool, page_table):
    g = pool[page_table]
    return g.reshape(g.shape[0], g.shape[1] * g.shape[2], *g.shape[3:])


def hybrid_layer(x, mem_k, mem_v, fox_past, s5_h0_re, s5_h0_im, ml_c0, ml_n0, ml_m0, pos0, W):
    f32 = jnp.float32
    bsz, t, _ = x.shape
    h = rmsnorm(x, W['g_mix'])
    (u_s5, fq, fk, fv, ff, mq, mk, mv, mi, mf, mo, gates) = split_columns(h @ W['w_in'])
    y_s5, s5_re, s5_im = s5_mixer(u_s5, s5_h0_re, s5_h0_im, W['s5_a_re'], W['s5_a_im'], W['s5_log_step'],
                                  W['s5_b_re'], W['s5_b_im'], W['s5_c_re'], W['s5_c_im'], W['s5_d'],
                                  W['s5_w_glu'], W['s5_b_glu'])
    q = rmsnorm(fq.reshape(bsz, t, FOX_HEADS, FOX_HD), W['fox_gq'])
    k = rmsnorm(fk.reshape(bsz, t, FOX_HEADS, FOX_HD), W['fox_gk'])
    v = fv.reshape(bsz, t, FOX_HEADS, FOX_HD)
    logf = jax.nn.log_sigmoid((ff + W['fox_bf']).astype(f32))
    q_pos = pos0 + jnp.arange(t)
    if fox_past is None:
        f_q = jnp.cumsum(logf, axis=1)
        k_all, v_all, f_k = k, v, f_q
    else:
        k_past, v_past, logf_past = fox_past
        f_past = jnp.cumsum(logf_past.astype(f32), axis=1)
        f_q = f_past[:, -1:] + jnp.cumsum(logf, axis=1)
        k_all = jnp.concatenate([k_past.astype(k.dtype), k], axis=1)
        v_all = jnp.concatenate([v_past.astype(v.dtype), v], axis=1)
        f_k = jnp.concatenate([f_past, f_q], axis=1)
    k_pos = jnp.arange(k_all.shape[1])
    y_fox = fox_attend(q, k_all, v_all, f_q, f_k, q_pos, k_pos).reshape(bsz, t, FOX_WIDTH)
    mq_h = mq.reshape(bsz, t, ML_HEADS, ML_HD)
    mk_h = mk.reshape(bsz, t, ML_HEADS, ML_HD) * (ML_HD ** -0.5)
    mv_h = mv.reshape(bsz, t, ML_HEADS, ML_HD)
    i_pre = (mi + W['ml_bi']).astype(f32)
    log_fg = jax.nn.log_sigmoid((mf + W['ml_bf']).astype(f32))
    h_ml, c_new, n_new, m_new = mlstm_chunkwise(mq_h, mk_h, mv_h, i_pre, log_fg, ml_c0, ml_n0, ml_m0)
    y_ml = (rmsnorm(h_ml, W['ml_gn']).reshape(bsz, t, ML_WIDTH) * jax.nn.sigmoid(mo.astype(f32))).astype(x.dtype)
    g = jax.nn.sigmoid(gates.astype(f32)).reshape(bsz, t, N_BRANCH, D_MODEL)
    merged = (g[:, :, 0] * (y_s5 @ W['w_br_s5']) + g[:, :, 1] * (y_fox @ W['w_br_fox'])
              + g[:, :, 2] * (y_ml @ W['w_br_ml'])).astype(x.dtype)
    x = x + merged @ W['w_out']
    hc = rmsnorm(x, W['g_cross'])
    qc = rmsnorm((hc @ W['w_cq']).reshape(bsz, t, MEM_HEADS, MEM_HD), W['cross_gq'])
    s = jnp.einsum('bthd,bmhd->bhtm', qc, mem_k.astype(qc.dtype)).astype(f32) * (MEM_HD ** -0.5)
    p = jax.nn.softmax(s, axis=-1).astype(x.dtype)
    oc = jnp.einsum('bhtm,bmhd->bthd', p, mem_v.astype(x.dtype)).reshape(bsz, t, MEM_WIDTH)
    x = x + oc @ W['w_co']
    hm = rmsnorm(x, W['g_mlp'])
    x = x + jnp.square(jax.nn.relu(hm @ W['w_up'])) @ W['w_down']
    return x, (k, v, logf, s5_re, s5_im, c_new, n_new, m_new)


def setup_inputs(seed: int = 0) -> dict:
    key = jax.random.key(seed)
    ks = jax.random.split(key, 48)
    f32 = jnp.float32
    n_pages = PAST_LEN // PAGE_SIZE
    n_used = DEC_BATCH * n_pages
    n_pool = n_used + n_used // 4

    def nrm(i, shape, scale=1.0):
        return jax.random.normal(ks[i], shape, f32) * scale

    def gain(i, shape):
        return 1.0 + 0.02 * jax.random.normal(ks[i], shape, f32)

    page_table = jax.random.permutation(ks[6], n_pool)[:n_used].reshape(DEC_BATCH, n_pages).astype(jnp.int32)
    a_im = jnp.pi * jnp.arange(S5_STATE, dtype=f32)[None, None, :] + 0.01 * nrm(17, (DEPTH, S5_GROUPS, S5_STATE))
    return {
        'x_prompt': nrm(0, (BATCH, SEQ, D_MODEL)),
        'x_sample': nrm(1, (DEC_BATCH, DEC_SEQ, D_MODEL)),
        'mem_prompt': nrm(2, (BATCH, N_MEM, D_MODEL)),
        'cache_fox_k': nrm(3, (DEPTH, n_pool, PAGE_SIZE, FOX_HEADS, FOX_HD)),
        'cache_fox_v': nrm(4, (DEPTH, n_pool, PAGE_SIZE, FOX_HEADS, FOX_HD)),
        'cache_fox_logf': jax.nn.log_sigmoid(3.0 + 0.5 * nrm(5, (DEPTH, n_pool, PAGE_SIZE, FOX_HEADS))),
        'page_table': page_table,
        'state_s5_re': nrm(7, (DEPTH, DEC_BATCH, S5_GROUPS, S5_STATE), 0.3),
        'state_s5_im': nrm(8, (DEPTH, DEC_BATCH, S5_GROUPS, S5_STATE), 0.3),
        'state_mlstm_C': nrm(9, (DEPTH, DEC_BATCH, ML_HEADS, ML_HD, ML_HD), 0.5),
        'state_mlstm_n': nrm(10, (DEPTH, DEC_BATCH, ML_HEADS, ML_HD), 0.5),
        'state_mlstm_m': nrm(11, (DEPTH, DEC_BATCH, ML_HEADS)),
        'cache_mem_k': nrm(12, (DEPTH, DEC_BATCH, N_MEM, MEM_HEADS, MEM_HD)),
        'cache_mem_v': nrm(13, (DEPTH, DEC_BATCH, N_MEM, MEM_HEADS, MEM_HD)),
        'g_mix': gain(14, (DEPTH, D_MODEL)),
        'w_in': nrm(15, (DEPTH, D_MODEL, D_IN), D_MODEL ** -0.5),
        's5_a_re': -0.5 * jnp.exp(0.05 * nrm(16, (DEPTH, S5_GROUPS, S5_STATE))),
        's5_a_im': a_im,
        's5_log_step': jax.random.uniform(ks[18], (DEPTH, S5_GROUPS), f32, math.log(1e-3), math.log(1e-1)),
        's5_b_re': nrm(19, (DEPTH, S5_GROUPS, S5_STATE, S5_GROUP), (2.0 * S5_GROUP) ** -0.5),
        's5_b_im': nrm(20, (DEPTH, S5_GROUPS, S5_STATE, S5_GROUP), (2.0 * S5_GROUP) ** -0.5),
        's5_c_re': nrm(21, (DEPTH, S5_GROUPS, S5_GROUP, S5_STATE), (2.0 / S5_STATE) ** 0.5),
        's5_c_im': nrm(22, (DEPTH, S5_GROUPS, S5_GROUP, S5_STATE), (2.0 / S5_STATE) ** 0.5),
        's5_d': nrm(23, (DEPTH, S5_WIDTH)),
        's5_w_glu': nrm(24, (DEPTH, S5_WIDTH, S5_WIDTH), S5_WIDTH ** -0.5),
        's5_b_glu': nrm(25, (DEPTH, S5_WIDTH), 0.02),
        'fox_gq': gain(26, (DEPTH, FOX_HD)),
        'fox_gk': gain(27, (DEPTH, FOX_HD)),
        'fox_bf': 3.0 + 0.5 * nrm(28, (DEPTH, FOX_HEADS)),
        'ml_bi': nrm(29, (DEPTH, ML_HEADS), 0.1),
        'ml_bf': jnp.linspace(3.0, 6.0, ML_HEADS, dtype=f32)[None, :] + nrm(30, (DEPTH, ML_HEADS), 0.1),
        'ml_gn': gain(31, (DEPTH, ML_HD)),
        'w_br_s5': nrm(32, (DEPTH, S5_WIDTH, D_MODEL), S5_WIDTH ** -0.5),
        'w_br_fox': nrm(33, (DEPTH, FOX_WIDTH, D_MODEL), FOX_WIDTH ** -0.5),
        'w_br_ml': nrm(34, (DEPTH, ML_WIDTH, D_MODEL), ML_WIDTH ** -0.5),
        'w_out': nrm(35, (DEPTH, D_MODEL, D_MODEL), D_MODEL ** -0.5),
        'g_cross': gain(36, (DEPTH, D_MODEL)),
        'w_cq': nrm(37, (DEPTH, D_MODEL, MEM_WIDTH), D_MODEL ** -0.5),
        'cross_gq': gain(38, (DEPTH, MEM_HD)),
        'g_mem': gain(39, (DEPTH, D_MODEL)),
        'w_mk': nrm(40, (DEPTH, D_MODEL, MEM_WIDTH), D_MODEL ** -0.5),
        'w_mv': nrm(41, (DEPTH, D_MODEL, MEM_WIDTH), D_MODEL ** -0.5),
        'cross_gk': gain(42, (DEPTH, MEM_HD)),
        'w_co': nrm(43, (DEPTH, MEM_WIDTH, D_MODEL), MEM_WIDTH ** -0.5),
        'g_mlp': gain(44, (DEPTH, D_MODEL)),
        'w_up': nrm(45, (DEPTH, D_MODEL, D_FF), D_MODEL ** -0.5),
        'w_down': nrm(46, (DEPTH, D_FF, D_MODEL), D_FF ** -0.5),
    }


def reference(x_prompt, x_sample, mem_prompt, cache_fox_k, cache_fox_v, cache_fox_logf, page_table,
              state_s5_re, state_s5_im, state_mlstm_C, state_mlstm_n, state_mlstm_m, cache_mem_k, cache_mem_v,
              g_mix, w_in, s5_a_re, s5_a_im, s5_log_step, s5_b_re, s5_b_im, s5_c_re, s5_c_im, s5_d,
              s5_w_glu, s5_b_glu, fox_gq, fox_gk, fox_bf, ml_bi, ml_bf, ml_gn, w_br_s5, w_br_fox, w_br_ml,
              w_out, g_cross, w_cq, cross_gq, g_mem, w_mk, w_mv, cross_gk, w_co, g_mlp, w_up, w_down):
    f32 = jnp.float32
    past_len = page_table.shape[1] * cache_fox_k.shape[2]
    bp = x_prompt.shape[0]
    yp, ys = x_prompt, x_sample
    st_p, st_s = [], []
    for l in range(DEPTH):
        W = {
            'g_mix': g_mix[l], 'w_in': w_in[l],
            's5_a_re': s5_a_re[l], 's5_a_im': s5_a_im[l], 's5_log_step': s5_log_step[l],
            's5_b_re': s5_b_re[l], 's5_b_im': s5_b_im[l], 's5_c_re': s5_c_re[l], 's5_c_im': s5_c_im[l],
            's5_d': s5_d[l], 's5_w_glu': s5_w_glu[l], 's5_b_glu': s5_b_glu[l],
            'fox_gq': fox_gq[l], 'fox_gk': fox_gk[l], 'fox_bf': fox_bf[l],
            'ml_bi': ml_bi[l], 'ml_bf': ml_bf[l], 'ml_gn': ml_gn[l],
            'w_br_s5': w_br_s5[l], 'w_br_fox': w_br_fox[l], 'w_br_ml': w_br_ml[l], 'w_out': w_out[l],
            'g_cross': g_cross[l], 'w_cq': w_cq[l], 'cross_gq': cross_gq[l], 'w_co': w_co[l],
            'g_mlp': g_mlp[l], 'w_up': w_up[l], 'w_down': w_down[l],
        }
        mk_p, mv_p = memory_kv(mem_prompt, g_mem[l], w_mk[l], w_mv[l], cross_gk[l])
        z_s5 = jnp.zeros((bp, S5_GROUPS, S5_STATE), f32)
        yp, sp = hybrid_layer(yp, mk_p, mv_p, None, z_s5, z_s5,
                              jnp.zeros((bp, ML_HEADS, ML_HD, ML_HD), f32),
                              jnp.zeros((bp, ML_HEADS, ML_HD), f32),
                              jnp.zeros((bp, ML_HEADS), f32), 0, W)
        st_p.append(sp + (mk_p, mv_p))
        past = (gather_pages(cache_fox_k[l], page_table), gather_pages(cache_fox_v[l], page_table),
                gather_pages(cache_fox_logf[l], page_table))
        ys, ss = hybrid_layer(ys, cache_mem_k[l], cache_mem_v[l], past, state_s5_re[l], state_s5_im[l],
                              state_mlstm_C[l], state_mlstm_n[l], state_mlstm_m[l], past_len, W)
        st_s.append(ss)
    (p_fox_k, p_fox_v, p_fox_logf, p_s5_re, p_s5_im, p_ml_C, p_ml_n, p_ml_m, p_mem_k, p_mem_v) = [jnp.stack(a) for a in zip(*st_p)]
    (s_fox_k, s_fox_v, s_fox_logf, s_s5_re, s_s5_im, s_ml_C, s_ml_n, s_ml_m) = [jnp.stack(a) for a in zip(*st_s)]
    return (yp, ys, p_fox_k, p_fox_v, p_fox_logf, p_s5_re, p_s5_im, p_ml_C, p_ml_n, p_ml_m, p_mem_k, p_mem_v,
            s_fox_k, s_fox_v, s_fox_logf, s_s5_re, s_s5_im, s_ml_C, s_ml_n, s_ml_m)
```

```python
import numpy as np
import concourse.bass as bass
import concourse.mybir as mybir
from concourse.bass_utils import run_bass_kernel_spmd

F32 = mybir.dt.float32
BF16 = mybir.dt.bfloat16
I32 = mybir.dt.int32
AF = mybir.ActivationFunctionType
ALU = mybir.AluOpType

D = 1024
SEQ = 4096
DEPTH = 2
NCORE = 8
SB = 16
ST = 4
NS = SB * ST
NPAGE = 16
PAGE = 128
NPOOL = 2560
D_IN = 7184
EPS = 1e-6
C_S5, C_FQ, C_FK, C_FV, C_FF, C_MQ, C_MK, C_MV, C_MI, C_MF, C_MO, C_G = (
    0, 512, 1024, 1536, 2048, 2056, 2568, 3080, 3592, 3596, 3600, 4112)


class Buf:
    def __init__(self, t):
        self.t = t
        self.w = []
        self.r = {}

    def __getitem__(self, idx):
        return self.t[idx]


class K:
    def __init__(self, nc):
        self.nc = nc
        self.eng = {'pe': nc.tensor, 'act': nc.scalar, 'dve': nc.vector, 'pool': nc.gpsimd, 'sp': nc.sync}
        self.sem = {e: nc.alloc_semaphore('sem_' + e) for e in self.eng}
        self.cnt = {e: 0 for e in self.eng}
        self.seen = {e: {} for e in self.eng}
        self.semobj = dict(self.sem)
        self.ndma = 40
        self.dsem = [nc.alloc_semaphore('dsem%d' % i) for i in range(self.ndma)]
        for i, s in enumerate(self.dsem):
            self.semobj['d%d' % i] = s
        self.dcnt = [0] * self.ndma
        self.dnext = 0
        self.out_waits = []
        self.nbuf = 0

    def sb(self, shape, dt=F32, name=None):
        self.nbuf += 1
        return Buf(self.nc.alloc_sbuf_tensor(name or ('sb%d' % self.nbuf), list(shape), dt))

    def ps(self, shape, dt=F32, name=None):
        self.nbuf += 1
        return Buf(self.nc.alloc_psum_tensor(name or ('ps%d' % self.nbuf), list(shape), dt))

    def _wait(self, e, key, val):
        if self.seen[e].get(key, 0) >= val:
            return
        self.seen[e][key] = val
        self.eng[e].wait_ge(self.semobj[key], val)

    def _deps(self, e, reads, writes):
        need = {}
        for b in reads:
            for (k, v) in b.w:
                need[k] = max(need.get(k, 0), v)
        for b in writes:
            for (k, v) in b.w:
                need[k] = max(need.get(k, 0), v)
            for k, v in b.r.items():
                need[k] = max(need.get(k, 0), v)
        for k, v in need.items():
            if k == e and e == 'pe':
                continue
            self._wait(e, k, v)

    def _mark(self, key, val, reads, writes):
        for b in reads:
            b.r[key] = max(b.r.get(key, 0), val)
        for b in writes:
            b.w = [(key, val)]
            b.r = {}

    def op(self, e, fn, reads=(), writes=()):
        reads = [b for b in reads if b is not None]
        writes = [b for b in writes if b is not None]
        self._deps(e, reads, writes)
        inst = fn(self.eng[e])
        self.cnt[e] += 1
        inst.then_inc(self.sem[e], 1)
        self._mark(e, self.cnt[e], reads, writes)

    def dma(self, e, fn, reads=(), writes=(), is_output=False):
        reads = [b for b in reads if b is not None]
        writes = [b for b in writes if b is not None]
        self._deps(e, reads, writes)
        i = self.dnext
        self.dnext = (self.dnext + 1) % self.ndma
        key = 'd%d' % i
        if self.dcnt[i] > 0:
            self._wait(e, key, 16 * self.dcnt[i])
        inst = fn(self.eng[e])
        self.dcnt[i] += 1
        inst.then_inc(self.dsem[i], 16)
        self._mark(key, 16 * self.dcnt[i], reads, writes)
        if is_output:
            self.out_waits.append((key, 16 * self.dcnt[i]))

    def finish(self):
        for (k, v) in self.out_waits:
            self._wait('sp', k, v)
        for e in ('pe', 'act', 'dve', 'pool'):
            if self.cnt[e] > 0:
                self._wait('sp', e, self.cnt[e])

    def mm(self, out_ap, lhsT, rhs, start, stop, reads, writes):
        self.op('pe', lambda g: g.matmul(out_ap, lhsT, rhs, start=start, stop=stop), reads, writes)

    def tr(self, out_ap, in_ap, ident_ap, reads, writes):
        self.op('pe', lambda g: g.transpose(out_ap, in_ap, ident_ap), reads, writes)

    def act(self, out_ap, in_ap, func, reads, writes, bias=None, scale=None, accum_out=None, e='act'):
        kw = {}
        if bias is not None:
            kw['bias'] = bias
        if scale is not None:
            kw['scale'] = scale
        if accum_out is not None:
            kw['accum_out'] = accum_out
        self.op('act', lambda g: g.activation(out=out_ap, in_=in_ap, func=func, **kw), reads, writes)

    def tt(self, out_ap, a, b, op, reads, writes, e='dve'):
        self.op(e, lambda g: g.tensor_tensor(out=out_ap, in0=a, in1=b, op=op), reads, writes)

    def ts(self, out_ap, a, s1, s2, op0, op1, reads, writes, e='dve'):
        if op1 is None:
            self.op(e, lambda g: g.tensor_scalar(out=out_ap, in0=a, scalar1=s1, scalar2=None, op0=op0), reads, writes)
        else:
            self.op(e, lambda g: g.tensor_scalar(out=out_ap, in0=a, scalar1=s1, scalar2=s2, op0=op0, op1=op1), reads, writes)

    def stt(self, out_ap, a, s, b, op0, op1, reads, writes):
        self.op('dve', lambda g: g.scalar_tensor_tensor(out=out_ap, in0=a, scalar=s, in1=b, op0=op0, op1=op1), reads, writes)

    def cp(self, out_ap, in_ap, reads, writes, e='dve'):
        if e == 'act':
            self.op('act', lambda g: g.copy(out=out_ap, in_=in_ap), reads, writes)
        else:
            self.op(e, lambda g: g.tensor_copy(out=out_ap, in_=in_ap), reads, writes)

    def memset(self, buf, ap, val, e='pool'):
        self.op(e, lambda g: g.memset(ap, val), (), (buf,))


TT = 128
NT = SEQ // TT


def build(mode='full'):
    import math
    nc = bass.Bass("TRN2", target_bir_lowering=False)
    k = K(nc)
    full = (mode == 'full')

    def din(name, shape, dt=F32):
        return nc.dram_tensor(name, list(shape), dt, kind="ExternalInput").ap()

    def dout(name, shape, dt=F32):
        return nc.dram_tensor(name, list(shape), dt, kind="ExternalOutput").ap()

    I = {}
    I['xp'] = din('xp', [SEQ, D]); I['memp'] = din('memp', [256, D])
    if full:
        I['xs'] = din('xs', [NS, D])
        I['ck'] = din('ck', [DEPTH, NPOOL * PAGE, 512]); I['cv'] = din('cv', [DEPTH, NPOOL * PAGE, 512])
        I['clf'] = din('clf', [DEPTH, NPOOL * PAGE, 8]); I['pt'] = din('pt', [1, SB * NPAGE], I32)
        I['s5re'] = din('s5re', [DEPTH, SB, 32, 64]); I['s5im'] = din('s5im', [DEPTH, SB, 32, 64])
        I['mlC'] = din('mlC', [DEPTH, SB, 4, 128, 128]); I['mln'] = din('mln', [DEPTH, SB, 4, 128])
        I['mlm'] = din('mlm', [DEPTH, SB, 4]); I['cmk'] = din('cmk', [DEPTH, SB, 256, 512]); I['cmv'] = din('cmv', [DEPTH, SB, 256, 512])
    wshapes = dict(g_mix=[DEPTH, D], w_in=[DEPTH, D, D_IN], s5_a_re=[DEPTH, 32, 64], s5_a_im=[DEPTH, 32, 64],
                   s5_log_step=[DEPTH, 32], s5_b_re=[DEPTH, 32, 64, 16], s5_b_im=[DEPTH, 32, 64, 16],
                   s5_c_re=[DEPTH, 32, 16, 64], s5_c_im=[DEPTH, 32, 16, 64], s5_d=[DEPTH, 512],
                   s5_w_glu=[DEPTH, 512, 512], s5_b_glu=[DEPTH, 512], fox_gq=[DEPTH, 64], fox_gk=[DEPTH, 64],
                   fox_bf=[DEPTH, 8], ml_bi=[DEPTH, 4], ml_bf=[DEPTH, 4], ml_gn=[DEPTH, 128],
                   w_br_s5=[DEPTH, 512, D], w_br_fox=[DEPTH, 512, D], w_br_ml=[DEPTH, 512, D], w_out=[DEPTH, D, D],
                   g_cross=[DEPTH, D], w_cq=[DEPTH, D, 512], cross_gq=[DEPTH, 128], g_mem=[DEPTH, D],
                   w_mk=[DEPTH, D, 512], w_mv=[DEPTH, D, 512], cross_gk=[DEPTH, 128], w_co=[DEPTH, 512, D],
                   g_mlp=[DEPTH, D], w_up=[DEPTH, D, 4096], w_down=[DEPTH, 4096, D])
    W = {n: din(n, s) for n, s in wshapes.items()}
    O = {}
    O['yp'] = dout('yp', [SEQ, D])
    O['pfk'] = dout('pfk', [DEPTH, SEQ, 512]); O['pfv'] = dout('pfv', [DEPTH, SEQ, 512]); O['pflf'] = dout('pflf', [DEPTH, SEQ, 8])
    O['ps5re'] = dout('ps5re', [DEPTH, 32, 64]); O['ps5im'] = dout('ps5im', [DEPTH, 32, 64])
    O['pmlC'] = dout('pmlC', [DEPTH, 4, 128, 128]); O['pmln'] = dout('pmln', [DEPTH, 4, 128]); O['pmlm'] = dout('pmlm', [DEPTH, 4])
    O['pmemk'] = dout('pmemk', [DEPTH, 256, 512]); O['pmemv'] = dout('pmemv', [DEPTH, 256, 512])
    if full:
        O['ys'] = dout('ys', [NS, D])
        O['sfk'] = dout('sfk', [DEPTH, NS, 512]); O['sfv'] = dout('sfv', [DEPTH, NS, 512]); O['sflf'] = dout('sflf', [DEPTH, NS, 8])
        O['ss5re'] = dout('ss5re', [DEPTH, SB, 32, 64]); O['ss5im'] = dout('ss5im', [DEPTH, SB, 32, 64])
        O['smlC'] = dout('smlC', [DEPTH, SB, 4, 128, 128]); O['smln'] = dout('smln', [DEPTH, SB, 4, 128]); O['smlm'] = dout('smlm', [DEPTH, SB, 4])
    xscr = nc.dram_tensor('xscr', [SEQ, D], F32, kind="Internal").ap()
    ktscr = nc.dram_tensor('ktscr', [8, 96, SEQ], BF16, kind="Internal").ap()
    vscr = nc.dram_tensor('vscr', [8, 128, SEQ // 128, 65], BF16, kind="Internal").ap()
    XSCR = Buf(None); KSCR = Buf(None); VSCR = Buf(None)

    identf = k.sb([128, 128], F32, 'identf'); k.memset(identf, identf[:], 1.0)
    k.op('pool', lambda g: g.affine_select(out=identf[:], in_=identf[:], pattern=[[-1, 128]], compare_op=ALU.is_equal,
                                           fill=0.0, base=0, channel_multiplier=1), (identf,), (identf,))
    identb = k.sb([128, 128], BF16, 'identb'); k.cp(identb[:], identf[:], (identf,), (identb,))
    trif = k.sb([128, 128], F32, 'trif'); k.memset(trif, trif[:], 1.0)
    k.op('pool', lambda g: g.affine_select(out=trif[:], in_=trif[:], pattern=[[1, 128]], compare_op=ALU.is_ge,
                                           fill=0.0, base=0, channel_multiplier=-1), (trif,), (trif,))
    trib = k.sb([128, 128], BF16, 'trib'); k.cp(trib[:], trif[:], (trif,), (trib,))
    onesf = k.sb([128, 128], F32, 'onesf'); k.memset(onesf, onesf[:], 1.0)
    onesb = k.sb([128, 128], BF16, 'onesb'); k.memset(onesb, onesb[:], 1.0)
    triR = k.sb([128, 128], F32, 'triR')
    k.tt(triR[:], onesf[:], trif[:], ALU.subtract, (onesf, trif), (triR,))

    psb = [k.ps([128, 512], F32, 'psb%d' % i) for i in range(3)]
    pst = [k.ps([128, 1024], BF16, 'pst%d' % i) for i in range(2)]
    ps_acc = [k.ps([128, 512], F32, 'psacc%d' % i) for i in range(2)]
    psden = k.ps([128, 512], F32, 'psden')
    st = {'ps': 0, 'pt': 0, 'w': 0}

    def nps():
        st['ps'] = (st['ps'] + 1) % len(psb)
        return psb[st['ps']]

    def npt():
        st['pt'] = (st['pt'] + 1) % len(pst)
        return pst[st['pt']]

    wbufs = [k.sb([128, 8, 512], BF16, 'wbuf%d' % i) for i in range(3)]

    def nwb():
        st['w'] = (st['w'] + 1) % len(wbufs)
        return wbufs[st['w']]

    wcache = {}

    def _wscratch(key, p, n):
        if key not in wcache:
            scr = nc.dram_tensor('ws%d' % len(wcache), [p, n], BF16, kind="Internal").ap()
            wcache[key] = [scr, Buf(None), False]
        return wcache[key]

    def wload(wap, r0, nr, c0, ncol, p=128):
        wb = nwb()
        kc = nr // p
        ent = _wscratch((wap.tensor.name, wap.offset, r0, nr, c0, ncol, p), p, kc * ncol)
        scr3 = ent[0].rearrange("p (k n) -> p k n", n=ncol)
        if not ent[2]:
            src = wap[r0:r0 + nr, c0:c0 + ncol].rearrange("(kc p) n -> p kc n", p=p)
            k.dma('pool', lambda g: g.dma_start(out=wb[0:p, 0:kc, 0:ncol], in_=src), (), (wb,))
            k.dma('sp', lambda g: g.dma_start(out=scr3, in_=wb[0:p, 0:kc, 0:ncol]), (wb,), (ent[1],))
            ent[2] = True
        else:
            k.dma('pool', lambda g: g.dma_start(out=wb[0:p, 0:kc, 0:ncol], in_=scr3), (ent[1],), (wb,))
        return wb

    def wload96(wap, c0, ncol):
        wb = nwb()
        ent = _wscratch((wap.tensor.name, wap.offset, c0, ncol, '96'), 96, 6 * ncol)
        scr3 = ent[0].rearrange("p (k n) -> p k n", n=ncol)
        if not ent[2]:
            k.dma('pool', lambda g: g.dma_start(out=wb[0:96, 0:5, 0:ncol], in_=wap[0:480, c0:c0 + ncol].rearrange("(c p) n -> p c n", p=96)), (), (wb,))
            k.dma('pool', lambda g: g.dma_start(out=wb[0:32, 5, 0:ncol], in_=wap[480:512, c0:c0 + ncol]), (), (wb,))
            k.dma('sp', lambda g: g.dma_start(out=scr3, in_=wb[0:96, 0:6, 0:ncol]), (wb,), (ent[1],))
            ent[2] = True
        else:
            k.dma('pool', lambda g: g.dma_start(out=wb[0:96, 0:6, 0:ncol], in_=scr3), (ent[1],), (wb,))
        return wb

    def bcast_load(dst, ap_row, n):
        k.dma('sp', lambda g: g.dma_start(out=dst[:, 0:n], in_=ap_row.partition_broadcast(128)), (), (dst,))

    def transpose_to_T(src, P, nchunks, dstT, t0, srcsl):
        for c0 in range(0, nchunks, 8):
            nch = min(8, nchunks - c0)
            pt_ = npt()
            for c in range(nch):
                k.tr(pt_[:, c * 128:c * 128 + P], srcsl(c0 + c), identb[0:P, 0:P], (src, identb), (pt_,))
            k.cp(dstT[:, c0:c0 + nch, t0:t0 + P],
                 pt_[:, 0:nch * 128].rearrange("p (c t) -> p c t", t=128)[:, :, 0:P], (pt_,), (dstT,), e='act')

    def transpose_heads(src, P, dst, t0, srcsl, nrows=64, dst_ap=None, prow=0):
        pt_ = npt()
        for c in range(4):
            k.tr(pt_[:, c * 128:c * 128 + P], srcsl(c), identb[0:P, 0:P], (src, identb), (pt_,))
        v = pt_[:, 0:512].rearrange("p (c t) -> p c t", t=128)
        d_ = dst_ap if dst_ap is not None else dst.t
        k.cp(d_[prow:prow + nrows, 0:8:2, t0:t0 + P], v[0:nrows, :, 0:P], (pt_,), (dst,), e='act')
        k.cp(d_[prow:prow + nrows, 1:8:2, t0:t0 + P], v[64:64 + nrows, :, 0:P], (pt_,), (dst,), e='dve')

    junkb = k.sb([128, D], BF16, 'junkb'); ss1 = k.sb([128, 1], F32, 'ss1')

    def rmsnorm_T(x, P, nsub, gb, hT, tmpb):
        for s in range(nsub):
            k.act(junkb[0:P, :], x[0:P, s, :], AF.Square, (x,), (junkb, ss1), accum_out=ss1[0:P, :])
            k.ts(ss1[0:P, :], ss1[0:P, :], 1.0 / D, EPS, ALU.mult, ALU.add, (ss1,), (ss1,))
            k.act(ss1[0:P, :], ss1[0:P, :], AF.Sqrt, (ss1,), (ss1,))
            k.op('dve', lambda g: g.reciprocal(out=ss1[0:P, :], in_=ss1[0:P, :]), (ss1,), (ss1,))
            k.stt(tmpb[0:P, :], x[0:P, s, :], ss1[0:P, :], gb[0:P, 0:D], ALU.mult, ALU.mult, (x, ss1, gb), (tmpb,))
            transpose_to_T(tmpb, P, 8, hT, s * 128, lambda c: tmpb[0:P, c * 128:(c + 1) * 128])

    def proj_tok(hT, kchunks, P, nsub, wap, r0, c0, ncols, consumer):
        for cb in range(0, ncols, 512):
            nc_ = min(512, ncols - cb)
            wb = wload(wap, r0, kchunks * 128, c0 + cb, nc_)
            for s in range(nsub):
                ps_ = nps()
                for kc in range(kchunks):
                    k.mm(ps_[0:P, 0:nc_], hT[:, kc, s * 128:s * 128 + P], wb[:, kc, 0:nc_], kc == 0, kc == kchunks - 1,
                         (hT, wb), (ps_,))
                consumer(ps_, s, cb, nc_)

    def proj_feat(hT, kchunks, ntok, wap, r0, c0, ncols, consumer, chunk=128):
        for cb in range(0, ncols, 512):
            nc_ = min(512, ncols - cb)
            wb = wload(wap, r0, kchunks * 128, c0 + cb, nc_)
            for j in range(0, nc_, chunk):
                m = min(chunk, nc_ - j)
                ps_ = nps()
                for kc in range(kchunks):
                    k.mm(ps_[0:m, 0:ntok], wb[:, kc, j:j + m], hT[:, kc, 0:ntok], kc == 0, kc == kchunks - 1, (hT, wb), (ps_,))
                consumer(ps_, (cb + j) // chunk, m)

    NSUBX = 1
    xres = k.sb([128, NSUBX, D], F32, 'xres')
    gb = k.sb([128, D], F32, 'gb')
    tmpb = k.sb([128, D], BF16, 'tmpb')
    hT = k.sb([128, 8, 128], BF16, 'hT')
    arena = k.sb([128, 32, TT], BF16, 'arena')
    arena_f = arena.t.bitcast(F32).reshape([128, 16 * TT])
    mrgf = k.sb([128, D], F32, 'mrgf')
    yT = {b: k.sb([128, 4, TT], BF16, 'yT_' + b) for b in ('s5', 'fox', 'ml')}
    memK = k.sb([128, 4, 256], BF16, 'memKT')
    memV = k.sb([128, 2, 512], BF16, 'memV')
    rowt = k.sb([128, 512], F32, 'rowt')
    kout = k.sb([128, 512], F32, 'kout')
    sm8 = k.sb([128, 8], F32, 'sm8')
    pT = k.sb([128, 2, TT], BF16, 'pT')
    den4 = k.sb([4, TT], F32, 'den4')
    ones4 = k.sb([4, 128], F32, 'ones4'); k.memset(ones4, ones4[:], 1.0)
    selh = []
    for h in range(4):
        s_ = k.sb([4, 128], F32, 'selh%d' % h)
        k.op('pool', lambda g, s_=s_, h=h: g.affine_select(out=s_[:], in_=ones4[:], pattern=[[0, 128]], compare_op=ALU.is_equal,
                                                       fill=0.0, base=-h, channel_multiplier=1), (ones4,), (s_,))
        selh.append(s_)
    onesel = k.sb([128, 4, 4], BF16, 'onesel'); k.memset(onesel, onesel[:], 0.0)
    for h in range(4):
        k.memset(onesel, onesel[:, h, h:h + 1], 1.0)
    gvb = {n_: k.sb([128, 128], F32, 'gvb_' + n_) for n_ in ('cgq', 'cgk', 'fgq', 'fgk')}

    def head_rms(ps_, P, nh, hd, gvec_b, out_buf, out_ap, scale):
        n = nh * hd
        k.act(rowt[0:P, 0:n], ps_[0:P, 0:n], AF.Square, (ps_,), (rowt,))
        k.op('dve', lambda g: g.tensor_reduce(out=sm8[0:P, 0:nh], in_=rowt[0:P, 0:n].rearrange("p (h d) -> p h d", d=hd),
                                              axis=mybir.AxisListType.X, op=ALU.add), (rowt,), (sm8,))
        k.ts(sm8[0:P, 0:nh], sm8[0:P, 0:nh], 1.0 / hd, EPS, ALU.mult, ALU.add, (sm8,), (sm8,))
        k.act(sm8[0:P, 0:nh], sm8[0:P, 0:nh], AF.Sqrt, (sm8,), (sm8,))
        k.op('dve', lambda g: g.reciprocal(out=sm8[0:P, 0:nh], in_=sm8[0:P, 0:nh]), (sm8,), (sm8,))
        k.tt(rowt[0:P, 0:n].rearrange("p (h d) -> p h d", d=hd), ps_[0:P, 0:n].rearrange("p (h d) -> p h d", d=hd),
             sm8[0:P, 0:nh].unsqueeze(2).to_broadcast([P, nh, hd]), ALU.mult, (ps_, sm8), (rowt,))
        k.stt(out_ap, rowt[0:P, 0:n].rearrange("p (h d) -> p h d", d=hd), float(scale),
              gvec_b[0:P, 0:hd].unsqueeze(1).to_broadcast([P, nh, hd]), ALU.mult, ALU.mult, (rowt, gvec_b), (out_buf,))

    def memory_kv(l):
        bcast_load(gb, W['g_mem'][l], D)
        bcast_load(gvb['cgk'], W['cross_gk'][l], 128)
        for s in range(2):
            k.dma('sp', lambda g: g.dma_start(out=xres[:, 0, :], in_=I['memp'][s * 128:(s + 1) * 128, :]), (), (xres,))
            rmsnorm_T(xres, 128, 1, gb, hT, tmpb)

            def cons_k(ps_, s_, cb, nc_, s=s):
                head_rms(ps_, 128, 4, 128, gvb['cgk'], kout, kout[:, 0:512].rearrange("p (h d) -> p h d", d=128), 1.0)
                k.dma('sp', lambda g: g.dma_start(out=O['pmemk'][l, s * 128:(s + 1) * 128, :], in_=kout[:, 0:512]), (kout,), (), True)
                k.cp(tmpb[:, 0:512], kout[:, 0:512], (kout,), (tmpb,))
                transpose_to_T(tmpb, 128, 4, memK, s * 128, lambda c: tmpb[:, c * 128:(c + 1) * 128])
            proj_tok(hT, 8, 128, 1, W['w_mk'][l], 0, 0, 512, cons_k)

            def cons_v(ps_, s_, cb, nc_, s=s):
                k.cp(rowt[:, 0:512], ps_[:, 0:512], (ps_,), (rowt,), e='act')
                k.dma('sp', lambda g: g.dma_start(out=O['pmemv'][l, s * 128:(s + 1) * 128, :], in_=rowt[:, 0:512]), (rowt,), (), True)
                k.cp(memV[:, s, :], rowt[:, 0:512], (rowt,), (memV,))
            proj_tok(hT, 8, 128, 1, W['w_mv'][l], 0, 0, 512, cons_v)

    def cross_q(l, P, ntok):
        bcast_load(gb, W['g_cross'][l], D)
        bcast_load(gvb['cgq'], W['cross_gq'][l], 128)
        rmsnorm_T(xres, P, 1, gb, hT, tmpb)
        qT = yT['s5']

        def cons_q(ps_, s, cb, nc_):
            head_rms(ps_, P, 4, 128, gvb['cgq'], tmpb, tmpb[0:P, 0:512].rearrange("p (h d) -> p h d", d=128), 128 ** -0.5)
            transpose_to_T(tmpb, P, 4, qT, 0, lambda c: tmpb[0:P, c * 128:(c + 1) * 128])
        proj_tok(hT, 8, P, 1, W['w_cq'][l], 0, 0, 512, cons_q)
        return qT

    def cross_finish(l, P, ntok):
        ocT = yT['fox']
        k.op('dve', lambda g: g.reciprocal(out=den4[:, 0:ntok], in_=psden[0:4, 0:ntok]), (psden,), (den4,))
        for h in range(4):
            psr = nps()
            k.mm(psr[:, 0:ntok], selh[h][:], den4[:, 0:ntok], True, True, (selh[h], den4), (psr,))
            k.tt(ocT[:, h, 0:ntok], arena_f[:, h * TT:h * TT + ntok], psr[:, 0:ntok], ALU.mult, (arena, psr), (ocT,))

        def cons_o(ps_, s, cb, nc_):
            k.tt(xres[0:P, 0, cb:cb + nc_], xres[0:P, 0, cb:cb + nc_], ps_[0:P, 0:nc_], ALU.add, (xres, ps_), (xres,))
        proj_tok(ocT, 4, P, 1, W['w_co'][l], 0, 0, D, cons_o)

    def cross_attn_prompt(l):
        P, ntok = 128, TT
        qT = cross_q(l, P, ntok)
        for h in range(4):
            for mc in range(2):
                ps_ = nps()
                k.mm(ps_[:, 0:ntok], memK[:, h, mc * 128:(mc + 1) * 128], qT[:, h, 0:ntok], True, True, (memK, qT), (ps_,))
                k.act(pT[:, mc, 0:ntok], ps_[:, 0:ntok], AF.Exp, (ps_,), (pT,), bias=-8.0)
            pso = nps()
            for mc in range(2):
                k.mm(pso[:, 0:ntok], memV[:, mc, h * 128:(h + 1) * 128], pT[:, mc, 0:ntok], mc == 0, mc == 1, (memV, pT), (pso,))
            k.cp(arena_f[:, h * TT:h * TT + ntok], pso[:, 0:ntok], (pso,), (arena,), e='act')
            for mc in range(2):
                k.mm(psden[0:4, 0:ntok], onesel[:, h, :], pT[:, mc, 0:ntok], h == 0 and mc == 0, h == 3 and mc == 1, (onesel, pT), (psden,))
        cross_finish(l, P, ntok)

    def mlp(l, P, ntok):
        bcast_load(gb, W['g_mlp'][l], D)
        rmsnorm_T(xres, P, 1, gb, hT, tmpb)

        def cons_up(ps_, j, m):
            k.act(rowt[:, 0:ntok], ps_[:, 0:ntok], AF.Relu, (ps_,), (rowt,))
            k.tt(arena[:, j, 0:ntok], rowt[:, 0:ntok], rowt[:, 0:ntok], ALU.mult, (rowt,), (arena,))
        proj_feat(hT, 8, ntok, W['w_up'][l], 0, 0, 4096, cons_up)
        for cb in range(2):
            ps_ = ps_acc[cb]
            for kg in range(4):
                wb = wload(W['w_down'][l], kg * 1024, 1024, cb * 512, 512)
                for kc in range(8):
                    k.mm(ps_[0:P, :], arena[:, kg * 8 + kc, 0:P], wb[:, kc, :], kg == 0 and kc == 0, kg == 3 and kc == 7, (arena, wb), (ps_,))
            k.tt(xres[0:P, 0, cb * 512:(cb + 1) * 512], xres[0:P, 0, cb * 512:(cb + 1) * 512], ps_[0:P, :], ALU.add, (xres, ps_), (xres,))

    S5T = Buf(None)
    def raw(name, shape, dt=F32):
        return nc.alloc_sbuf_tensor('s5_' + name, list(shape), dt)
    are = raw('are', [128, 16]); aim = raw('aim', [128, 16]); lsb = raw('lsb', [128, 16]); dtt = raw('dtt', [128, 16])
    mag = raw('mag', [128, 16]); cs_c = raw('cs_c', [128, 16]); cs_s = raw('cs_s', [128, 16])
    u1 = raw('u1', [128, 16]); u2 = raw('u2', [128, 16]); u3 = raw('u3', [128, 16]); u4 = raw('u4', [128, 16])
    Bre = raw('Bre', [128, 16, 16]); Bim = raw('Bim', [128, 16, 16]); Cre = raw('Cre', [128, 16, 16]); Cim = raw('Cim', [128, 16, 16])
    Bbr = raw('Bbr', [128, 16, 16]); Bbi = raw('Bbi', [128, 16, 16])
    pwr = raw('pwr', [128, 9, 16]); pwi = raw('pwi', [128, 9, 16])
    V1 = raw('V1', [128, 16, 16]); V2 = raw('V2', [128, 16, 16]); V3 = raw('V3', [128, 16, 16]); V4 = raw('V4', [128, 16, 16])
    QBD = raw('QBD', [128, 9, 2, 16, 32], BF16); BBD = raw('BBD', [128, 2, 16, 32], BF16)
    BDt = raw('BDt', [128, 96], BF16)
    W1 = raw('W1', [96, 8, 2, 6, 128], BF16); Kt = raw('Kt', [96, 6, 8, 32], BF16)
    NJ = TT // 8
    MUFr = raw('MUFr', [128, 16, NJ]); MUFi = raw('MUFi', [128, 16, NJ]); MUIr = raw('MUIr', [128, 16, NJ]); MUIi = raw('MUIi', [128, 16, NJ])
    M1 = raw('M1', [128, 16, NJ]); M2 = raw('M2', [128, 16, NJ]); M3 = raw('M3', [128, 16, NJ]); M4 = raw('M4', [128, 16, NJ])
    rmask = raw('rmask', [128, 16, NJ]); dcol = raw('dcol', [96, 6]); bgcol = raw('bgcol', [96, 6])
    Xr = raw('Xr', [128, 16, NJ + 1]); Xi = raw('Xi', [128, 16, NJ + 1])
    Xbr = raw('Xbr', [128, 16, NJ + 1], BF16); Xbi = raw('Xbi', [128, 16, NJ + 1], BF16)
    uT = k.sb([96, 6, TT], BF16, 'uT'); zt = k.sb([96, TT], F32, 'zt'); ygT = k.sb([96, 6, TT], BF16, 'ygT')
    ys5T = k.sb([96, 6, TT], BF16, 'ys5T')
    for t_ in (QBD, BBD, BDt):
        k.op('pool', lambda g, t_=t_: g.memset(t_[:], 0.0), (), (S5T,))
    k.op('pool', lambda g: g.memset(rmask[:], 1.0), (), (S5T,))
    k.op('pool', lambda g: g.memset(rmask[:, :, 0:1], 0.0), (), (S5T,))

    def sop(fn, e='dve'):
        k.op(e, fn, (S5T,), (S5T,))

    def s_tt(o, a, b_, op, e='dve'):
        sop(lambda g: g.tensor_tensor(out=o, in0=a, in1=b_, op=op), e)

    def s_cmul(o_re, o_im, a_re, a_im, b_re, b_im, t1, t2):
        s_tt(t1, a_re, b_re, ALU.mult); s_tt(t2, a_im, b_im, ALU.mult)
        s_tt(t1, t1, t2, ALU.subtract)
        s_tt(t2, a_re, b_im, ALU.mult); s_tt(o_im, a_im, b_re, ALU.mult)
        s_tt(o_im, o_im, t2, ALU.add)
        sop(lambda g: g.tensor_copy(out=o_re, in_=t1))

    def s5_setup(l):
        def ld(dst, src, eng='sp'):
            k.dma(eng, lambda g: g.dma_start(out=dst, in_=src, allow_slow_non_contiguous=True), (S5T,), (S5T,))
        ld(are[:], W['s5_a_re'][l].rearrange("(q r) p -> (r p) q", r=2))
        ld(aim[:], W['s5_a_im'][l].rearrange("(q r) p -> (r p) q", r=2))
        lsv = W['s5_log_step'][l].rearrange("(q r) -> r q", r=2)
        for r in range(2):
            ld(lsb[r * 64:(r + 1) * 64, :], lsv[r].partition_broadcast(64))
        ld(Bre[:], W['s5_b_re'][l].rearrange("(q r) p c -> (r p) q c", r=2))
        ld(Bim[:], W['s5_b_im'][l].rearrange("(q r) p c -> (r p) q c", r=2))
        for r in range(2):
            for q in range(16):
                ld(Cre[r * 64:(r + 1) * 64, q, :], W['s5_c_re'][l][2 * q + r].rearrange("c p -> p c"))
                ld(Cim[r * 64:(r + 1) * 64, q, :], W['s5_c_im'][l][2 * q + r].rearrange("c p -> p c"))
        for (dst, src) in ((dcol, W['s5_d'][l]), (bgcol, W['s5_b_glu'][l])):
            ld(dst[:, 0:5], src[0:480].rearrange("(c p) -> p c", p=96))
            ld(dst[0:32, 5:6], src[480:512].rearrange("(c p) -> p c", p=32))
        sop(lambda g: g.activation(out=dtt[:], in_=lsb[:], func=AF.Exp), 'act')
        s_tt(u1[:], are[:], dtt[:], ALU.mult)
        sop(lambda g: g.activation(out=mag[:], in_=u1[:], func=AF.Exp), 'act')
        s_tt(u1[:], aim[:], dtt[:], ALU.mult)
        sop(lambda g: g.activation(out=cs_s[:], in_=u1[:], func=AF.Sin, scale=1.0 / 16), 'act')
        sop(lambda g: g.tensor_scalar(out=u2[:], in0=u1[:], scalar1=1.0 / 16, scalar2=math.pi / 2, op0=ALU.mult, op1=ALU.add))
        sop(lambda g: g.activation(out=cs_c[:], in_=u2[:], func=AF.Sin), 'act')
        for _ in range(4):
            s_tt(u2[:], cs_c[:], cs_c[:], ALU.mult); s_tt(u3[:], cs_s[:], cs_s[:], ALU.mult)
            s_tt(u4[:], cs_c[:], cs_s[:], ALU.mult)
            s_tt(cs_c[:], u2[:], u3[:], ALU.subtract)
            sop(lambda g: g.tensor_scalar(out=cs_s[:], in0=u4[:], scalar1=2.0, scalar2=None, op0=ALU.mult))
        sop(lambda g: g.memset(pwr[:, 0, :], 1.0), 'pool'); sop(lambda g: g.memset(pwi[:, 0, :], 0.0), 'pool')
        s_tt(pwr[:, 1, :], mag[:], cs_c[:], ALU.mult); s_tt(pwi[:, 1, :], mag[:], cs_s[:], ALU.mult)
        for kk in range(2, 9):
            s_cmul(pwr[:, kk, :], pwi[:, kk, :], pwr[:, kk - 1, :], pwi[:, kk - 1, :], pwr[:, 1, :], pwi[:, 1, :], u1[:], u2[:])
        sop(lambda g: g.tensor_scalar(out=u1[:], in0=pwr[:, 1, :], scalar1=-1.0, scalar2=None, op0=ALU.add))
        s_tt(u2[:], are[:], are[:], ALU.mult); s_tt(u3[:], aim[:], aim[:], ALU.mult); s_tt(u2[:], u2[:], u3[:], ALU.add)
        sop(lambda g: g.reciprocal(out=u2[:], in_=u2[:]))
        s_tt(u3[:], u1[:], are[:], ALU.mult); s_tt(u4[:], pwi[:, 1, :], aim[:], ALU.mult); s_tt(u3[:], u3[:], u4[:], ALU.add)
        s_tt(u3[:], u3[:], u2[:], ALU.mult)
        s_tt(u4[:], pwi[:, 1, :], are[:], ALU.mult); s_tt(u1[:], u1[:], aim[:], ALU.mult); s_tt(u4[:], u4[:], u1[:], ALU.subtract)
        s_tt(u4[:], u4[:], u2[:], ALU.mult)
        bc = lambda a: a.unsqueeze(2).to_broadcast([128, 16, 16])
        s_cmul(Bbr[:], Bbi[:], bc(u3[:]), bc(u4[:]), Bre[:], Bim[:], V1[:], V2[:])
        for r in range(2):
            sop(lambda g, r=r: g.tensor_copy(out=BBD[r * 64:(r + 1) * 64, 0, :, r * 16:(r + 1) * 16], in_=Bbr[r * 64:(r + 1) * 64, :, :]))
            sop(lambda g, r=r: g.tensor_copy(out=BBD[r * 64:(r + 1) * 64, 1, :, r * 16:(r + 1) * 16], in_=Bbi[r * 64:(r + 1) * 64, :, :]))
        for kk in range(9):
            s_cmul(V3[:], V4[:], Cre[:], Cim[:], bc(pwr[:, kk, :]), bc(pwi[:, kk, :]), V1[:], V2[:])
            for r in range(2):
                sop(lambda g, r=r, kk=kk: g.tensor_copy(out=QBD[r * 64:(r + 1) * 64, kk, 0, :, r * 16:(r + 1) * 16], in_=V3[r * 64:(r + 1) * 64, :, :]))
                sop(lambda g, r=r, kk=kk: g.tensor_scalar(out=QBD[r * 64:(r + 1) * 64, kk, 1, :, r * 16:(r + 1) * 16], in0=V4[r * 64:(r + 1) * 64, :, :],
                                                          scalar1=-1.0, scalar2=None, op0=ALU.mult))
        for kk in range(8):
            s_cmul(V3[:], V4[:], Bbr[:], Bbi[:], bc(pwr[:, kk, :]), bc(pwi[:, kk, :]), V1[:], V2[:])
            for ri, Vx in enumerate((V3, V4)):
                for tq in range(6):
                    npair = 3 if tq < 5 else 1
                    bdv = BDt[:, :].rearrange("p (a r c) -> p a r c", r=2, c=16)
                    for r in range(2):
                        sop(lambda g, r=r, tq=tq, npair=npair, Vx=Vx: g.tensor_copy(out=bdv[r * 64:(r + 1) * 64, 0:npair, r, :],
                                                                               in_=Vx[r * 64:(r + 1) * 64, tq * 3:tq * 3 + npair, :]))
                    pt_ = npt()
                    k.op('pe', lambda g, pt_=pt_, npair=npair: g.transpose(pt_[0:32 * npair, 0:128], BDt[:, 0:32 * npair], identb[:, :]), (S5T, identb), (pt_,))
                    k.op('act', lambda g, pt_=pt_, npair=npair, kk=kk, ri=ri, tq=tq: g.copy(out=W1[0:32 * npair, kk, ri, tq, :], in_=pt_[0:32 * npair, 0:128]), (pt_, S5T), (S5T,))
        for q in range(16):
            po, tq = 32 * (q % 3), q // 3
            ps_ = nps()
            for tau in range(8):
                k.op('pe', lambda g, tau=tau, q=q, ps_=ps_, po=po: g.matmul(ps_[po:po + 32, tau * 32:(tau + 1) * 32], BBD[:, 0, q, :], QBD[:, tau, 0, q, :], start=True, stop=False), (S5T,), (ps_,))
                k.op('pe', lambda g, tau=tau, q=q, ps_=ps_, po=po: g.matmul(ps_[po:po + 32, tau * 32:(tau + 1) * 32], BBD[:, 1, q, :], QBD[:, tau, 1, q, :], start=False, stop=True), (S5T,), (ps_,))
            k.op('act', lambda g, ps_=ps_, po=po, tq=tq: g.copy(out=Kt[po:po + 32, tq, :, :], in_=ps_[po:po + 32, 0:256].rearrange("p (a b) -> p a b", b=32)), (ps_, S5T), (S5T,))
        sop(lambda g: g.tensor_copy(out=MUFr[:, :, 0], in_=pwr[:, 8, :])); sop(lambda g: g.tensor_copy(out=MUFi[:, :, 0], in_=pwi[:, 8, :]))
        for j in range(1, NJ):
            s_cmul(MUFr[:, :, j], MUFi[:, :, j], MUFr[:, :, j - 1], MUFi[:, :, j - 1], pwr[:, 8, :], pwi[:, 8, :], u1[:], u2[:])
        s_tt(M1[:], MUFr[:], MUFr[:], ALU.mult); s_tt(M2[:], MUFi[:], MUFi[:], ALU.mult); s_tt(M1[:], M1[:], M2[:], ALU.add)
        sop(lambda g: g.reciprocal(out=M1[:], in_=M1[:]))
        s_tt(MUIr[:], MUFr[:], M1[:], ALU.mult); s_tt(MUIi[:], MUFi[:], M1[:], ALU.mult)
        sop(lambda g: g.tensor_scalar(out=MUIi[:], in0=MUIi[:], scalar1=-1.0, scalar2=None, op0=ALU.mult))
        sop(lambda g: g.memset(Xr[:], 0.0), 'pool'); sop(lambda g: g.memset(Xi[:], 0.0), 'pool')

    def s5_tile_prompt(l, ntok):
        L, nj = 8, ntok // 8
        sop(lambda g: g.tensor_copy(out=Xr[:, :, 0], in_=Xr[:, :, nj])); sop(lambda g: g.tensor_copy(out=Xi[:, :, 0], in_=Xi[:, :, nj]))
        psS = [nps(), nps()]
        for q in range(16):
            po, tq = 32 * (q % 3), q // 3
            for ri in range(2):
                for sp in range(L):
                    k.op('pe', lambda g, q=q, ri=ri, sp=sp, po=po, tq=tq: g.matmul(psS[ri][:, q * nj:(q + 1) * nj], W1[po:po + 32, L - 1 - sp, ri, tq, :],
                                                                                uT[po:po + 32, tq, sp:ntok:L], start=(sp == 0), stop=(sp == L - 1)), (S5T, uT), (psS[ri],))
        Sr = psS[0][:, 0:16 * nj].rearrange("p (q j) -> p q j", j=nj); Si = psS[1][:, 0:16 * nj].rearrange("p (q j) -> p q j", j=nj)
        rd = (S5T, psS[0], psS[1])
        def mop(fn):
            k.op('dve', fn, rd, (S5T,))
        mop(lambda g: g.tensor_tensor(out=M1[:], in0=Sr, in1=MUIr[:], op=ALU.mult)); mop(lambda g: g.tensor_tensor(out=M2[:], in0=Si, in1=MUIi[:], op=ALU.mult))
        mop(lambda g: g.tensor_tensor(out=M1[:], in0=M1[:], in1=M2[:], op=ALU.subtract))
        mop(lambda g: g.tensor_tensor(out=M2[:], in0=Sr, in1=MUIi[:], op=ALU.mult)); mop(lambda g: g.tensor_tensor(out=M3[:], in0=Si, in1=MUIr[:], op=ALU.mult))
        mop(lambda g: g.tensor_tensor(out=M2[:], in0=M2[:], in1=M3[:], op=ALU.add))
        fl = lambda a: a[:, :, :].rearrange("p q j -> p (q j)")
        mop(lambda g: g.tensor_tensor_scan(out=fl(M3), data0=fl(rmask), data1=fl(M1), initial=0.0, op0=ALU.mult, op1=ALU.add))
        mop(lambda g: g.tensor_tensor_scan(out=fl(M4), data0=fl(rmask), data1=fl(M2), initial=0.0, op0=ALU.mult, op1=ALU.add))
        mop(lambda g: g.tensor_tensor(out=M3[:], in0=M3[:], in1=Xr[:, :, 0:1].to_broadcast([128, 16, nj]), op=ALU.add))
        mop(lambda g: g.tensor_tensor(out=M4[:], in0=M4[:], in1=Xi[:, :, 0:1].to_broadcast([128, 16, nj]), op=ALU.add))
        s_cmul(Xr[:, :, 1:nj + 1], Xi[:, :, 1:nj + 1], M3[:], M4[:], MUFr[:], MUFi[:], M1[:], M2[:])
        sop(lambda g: g.tensor_copy(out=Xbr[:], in_=Xr[:])); sop(lambda g: g.tensor_copy(out=Xbi[:], in_=Xi[:]))
        s5_outputs(l, ntok, L, nj, Xbr, Xbi, lambda q: (slice(0, nj)))

    def s5_outputs(l, ntok, L, nj, Xb_r, Xb_i, xsl, xtok=None):
        xrd = (S5T,) if xtok is None else (S5T, xtok)
        for tq in range(6):
            npair = 3 if tq < 5 else 1
            nr = 32 * npair
            psy = nps()
            for qq in range(npair):
                q = tq * 3 + qq
                po = 32 * qq
                for r in range(L):
                    o_ap = psy[po:po + 32, r * nj:(r + 1) * nj]
                    k.op('pe', lambda g, o_ap=o_ap, r=r, q=q: g.matmul(o_ap, QBD[:, r + 1, 0, q, :], Xb_r[:, q, 0:nj], start=True, stop=False), xrd, (psy,))
                    k.op('pe', lambda g, o_ap=o_ap, r=r, q=q: g.matmul(o_ap, QBD[:, r + 1, 1, q, :], Xb_i[:, q, 0:nj], start=False, stop=False), xrd, (psy,))
                    for sp in range(r + 1):
                        k.op('pe', lambda g, o_ap=o_ap, r=r, sp=sp, po=po, tq=tq: g.matmul(o_ap, Kt[po:po + 32, tq, r - sp, :], uT[po:po + 32, tq, sp:ntok:L],
                                                                                       start=False, stop=(sp == r)), (S5T, uT), (psy,))
            k.op('dve', lambda g, tq=tq, nr=nr, psy=psy: g.scalar_tensor_tensor(
                out=zt[0:nr, 0:ntok].rearrange("p (j r) -> p r j", r=L), in0=uT[0:nr, tq, 0:ntok].rearrange("p (j r) -> p r j", r=L),
                scalar=dcol[0:nr, tq:tq + 1], in1=psy[0:nr, 0:ntok].rearrange("p (r j) -> p r j", j=nj), op0=ALU.mult, op1=ALU.add), (uT, S5T, psy), (zt,))
            k.act(ygT[0:nr, tq, 0:ntok], zt[0:nr, 0:ntok], AF.Gelu, (zt,), (ygT,))
        wgl = wload96(W['s5_w_glu'][l], 0, 512)
        for to in range(6):
            nro = 96 if to < 5 else 32
            psg = nps()
            for ti_ in range(6):
                nri = 96 if ti_ < 5 else 32
                k.op('pe', lambda g, to=to, ti_=ti_, nro=nro, nri=nri, psg=psg: g.matmul(psg[0:nro, 0:ntok], wgl[0:nri, ti_, to * 96:to * 96 + nro], ygT[0:nri, ti_, 0:ntok],
                                                                                   start=(ti_ == 0), stop=(ti_ == 5)), (wgl, ygT), (psg,))
            k.op('act', lambda g, to=to, nro=nro, psg=psg: g.activation(out=zt[0:nro, 0:ntok], in_=psg[0:nro, 0:ntok], func=AF.Sigmoid, bias=bgcol[0:nro, to:to + 1]), (psg, S5T), (zt,))
            k.tt(ys5T[0:nro, to, 0:ntok], ygT[0:nro, to, 0:ntok], zt[0:nro, 0:ntok], ALU.mult, (ygT, zt), (ys5T,))

    def s5_final_state_out(l):
        k.dma('sp', lambda g: g.dma_start(out=O['ps5re'][l].rearrange("(q r) p -> (r p) q", r=2), in_=Xr[:, :, NJ], allow_slow_non_contiguous=True), (S5T,), (), True)
        k.dma('sp', lambda g: g.dma_start(out=O['ps5im'][l].rearrange("(q r) p -> (r p) q", r=2), in_=Xi[:, :, NJ], allow_slow_non_contiguous=True), (S5T,), (), True)

    NB = SEQ // 128
    Fcarry = k.sb([128, 8], F32, 'Fcarry')
    QTh = k.sb([96, 8, TT], BF16, 'QTh')
    AQ = k.sb([128, 512], BF16, 'AQ'); k.memset(AQ, AQ[:], 0.0)
    k.memset(AQ, AQ[:, :].rearrange("p (h c) -> p h c", c=64)[:, :, 2:4], 1.0)
    AQTh = k.sb([64, 8, TT], BF16, 'AQTh')
    kst = k.sb([96, 8, TT], BF16, 'kst')
    AK = k.sb([128, 512], BF16, 'AK'); k.memset(AK, AK[:], 0.0); k.memset(AK, AK[:, :].rearrange("p (h c) -> p h c", c=64)[:, :, 0:2], 1.0)
    vst = k.sb([128, 8, 65], BF16, 'vst'); k.memset(vst, vst[:], 1.0)
    kbuf = [k.sb([128, SEQ], BF16, 'kbuf%d' % i) for i in range(2)]
    vbuf = [k.sb([128, NB, 65], BF16, 'vbuf%d' % i) for i in range(2)]
    bfb = k.sb([128, 8], F32, 'bfb')
    lf8 = k.sb([128, 8], F32, 'lf8'); F8 = k.sb([128, 8], F32, 'F8'); t8 = k.sb([128, 8], F32, 't8')
    pTf = k.sb([128, 2, 4 * TT], BF16, 'pTf')
    oTf = k.sb([65, TT], F32, 'oTf')
    recf = k.sb([64, TT], F32, 'recf')
    yfx = k.sb([64, 8, TT], BF16, 'yfx')
    sel65 = k.sb([65, 64], F32, 'sel65'); k.memset(sel65, sel65[:], 0.0); k.memset(sel65, sel65[64:65, :], 1.0)

    def fox_gates(ps_, P, blk_negF, out_lf_ap):
        k.tt(t8[0:P, :], ps_[0:P, 0:8], bfb[0:P, :], ALU.add, (ps_, bfb), (t8,))
        k.act(t8[0:P, :], t8[0:P, :], AF.Exp, (t8,), (t8,), scale=-1.0)
        k.act(t8[0:P, :], t8[0:P, :], AF.Ln, (t8,), (t8,), bias=1.0)
        k.ts(lf8[0:P, :], t8[0:P, :], -1.0, None, ALU.mult, None, (t8,), (lf8,))
        k.dma('sp', lambda g: g.dma_start(out=out_lf_ap, in_=lf8[0:P, :]), (lf8,), (), True)

    def fox_prompt_tile(l, ti):
        P, ntok = 128, TT
        t0 = ti * TT
        Win = W['w_in'][l]

        def cons_ff(ps_, s, cb, nc_):
            fox_gates(ps_, P, None, O['pflf'][l, t0:t0 + P, :])
            p1 = nps()
            k.mm(p1[:, 0:8], trif[:], lf8[:], True, True, (trif, lf8), (p1,))
            k.tt(F8[:], p1[:, 0:8], Fcarry[:], ALU.add, (p1, Fcarry), (F8,))
            p2 = nps()
            k.mm(p2[:, 0:8], onesf[:], lf8[:], True, True, (onesf, lf8), (p2,))
            k.tt(Fcarry[:], Fcarry[:], p2[:, 0:8], ALU.add, (Fcarry, p2), (Fcarry,))
            aqv = AQ[:, :].rearrange("p (h c) -> p h c", c=64)
            akv = AK[:, :].rearrange("p (h c) -> p h c", c=64)
            k.cp(aqv[:, :, 0], F8[:], (F8,), (AQ,))
            k.cp(t8[:], aqv[:, :, 0], (AQ,), (t8,))
            k.tt(aqv[:, :, 1], F8[:], t8[:], ALU.subtract, (F8, t8), (AQ,))
            k.ts(akv[:, :, 2], aqv[:, :, 0], -1.0, None, ALU.mult, None, (AQ,), (AK,))
            k.ts(akv[:, :, 3], aqv[:, :, 1], -1.0, None, ALU.mult, None, (AQ,), (AK,))
            transpose_heads(AQ, P, QTh, 0, lambda c: AQ[0:P, c * 128:(c + 1) * 128], nrows=32, prow=64)
            transpose_heads(AK, P, kst, 0, lambda c: AK[0:P, c * 128:(c + 1) * 128], nrows=32, prow=64)
        proj_tok(hT, 8, P, 1, Win, 0, C_FF, 8, cons_ff)

        def cons_fq(ps_, s, cb, nc_):
            head_rms(ps_, P, 8, 64, gvb['fgq'], tmpb, tmpb[0:P, 0:512].rearrange("p (h d) -> p h d", d=64), 0.125)
            transpose_heads(tmpb, P, QTh, 0, lambda c: tmpb[0:P, c * 128:(c + 1) * 128])
        proj_tok(hT, 8, P, 1, Win, 0, C_FQ, 512, cons_fq)

        def cons_fk(ps_, s, cb, nc_):
            head_rms(ps_, P, 8, 64, gvb['fgk'], kout, kout[0:P, 0:512].rearrange("p (h d) -> p h d", d=64), 1.0)
            k.dma('sp', lambda g: g.dma_start(out=O['pfk'][l, t0:t0 + P, :], in_=kout[0:P, 0:512]), (kout,), (), True)
            k.cp(tmpb[0:P, 0:512], kout[0:P, 0:512], (kout,), (tmpb,))
            transpose_heads(tmpb, P, kst, 0, lambda c: tmpb[0:P, c * 128:(c + 1) * 128])
            k.dma('sp', lambda g: g.dma_start(out=ktscr.rearrange("h d t -> d h t")[:, :, t0:t0 + P], in_=kst[:, :, 0:P]), (kst,), (KSCR,))
        proj_tok(hT, 8, P, 1, Win, 0, C_FK, 512, cons_fk)

        def cons_fv(ps_, s, cb, nc_):
            k.cp(rowt[0:P, 0:512], ps_[0:P, 0:512], (ps_,), (rowt,), e='act')
            k.dma('sp', lambda g: g.dma_start(out=O['pfv'][l, t0:t0 + P, :], in_=rowt[0:P, 0:512]), (rowt,), (), True)
            k.cp(vst[:, :, 0:64], rowt[:, 0:512].rearrange("p (h d) -> p h d", d=64), (rowt,), (vst,))
            k.dma('sp', lambda g: g.dma_start(out=vscr.rearrange("h p b c -> p h b c")[:, :, ti, :], in_=vst[:, :, :]), (vst,), (VSCR,))
        proj_tok(hT, 8, P, 1, Win, 0, C_FV, 512, cons_fv)

        nkb = ti + 1
        for h in range(8):
            kb_, vb_ = kbuf[h % 2], vbuf[h % 2]
            k.dma('sp', lambda g: g.dma_start(out=kb_[0:96, 0:nkb * 128], in_=ktscr[h, :, 0:nkb * 128]), (KSCR,), (kb_,))
            k.dma('sp', lambda g: g.dma_start(out=vb_[:, 0:nkb, :], in_=vscr[h, :, 0:nkb, :]), (VSCR,), (vb_,))
            pso = ps_acc[h % 2]
            for g0 in range(0, nkb, 4):
                ng = min(4, nkb - g0)
                ps_ = nps()
                par = (g0 // 4) % 2
                for i in range(ng):
                    kb = g0 + i
                    k.mm(ps_[:, i * ntok:(i + 1) * ntok], kb_[0:96, kb * 128:(kb + 1) * 128], QTh[0:96, h, 0:ntok], True, True, (kb_, QTh), (ps_,))
                k.act(pTf[:, par, 0:ng * ntok], ps_[:, 0:ng * ntok], AF.Exp, (ps_,), (pTf,), bias=-8.0)
                if g0 + ng == nkb:
                    i = ng - 1
                    k.tt(pTf[:, par, i * ntok:(i + 1) * ntok], pTf[:, par, i * ntok:(i + 1) * ntok], trib[:, 0:ntok], ALU.mult, (pTf, trib), (pTf,))
                for i in range(ng):
                    kb = g0 + i
                    k.mm(pso[0:65, 0:ntok], vb_[:, kb, :], pTf[:, par, i * ntok:(i + 1) * ntok], kb == 0, kb == nkb - 1, (vb_, pTf), (pso,))
            fox_finish_head(h, pso, ntok)

    def fox_finish_head(h, pso, ntok):
        if pso is not None:
            k.cp(oTf[:, 0:ntok], pso[0:65, 0:ntok], (pso,), (oTf,), e='act')
        psr = nps()
        k.mm(psr[0:64, 0:ntok], sel65[:], oTf[:, 0:ntok], True, True, (sel65, oTf), (psr,))
        k.op('dve', lambda g: g.reciprocal(out=recf[:, 0:ntok], in_=psr[0:64, 0:ntok]), (psr,), (recf,))
        k.tt(yfx[:, h, 0:ntok], oTf[0:64, 0:ntok], recf[:, 0:ntok], ALU.mult, (oTf, recf), (yfx,))

    Cst = k.sb([128, 4, 129], F32, 'Cst'); CsTb = k.sb([128, 4, 128], BF16, 'CsTb'); nselb = k.sb([128, 4, 4], BF16, 'nselb')
    Fm = k.sb([4, 1], F32, 'Fm'); Mm = k.sb([4, 1], F32, 'Mm'); Mpe_bc = k.sb([128, 4], F32, 'Mpe_bc'); Me_bc = k.sb([128, 4], F32, 'Me_bc')
    G4 = {n_: k.sb([4, TT], F32, 'g4_' + n_) for n_ in ('mi', 'mf', 'lf', 'F', 'a', 'M', 'negM', 'emt', 'zero', 'one', 'rden')}
    k.memset(G4['zero'], G4['zero'][:], 0.0); k.memset(G4['one'], G4['one'][:], 1.0)
    negbf = k.sb([4, 1], F32, 'negbf'); bicol = k.sb([4, 1], F32, 'bicol'); gncol = k.sb([128, 1], F32, 'gncol')
    mqT = k.sb([128, 4, TT], BF16, 'mqT'); mkT = k.sb([128, 4, TT], BF16, 'mkT'); so4 = k.sb([128, 4, TT], BF16, 'so4')
    mk_tok = k.sb([128, 512], BF16, 'mk_tok'); VM = k.sb([128, 4, 129], BF16, 'VM'); k.memset(VM, VM[:], 1.0)
    atok = k.sb([128, 4], F32, 'atok'); diag4 = k.sb([4, 4], F32, 'diag4'); dec_bc = k.sb([128, 4], F32, 'dec_bc'); wend = k.sb([128, 4], F32, 'wend')
    iwb = k.sb([128, TT], F32, 'iwb'); QpT = k.sb([128, TT], BF16, 'QpT'); Et = k.sb([128, TT], F32, 'Et'); SWb = k.sb([128, TT], BF16, 'SWb')
    Kw = k.sb([128, 128], BF16, 'Kw'); hh = k.sb([128, TT], F32, 'hh'); sqb = k.sb([128, TT], BF16, 'sqb'); rst = k.sb([128, TT], F32, 'rst')

    def mlstm_setup(l):
        k.dma('sp', lambda g: g.dma_start(out=negbf[:], in_=W['ml_bf'][l].rearrange("(h o) -> h o", o=1)), (), (negbf,))
        k.ts(negbf[:], negbf[:], -1.0, None, ALU.mult, None, (negbf,), (negbf,))
        k.dma('sp', lambda g: g.dma_start(out=bicol[:], in_=W['ml_bi'][l].rearrange("(h o) -> h o", o=1)), (), (bicol,))
        k.dma('sp', lambda g: g.dma_start(out=gncol[:], in_=W['ml_gn'][l].rearrange("(p o) -> p o", o=1)), (), (gncol,))

    def mlstm_zero_state():
        k.memset(Cst, Cst[:], 0.0); k.memset(CsTb, CsTb[:], 0.0); k.memset(nselb, nselb[:], 0.0)
        k.memset(Fm, Fm[:], 0.0); k.memset(Mm, Mm[:], 0.0); k.memset(Mpe_bc, Mpe_bc[:], 0.0)

    def mlstm_inproj(l, P, ntok):
        Win = W['w_in'][l]

        def c_mq(ps_, j, m):
            k.cp(mqT[:, j, 0:ntok], ps_[:, 0:ntok], (ps_,), (mqT,), e='act')
        proj_feat(hT, 8, ntok, Win, 0, C_MQ, 512, c_mq)

        def c_mk(ps_, j, m):
            k.op('act', lambda g: g.mul(out=mkT[:, j, 0:ntok], in_=ps_[:, 0:ntok], mul=128 ** -0.5), (ps_,), (mkT,))
        proj_feat(hT, 8, ntok, Win, 0, C_MK, 512, c_mk)

        def c_mkt(ps_, s, cb, nc_):
            k.op('act', lambda g: g.mul(out=mk_tok[0:P, :], in_=ps_[0:P, 0:512], mul=128 ** -0.5), (ps_,), (mk_tok,))
        proj_tok(hT, 8, P, 1, Win, 0, C_MK, 512, c_mkt)

        def c_mvt(ps_, s, cb, nc_):
            k.cp(VM[0:P, :, 0:128], ps_[0:P, 0:512].rearrange("p (h d) -> p h d", d=128), (ps_,), (VM,))
        proj_tok(hT, 8, P, 1, Win, 0, C_MV, 512, c_mvt)
        wb = wload(Win, 0, 1024, C_MI, 8)
        for gi, nm in enumerate(('mi', 'mf')):
            ps_ = nps()
            for kc in range(8):
                k.mm(ps_[0:4, 0:ntok], wb[:, kc, 4 * gi:4 * gi + 4], hT[:, kc, 0:ntok], kc == 0, kc == 7, (hT, wb), (ps_,))
            k.cp(G4[nm][:, 0:ntok], ps_[0:4, 0:ntok], (ps_,), (G4[nm],))

        def c_mo(ps_, j, m):
            k.act(so4[:, j, 0:ntok], ps_[:, 0:ntok], AF.Sigmoid, (ps_,), (so4,))
        proj_feat(hT, 8, ntok, Win, 0, C_MO, 512, c_mo)

    def mlstm_gates(ntok, scan_mask=None, a_override=None):
        g = G4
        k.act(g['lf'][:, 0:ntok], g['mf'][:, 0:ntok], AF.Exp, (g['mf'], negbf), (g['lf'],), bias=negbf[:], scale=-1.0)
        k.act(g['lf'][:, 0:ntok], g['lf'][:, 0:ntok], AF.Ln, (g['lf'],), (g['lf'],), bias=1.0)
        k.ts(g['lf'][:, 0:ntok], g['lf'][:, 0:ntok], -1.0, None, ALU.mult, None, (g['lf'],), (g['lf'],))
        k.ts(g['mi'][:, 0:ntok], g['mi'][:, 0:ntok], bicol[:], None, ALU.add, None, (g['mi'], bicol), (g['mi'],))

    def mlstm_prompt_tile(l):
        ntok = TT
        g = G4
        mlstm_gates(ntok)
        k.op('dve', lambda e: e.tensor_tensor_scan(out=g['F'][:, 0:ntok], data0=g['one'][:, 0:ntok], data1=g['lf'][:, 0:ntok], initial=Fm[:, 0:1],
                                                   op0=ALU.mult, op1=ALU.add), (g['one'], g['lf'], Fm), (g['F'],))
        k.tt(g['a'][:, 0:ntok], g['mi'][:, 0:ntok], g['F'][:, 0:ntok], ALU.subtract, (g['mi'], g['F']), (g['a'],))
        k.op('dve', lambda e: e.tensor_tensor_scan(out=g['M'][:, 0:ntok], data0=g['zero'][:, 0:ntok], data1=g['a'][:, 0:ntok], initial=Mm[:, 0:1],
                                                   op0=ALU.add, op1=ALU.max), (g['zero'], g['a'], Mm), (g['M'],))
        k.tt(g['emt'][:, 0:ntok], g['F'][:, 0:ntok], g['M'][:, 0:ntok], ALU.add, (g['F'], g['M']), (g['emt'],))
        k.act(g['emt'][:, 0:ntok], g['emt'][:, 0:ntok], AF.Exp, (g['emt'],), (g['emt'],), scale=-1.0)
        k.ts(g['negM'][:, 0:ntok], g['M'][:, 0:ntok], -1.0, None, ALU.mult, None, (g['M'],), (g['negM'],))
        ps_ = nps()
        k.tr(ps_[:, 0:4], g['a'][0:4, 0:ntok], identf[0:4, 0:4], (g['a'], identf), (ps_,))
        k.cp(atok[:], ps_[:, 0:4], (ps_,), (atok,))
        k.ts(diag4[:], identf[0:4, 0:4], g['M'][:, ntok - 1:ntok], None, ALU.mult, None, (identf, g['M']), (diag4,))
        ps_ = nps()
        k.mm(ps_[:, 0:4], onesf[0:4, 0:128], diag4[:], True, True, (onesf, diag4), (ps_,))
        k.cp(Me_bc[:], ps_[:, 0:4], (ps_,), (Me_bc,))
        k.tt(dec_bc[:], Mpe_bc[:], Me_bc[:], ALU.subtract, (Mpe_bc, Me_bc), (dec_bc,))
        k.act(dec_bc[:], dec_bc[:], AF.Exp, (dec_bc,), (dec_bc,))
        k.tt(wend[:], atok[:], Me_bc[:], ALU.subtract, (atok, Me_bc), (wend,))
        k.act(wend[:], wend[:], AF.Exp, (wend,), (wend,))
        for h in range(4):
            psn = nps()
            k.mm(psn[:, 0:ntok], selh[h][:], g['negM'][:, 0:ntok], True, True, (selh[h], g['negM']), (psn,))
            k.act(iwb[:, 0:ntok], psn[:, 0:ntok], AF.Exp, (psn, Mpe_bc), (iwb,), bias=Mpe_bc[:, h:h + 1])
            k.tt(QpT[:, 0:ntok], mqT[:, h, 0:ntok], iwb[:, 0:ntok], ALU.mult, (mqT, iwb), (QpT,))
            k.ts(Et[:, 0:ntok], psn[:, 0:ntok], atok[:, h:h + 1], 0.0, ALU.add, ALU.min, (psn, atok), (Et,))
            k.act(Et[:, 0:ntok], Et[:, 0:ntok], AF.Exp, (Et,), (Et,))
            k.tt(Et[:, 0:ntok], Et[:, 0:ntok], trif[:, 0:ntok], ALU.mult, (Et, trif), (Et,))
            pss = nps()
            k.mm(pss[:, 0:ntok], mkT[:, h, 0:ntok], mqT[:, h, 0:ntok], True, True, (mkT, mqT), (pss,))
            k.tt(SWb[:, 0:ntok], pss[:, 0:ntok], Et[:, 0:ntok], ALU.mult, (pss, Et), (SWb,))
            psnum = nps()
            k.mm(psnum[:, 0:ntok], VM[:, h, 0:128], SWb[:, 0:ntok], True, False, (VM, SWb), (psnum,))
            k.mm(psnum[:, 0:ntok], CsTb[:, h, :], QpT[:, 0:ntok], False, True, (CsTb, QpT), (psnum,))
            k.cp(arena_f[:, h * TT:h * TT + ntok], psnum[:, 0:ntok], (psnum,), (arena,), e='act')
            k.mm(psden[0:4, 0:ntok], onesel[:, h, :], SWb[:, 0:ntok], h == 0, False, (onesel, SWb), (psden,))
            k.mm(psden[0:4, 0:ntok], nselb[:, h, :], QpT[:, 0:ntok], False, h == 3, (nselb, QpT), (psden,))
            k.ts(Kw[:], mk_tok[:, h * 128:(h + 1) * 128], wend[:, h:h + 1], None, ALU.mult, None, (mk_tok, wend), (Kw,))
            psd = nps()
            k.mm(psd[:, 0:129], Kw[:], VM[:, h, :], True, True, (Kw, VM), (psd,))
            k.stt(Cst[:, h, :], Cst[:, h, :], dec_bc[:, h:h + 1], psd[:, 0:129], ALU.mult, ALU.add, (Cst, dec_bc, psd), (Cst,))
            k.cp(CsTb[:, h, :], Cst[:, h, 0:128], (Cst,), (CsTb,))
            k.cp(nselb[:, h, h:h + 1], Cst[:, h, 128:129], (Cst,), (nselb,))
        mlstm_finish(ntok)
        k.cp(Fm[:], g['F'][:, ntok - 1:ntok], (g['F'],), (Fm,))
        k.cp(Mm[:], g['M'][:, ntok - 1:ntok], (g['M'],), (Mm,))
        k.cp(Mpe_bc[:], Me_bc[:], (Me_bc,), (Mpe_bc,))

    def mlstm_finish(ntok):
        g = G4
        k.act(g['rden'][:, 0:ntok], psden[0:4, 0:ntok], AF.Abs, (psden,), (g['rden'],))
        k.tt(g['rden'][:, 0:ntok], g['rden'][:, 0:ntok], g['emt'][:, 0:ntok], ALU.max, (g['rden'], g['emt']), (g['rden'],))
        k.op('dve', lambda e: e.reciprocal(out=g['rden'][:, 0:ntok], in_=g['rden'][:, 0:ntok]), (g['rden'],), (g['rden'],))
        for h in range(4):
            psr = nps()
            k.mm(psr[:, 0:ntok], selh[h][:], g['rden'][:, 0:ntok], True, True, (selh[h], g['rden']), (psr,))
            k.tt(hh[:, 0:ntok], arena_f[:, h * TT:h * TT + ntok], psr[:, 0:ntok], ALU.mult, (arena, psr), (hh,))
            k.act(sqb[:, 0:ntok], hh[:, 0:ntok], AF.Square, (hh,), (sqb,))
            ps2 = nps()
            k.mm(ps2[:, 0:ntok], onesb[:, :], sqb[:, 0:ntok], True, True, (onesb, sqb), (ps2,))
            k.ts(rst[:, 0:ntok], ps2[:, 0:ntok], 1.0 / 128, EPS, ALU.mult, ALU.add, (ps2,), (rst,))
            k.act(rst[:, 0:ntok], rst[:, 0:ntok], AF.Sqrt, (rst,), (rst,))
            k.op('dve', lambda e: e.reciprocal(out=rst[:, 0:ntok], in_=rst[:, 0:ntok]), (rst,), (rst,))
            k.tt(hh[:, 0:ntok], hh[:, 0:ntok], rst[:, 0:ntok], ALU.mult, (hh, rst), (hh,))
            k.stt(yT['ml'][:, h, 0:ntok], hh[:, 0:ntok], gncol[:, 0:1], so4[:, h, 0:ntok], ALU.mult, ALU.mult, (hh, gncol, so4), (yT['ml'],))

    def mlstm_prompt_out(l):
        for h in range(4):
            ps_ = nps()
            k.tr(ps_[:, 0:128], Cst[:, h, 0:128], identf[:, :], (Cst, identf), (ps_,))
            k.cp(rowt[:, 0:128], ps_[:, 0:128], (ps_,), (rowt,), e='act')
            k.dma('sp', lambda g: g.dma_start(out=O['pmlC'][l, h], in_=rowt[:, 0:128]), (rowt,), (), True)
            k.dma('sp', lambda g: g.dma_start(out=O['pmln'][l, h].rearrange("(p o) -> p o", o=1), in_=Cst[:, h, 128:129]), (Cst,), (), True)
        k.tt(G4['rden'][:, 0:1], Fm[:], Mm[:], ALU.add, (Fm, Mm), (G4['rden'],))
        k.dma('sp', lambda g: g.dma_start(out=O['pmlm'][l].rearrange("(h o) -> h o", o=1), in_=G4['rden'][:, 0:1]), (G4['rden'],), (), True)

    def merge_out(l, P, ntok):
        Win = W['w_in'][l]
        Gt = arena[:, :, :].rearrange("p a t -> p (a t)")

        def cons_g(ps_, s, cb, nc_):
            k.act(Gt[0:P, cb:cb + nc_], ps_[0:P, 0:nc_], AF.Sigmoid, (ps_,), (arena,))
        proj_tok(hT, 8, P, 1, Win, 0, C_G, 3072, cons_g)
        for cb in range(2):
            cs = slice(cb * 512, (cb + 1) * 512)
            wb = wload96(W['w_br_s5'][l], cb * 512, 512)
            ps_ = nps()
            for tq in range(6):
                nr = 96 if tq < 5 else 32
                k.mm(ps_[0:P, :], ys5T[0:nr, tq, 0:P], wb[0:nr, tq, :], tq == 0, tq == 5, (ys5T, wb), (ps_,))
            k.tt(mrgf[0:P, cs], ps_[0:P, :], Gt[0:P, cb * 512:(cb + 1) * 512], ALU.mult, (ps_, arena), (mrgf,))
            wb = wload(W['w_br_fox'][l], 0, 512, cb * 512, 512, p=64)
            ps_ = nps()
            for h in range(8):
                k.mm(ps_[0:P, :], yfx[:, h, 0:P], wb[0:64, h, :], h == 0, h == 7, (yfx, wb), (ps_,))
            k.tt(rowt[0:P, :], ps_[0:P, :], Gt[0:P, 1024 + cb * 512:1024 + (cb + 1) * 512], ALU.mult, (ps_, arena), (rowt,))
            k.tt(mrgf[0:P, cs], mrgf[0:P, cs], rowt[0:P, :], ALU.add, (mrgf, rowt), (mrgf,))
            wb = wload(W['w_br_ml'][l], 0, 512, cb * 512, 512)
            ps_ = nps()
            for h in range(4):
                k.mm(ps_[0:P, :], yT['ml'][:, h, 0:P], wb[:, h, :], h == 0, h == 3, (yT['ml'], wb), (ps_,))
            k.tt(rowt[0:P, :], ps_[0:P, :], Gt[0:P, 2048 + cb * 512:2048 + (cb + 1) * 512], ALU.mult, (ps_, arena), (rowt,))
            k.tt(mrgf[0:P, cs], mrgf[0:P, cs], rowt[0:P, :], ALU.add, (mrgf, rowt), (mrgf,))
        k.cp(tmpb[0:P, :], mrgf[0:P, :], (mrgf,), (tmpb,))
        transpose_to_T(tmpb, P, 8, hT, 0, lambda c: tmpb[0:P, c * 128:(c + 1) * 128])

        def cons_o(ps_, s, cb, nc_):
            k.tt(xres[0:P, 0, cb:cb + nc_], xres[0:P, 0, cb:cb + nc_], ps_[0:P, 0:nc_], ALU.add, (xres, ps_), (xres,))
        proj_tok(hT, 8, P, 1, W['w_out'][l], 0, 0, D, cons_o)

    def s5_inproj(l, ntok):
        def c_u(ps_, j, m):
            k.cp(uT[0:m, j, 0:ntok], ps_[0:m, 0:ntok], (ps_,), (uT,), e='act')
        proj_feat(hT, 8, ntok, W['w_in'][l], 0, C_S5, 512, c_u, chunk=96)

    BIG = 1.0e30

    def carve(parent, off, shape, dt):
        pt_ = parent.t
        pshape = list(pt_.shape)
        n0 = 1
        for d_ in pshape[1:]:
            n0 *= d_
        flat = pt_.reshape([pshape[0], n0]) if len(pshape) > 2 else pt_
        esz = mybir.dt.size(pt_.dtype)
        n = 1
        for d_ in shape[1:]:
            n *= d_
        nbytes = n * mybir.dt.size(dt)
        assert off % esz == 0 and nbytes % esz == 0 and (off + nbytes) <= n0 * esz and shape[0] <= pshape[0]
        ap = flat[0:shape[0], off // esz:(off + nbytes) // esz]
        if dt != pt_.dtype:
            ap = ap.bitcast(dt)
        if len(shape) == 3:
            ap = ap.rearrange("p (a b) -> p a b", b=shape[2])
        elif len(shape) == 4:
            ap = ap.rearrange("p (a b c) -> p a b c", b=shape[2], c=shape[3])
        return ap

    def sample_group():
        P = ntok = NS
        GB = []
        for par in range(2):
            GB.append(dict(tok=kbuf[par], gK=carve(kbuf[par], 0, [128, 512], F32), gV=carve(kbuf[par], 2048, [128, 512], F32),
                           gL=carve(kbuf[par], 4096, [128, 8], F32)))
        kpg = carve(vbuf[0], 0, [64, 8, 128], BF16); vpg = carve(vbuf[0], 2048, [128, 8, 65], BF16)
        k.op('pool', lambda g: g.memset(vpg, 1.0), (), (vbuf[0],))
        Et2 = carve(vbuf[1], 0, [128, 32], F32); PT2 = carve(vbuf[1], 128, [128, 8, 4], BF16)
        nb8 = carve(vbuf[1], 256, [128, 8], F32); Rc = carve(vbuf[1], 288, [128, 8], F32)
        Oacc = carve(Cst, 0, [65, 8, NS], F32)
        C0 = carve(CsTb, 0, [128, 128], F32); C0T = carve(CsTb, 512, [128, 128], BF16)
        X0r = carve(mrgf, 0, [128, 16, 16], F32); X0i = carve(mrgf, 1024, [128, 16, 16], F32)
        X0br = carve(mrgf, 2048, [128, 16, 16], BF16); X0bi = carve(mrgf, 2560, [128, 16, 16], BF16)
        n0tok = carve(junkb, 0, [16, 512], F32)
        ptb = carve(rowt, 0, [128, SB * NPAGE], I32)
        idx_all = k.sb([128, SB * NPAGE], I32, 'idx_all'); iot = k.sb([128, 1], I32, 'iot')
        E16 = k.sb([16, NS], F32, 'E16'); ETf = k.sb([NS, 16], F32, 'ETf'); blkF = k.sb([NS, NS], F32, 'blkF'); blkB = k.sb([NS, NS], BF16, 'blkB')
        ones8 = k.sb([8, 128], F32, 'ones8'); k.memset(ones8, ones8[:], 1.0)
        sel8buf = k.sb([8, 128], F32, 'sel8buf')
        eD = k.sb([8, NS], F32, 'eD'); negDk = k.sb([NS, 8], F32, 'negDk')
        S4 = {n_: k.sb([4, NS], F32, 's4_' + n_) for n_ in ('m0full', 'Mefull', 'iw4', 'a2')}
        m0T = k.sb([4, 16], F32, 'm0T'); dec4 = k.sb([4, 16], F32, 'dec4'); mnew4 = k.sb([4, 16], F32, 'mnew4'); dd4 = k.sb([4, 64], F32, 'dd4')
        Metok = k.sb([NS, 4], F32, 'Metok'); decT = k.sb([16, 4], F32, 'decT'); decbc = k.sb([128, 64], F32, 'decbc')
        n0T = k.sb([128, 64], F32, 'n0T'); n0sel = k.sb([128, 16, 4, 4], BF16, 'n0sel'); k.memset(n0sel, n0sel[:], 0.0)
        Wm = k.sb([NS, 16], F32, 'Wm'); Wmb = k.sb([NS, 16], BF16, 'Wmb'); Vw = k.sb([NS, 128], BF16, 'Vw')
        rmaskF, reset4 = G4['one'], G4['zero']

        k.memset(E16, E16[:], 1.0)
        k.op('pool', lambda g: g.affine_select(out=E16[:], in_=E16[:], pattern=[[1, NS]], compare_op=ALU.is_ge, fill=0.0, base=0, channel_multiplier=-ST), (E16,), (E16,))
        k.op('pool', lambda g: g.affine_select(out=E16[:], in_=E16[:], pattern=[[-1, NS]], compare_op=ALU.is_ge, fill=0.0, base=ST - 1, channel_multiplier=ST), (E16,), (E16,))
        ps_ = nps()
        k.mm(ps_[0:NS, 0:NS], E16[:, :], E16[:, :], True, True, (E16,), (ps_,))
        k.tt(blkF[:], ps_[0:NS, 0:NS], trif[0:NS, 0:NS], ALU.mult, (ps_, trif), (blkF,))
        k.cp(blkB[:], blkF[:], (blkF,), (blkB,))
        ps_ = nps()
        k.tr(ps_[0:NS, 0:16], E16[:, :], identf[0:16, 0:16], (E16, identf), (ps_,))
        k.cp(ETf[:], ps_[0:NS, 0:16], (ps_,), (ETf,))
        k.dma('sp', lambda g: g.dma_start(out=ptb, in_=I['pt'][0].partition_broadcast(128)), (), (rowt,))
        k.op('pool', lambda g: g.iota(iot[:], pattern=[[0, 1]], base=0, channel_multiplier=1), (), (iot,))
        k.ts(idx_all[:], ptb, PAGE, iot[:], ALU.mult, ALU.add, (rowt, iot), (idx_all,))
        k.memset(rmaskF, rmaskF[:], 1.0); k.memset(rmaskF, rmaskF[:, 0:NS].rearrange("p (b t) -> p b t", t=ST)[:, :, 0], 0.0)
        k.memset(reset4, reset4[:], BIG); k.memset(reset4, reset4[:, 0:NS].rearrange("p (b t) -> p b t", t=ST)[:, :, 0], -BIG)

        k.dma('sp', lambda g: g.dma_start(out=xres[0:NS, 0, :], in_=I['xs'][:, :]), (), (xres,))

        def fox_sample(l):
            Win = W['w_in'][l]

            def cons_ff(ps_, s, cb, nc_):
                fox_gates(ps_, P, None, O['sflf'][l, :, :])
                p1 = nps()
                k.mm(p1[0:P, 0:8], blkF[:, :], lf8[0:P, :], True, True, (blkF, lf8), (p1,))
                k.cp(F8[0:P, :], p1[0:P, 0:8], (p1,), (F8,))
                k.ts(negDk[:], F8[0:P, :], -1.0, -8.0, ALU.mult, ALU.add, (F8,), (negDk,))
                aqv = AQ[:, :].rearrange("p (h c) -> p h c", c=64)
                k.cp(aqv[0:P, :, 0], F8[0:P, :], (F8,), (AQ,))
                k.cp(t8[0:P, :], aqv[0:P, :, 0], (AQ,), (t8,))
                k.tt(aqv[0:P, :, 1], F8[0:P, :], t8[0:P, :], ALU.subtract, (F8, t8), (AQ,))
                transpose_heads(AQ, P, AQTh, 0, lambda c: AQ[0:P, c * 128:(c + 1) * 128], nrows=32)
                p2 = nps()
                k.tr(p2[0:8, 0:P], F8[0:P, 0:8], identf[0:P, 0:P], (F8, identf), (p2,))
                k.act(eD[:, 0:P], p2[0:8, 0:P], AF.Exp, (p2,), (eD,))
            proj_tok(hT, 8, P, 1, Win, 0, C_FF, 8, cons_ff)

            def cons_fq(ps_, s, cb, nc_):
                head_rms(ps_, P, 8, 64, gvb['fgq'], tmpb, tmpb[0:P, 0:512].rearrange("p (h d) -> p h d", d=64), 0.125)
                transpose_heads(tmpb, P, QTh, 0, lambda c: tmpb[0:P, c * 128:(c + 1) * 128])
            proj_tok(hT, 8, P, 1, Win, 0, C_FQ, 512, cons_fq)

            def cons_fk(ps_, s, cb, nc_):
                head_rms(ps_, P, 8, 64, gvb['fgk'], kout, kout[0:P, 0:512].rearrange("p (h d) -> p h d", d=64), 1.0)
                k.dma('sp', lambda g: g.dma_start(out=O['sfk'][l, :, :], in_=kout[0:P, 0:512]), (kout,), (), True)
                k.cp(tmpb[0:P, 0:512], kout[0:P, 0:512], (kout,), (tmpb,))
                transpose_heads(tmpb, P, kst, 0, lambda c: tmpb[0:P, c * 128:(c + 1) * 128])
            proj_tok(hT, 8, P, 1, Win, 0, C_FK, 512, cons_fk)

            def cons_fv(ps_, s, cb, nc_):
                k.cp(rowt[0:P, 0:512], ps_[0:P, 0:512], (ps_,), (rowt,), e='act')
                k.dma('sp', lambda g: g.dma_start(out=O['sfv'][l, :, :], in_=rowt[0:P, 0:512]), (rowt,), (), True)
                k.cp(vst[0:P, :, 0:64], rowt[0:P, 0:512].rearrange("p (h d) -> p h d", d=64), (rowt,), (vst,))
            proj_tok(hT, 8, P, 1, Win, 0, C_FV, 512, cons_fv)

            if l > 0:
                k.ts(idx_all[:], idx_all[:], NPOOL * PAGE, None, ALU.add, None, (idx_all,), (idx_all,))
            ckf = I['ck'].rearrange("l n c -> (l n) c"); cvf = I['cv'].rearrange("l n c -> (l n) c"); clff = I['clf'].rearrange("l n c -> (l n) c")
            k.op('pool', lambda g: g.memset(Oacc, 0.0), (), (Cst,))
            cnt = 0
            for b in range(SB):
                k.op('pool', lambda g: g.memset(Rc, 0.0), (), (vbuf[1],))
                for j in range(NPAGE - 1, -1, -1):
                    G = GB[cnt % 2]; cnt += 1
                    col = b * NPAGE + j
                    off = bass.IndirectOffsetOnAxis(ap=idx_all[:, col:col + 1], axis=0)
                    k.dma('pool', lambda g: g.indirect_dma_start(out=G['gK'], out_offset=None, in_=ckf, in_offset=off), (idx_all,), (G['tok'],))
                    k.dma('pool', lambda g: g.indirect_dma_start(out=G['gV'], out_offset=None, in_=cvf, in_offset=off), (idx_all,), (G['tok'],))
                    k.dma('pool', lambda g: g.indirect_dma_start(out=G['gL'], out_offset=None, in_=clff, in_offset=off), (idx_all,), (G['tok'],))
                    p1 = nps()
                    k.mm(p1[:, 0:8], triR[:, :], G['gL'], True, True, (triR, G['tok']), (p1,))
                    k.stt(nb8, p1[:, 0:8], 1.0, Rc, ALU.mult, ALU.add, (p1, vbuf[1]), (vbuf[1],))
                    k.ts(nb8, nb8, -8.0, None, ALU.add, None, (vbuf[1],), (vbuf[1],))
                    p2 = nps()
                    k.mm(p2[:, 0:8], onesf[:, :], G['gL'], True, True, (onesf, G['tok']), (p2,))
                    k.tt(Rc, Rc, p2[:, 0:8], ALU.add, (vbuf[1], p2), (vbuf[1],))
                    k.cp(tmpb[:, 0:512], G['gK'], (G['tok'],), (tmpb,), e='act')
                    transpose_heads(tmpb, 128, vbuf[0], 0, lambda c: tmpb[:, c * 128:(c + 1) * 128], dst_ap=kpg)
                    k.cp(vpg[:, :, 0:64], G['gV'].rearrange("p (h d) -> p h d", d=64), (G['tok'],), (vbuf[0],))
                    ps_ = nps()
                    for h in range(8):
                        k.mm(ps_[:, h * 4:(h + 1) * 4], kpg[0:64, h, :], QTh[0:64, h, ST * b:ST * b + ST], True, True, (vbuf[0], QTh), (ps_,))
                    k.tt(Et2.rearrange("p (h q) -> p h q", q=ST), ps_[:, 0:32].rearrange("p (h q) -> p h q", q=ST),
                         nb8.unsqueeze(2).to_broadcast([128, 8, ST]), ALU.add, (ps_, vbuf[1]), (vbuf[1],))
                    k.act(PT2.rearrange("p h q -> p (h q)"), Et2, AF.Exp, (vbuf[1],), (vbuf[1],))
                    pso = nps()
                    for h in range(8):
                        k.mm(pso[0:65, h * 4:(h + 1) * 4], vpg[:, h, :], PT2[:, h, :], True, True, (vbuf[0], vbuf[1]), (pso,))
                    k.tt(Oacc[:, :, ST * b:ST * b + ST], Oacc[:, :, ST * b:ST * b + ST], pso[0:65, 0:32].rearrange("p (h q) -> p h q", q=ST),
                         ALU.add, (Cst, pso), (Cst,))
            for h in range(8):
                ps_ = nps()
                k.mm(ps_[0:P, 0:P], kst[0:64, h, 0:P], QTh[0:64, h, 0:P], True, False, (kst, QTh), (ps_,))
                k.mm(ps_[0:P, 0:P], onesb[0:2, 0:P], AQTh[0:2, h, 0:P], False, True, (onesb, AQTh), (ps_,))
                k.act(pTf[0:P, 0, 0:P], ps_[0:P, 0:P], AF.Exp, (ps_, negDk), (pTf,), bias=negDk[:, h:h + 1])
                k.tt(pTf[0:P, 0, 0:P], pTf[0:P, 0, 0:P], blkB[:, :], ALU.mult, (pTf, blkB), (pTf,))
                pso = ps_acc[h % 2]
                k.mm(pso[0:65, 0:P], vst[0:P, h, :], pTf[0:P, 0, 0:P], True, True, (vst, pTf), (pso,))
                psr = nps()
                k.op('pool', lambda g: g.affine_select(out=sel8buf[:], in_=ones8[:], pattern=[[0, 128]], compare_op=ALU.is_equal,
                                                       fill=0.0, base=-h, channel_multiplier=1), (ones8,), (sel8buf,))
                k.mm(psr[0:65, 0:P], sel8buf[:, 0:65], eD[:, 0:P], True, True, (sel8buf, eD), (psr,))
                k.tt(oTf[:, 0:P], Oacc[:, h, :], psr[0:65, 0:P], ALU.mult, (Cst, psr), (oTf,))
                k.tt(oTf[:, 0:P], oTf[:, 0:P], pso[0:65, 0:P], ALU.add, (oTf, pso), (oTf,))
                fox_finish_head(h, None, P)

        def s5_sample(l):
            L, nj = ST, SB
            s5_inproj(l, ntok)
            for q in range(16):
                for r in range(2):
                    k.dma('sp', lambda g: g.dma_start(out=X0r[r * 64:(r + 1) * 64, q, :], in_=I['s5re'][l][:, 2 * q + r, :].rearrange("b p -> p b"),
                                                      allow_slow_non_contiguous=True), (), (mrgf,))
                    k.dma('sp', lambda g: g.dma_start(out=X0i[r * 64:(r + 1) * 64, q, :], in_=I['s5im'][l][:, 2 * q + r, :].rearrange("b p -> p b"),
                                                      allow_slow_non_contiguous=True), (), (mrgf,))
            k.cp(X0br, X0r, (mrgf,), (mrgf,)); k.cp(X0bi, X0i, (mrgf,), (mrgf,))
            psS = [nps(), nps()]
            for q in range(16):
                po, tq = 32 * (q % 3), q // 3
                for ri in range(2):
                    for sp in range(L):
                        k.op('pe', lambda g: g.matmul(psS[ri][:, q * nj:(q + 1) * nj], W1[po:po + 32, L - 1 - sp, ri, tq, :],
                                                      uT[po:po + 32, tq, sp:ntok:L], start=(sp == 0), stop=(sp == L - 1)), (S5T, uT), (psS[ri],))
            bc16 = lambda a: a.unsqueeze(2).to_broadcast([128, 16, 16])
            rd = (S5T, mrgf, psS[0], psS[1])
            def mop(fn):
                k.op('dve', fn, rd, (S5T,))
            mop(lambda g: g.tensor_tensor(out=M1[:], in0=X0r, in1=bc16(pwr[:, 4, :]), op=ALU.mult))
            mop(lambda g: g.tensor_tensor(out=M3[:], in0=X0i, in1=bc16(pwi[:, 4, :]), op=ALU.mult))
            mop(lambda g: g.tensor_tensor(out=M1[:], in0=M1[:], in1=M3[:], op=ALU.subtract))
            mop(lambda g: g.tensor_tensor(out=M2[:], in0=X0r, in1=bc16(pwi[:, 4, :]), op=ALU.mult))
            mop(lambda g: g.tensor_tensor(out=M3[:], in0=X0i, in1=bc16(pwr[:, 4, :]), op=ALU.mult))
            mop(lambda g: g.tensor_tensor(out=M2[:], in0=M2[:], in1=M3[:], op=ALU.add))
            mop(lambda g: g.tensor_tensor(out=M1[:], in0=M1[:], in1=psS[0][:, 0:256].rearrange("p (q j) -> p q j", j=nj), op=ALU.add))
            mop(lambda g: g.tensor_tensor(out=M2[:], in0=M2[:], in1=psS[1][:, 0:256].rearrange("p (q j) -> p q j", j=nj), op=ALU.add))
            for q in range(16):
                for r in range(2):
                    k.dma('sp', lambda g: g.dma_start(out=O['ss5re'][l][:, 2 * q + r, :].rearrange("b p -> p b"), in_=M1[r * 64:(r + 1) * 64, q, :],
                                                      allow_slow_non_contiguous=True), (S5T,), (), True)
                    k.dma('sp', lambda g: g.dma_start(out=O['ss5im'][l][:, 2 * q + r, :].rearrange("b p -> p b"), in_=M2[r * 64:(r + 1) * 64, q, :],
                                                      allow_slow_non_contiguous=True), (S5T,), (), True)
            s5_outputs(l, ntok, L, nj, X0br, X0bi, None, xtok=mrgf)

        def mlstm_sample(l):
            g = G4
            s4 = S4
            mlstm_inproj(l, P, ntok)
            mlstm_gates(ntok)
            k.dma('sp', lambda e: e.dma_start(out=m0T[:], in_=I['mlm'][l].rearrange("b h -> h b"), allow_slow_non_contiguous=True), (), (m0T,))
            v3 = lambda a: a[:, 0:ntok].rearrange("p (b t) -> p b t", t=ST)
            k.cp(v3(s4['m0full']), m0T[:, :].unsqueeze(2).to_broadcast([4, SB, ST]), (m0T,), (s4['m0full'],))
            k.op('dve', lambda e: e.tensor_tensor_scan(out=g['F'][:, 0:ntok], data0=rmaskF[:, 0:ntok], data1=g['lf'][:, 0:ntok], initial=0.0,
                                                       op0=ALU.mult, op1=ALU.add), (rmaskF, g['lf']), (g['F'],))
            k.tt(g['a'][:, 0:ntok], g['mi'][:, 0:ntok], g['F'][:, 0:ntok], ALU.subtract, (g['mi'], g['F']), (g['a'],))
            k.tt(s4['a2'][:, 0:ntok], g['a'][:, 0:ntok], s4['m0full'][:, 0:ntok], ALU.max, (g['a'], s4['m0full']), (s4['a2'],))
            k.op('dve', lambda e: e.tensor_tensor_scan(out=g['M'][:, 0:ntok], data0=reset4[:, 0:ntok], data1=s4['a2'][:, 0:ntok], initial=-BIG,
                                                       op0=ALU.min, op1=ALU.max), (reset4, s4['a2']), (g['M'],))
            k.tt(g['emt'][:, 0:ntok], g['F'][:, 0:ntok], g['M'][:, 0:ntok], ALU.add, (g['F'], g['M']), (g['emt'],))
            k.cp(mnew4[:], v3(g['emt'])[:, :, ST - 1], (g['emt'],), (mnew4,))
            k.dma('sp', lambda e: e.dma_start(out=O['smlm'][l].rearrange("b h -> h b"), in_=mnew4[:], allow_slow_non_contiguous=True), (mnew4,), (), True)
            k.act(g['emt'][:, 0:ntok], g['emt'][:, 0:ntok], AF.Exp, (g['emt'],), (g['emt'],), scale=-1.0)
            k.ts(g['negM'][:, 0:ntok], g['M'][:, 0:ntok], -1.0, None, ALU.mult, None, (g['M'],), (g['negM'],))
            k.cp(v3(s4['Mefull']), v3(g['M'])[:, :, ST - 1:ST].to_broadcast([4, SB, ST]), (g['M'],), (s4['Mefull'],))
            k.tt(s4['iw4'][:, 0:ntok], s4['m0full'][:, 0:ntok], g['M'][:, 0:ntok], ALU.subtract, (s4['m0full'], g['M']), (s4['iw4'],))
            k.act(s4['iw4'][:, 0:ntok], s4['iw4'][:, 0:ntok], AF.Exp, (s4['iw4'],), (s4['iw4'],))
            k.tt(dec4[:], m0T[:], v3(g['M'])[:, :, ST - 1], ALU.subtract, (m0T, g['M']), (dec4,))
            k.act(dec4[:], dec4[:], AF.Exp, (dec4,), (dec4,))
            ps_ = nps()
            k.tr(ps_[0:ntok, 0:4], g['a'][0:4, 0:ntok], identf[0:4, 0:4], (g['a'], identf), (ps_,))
            k.cp(atok[0:ntok, :], ps_[0:ntok, 0:4], (ps_,), (atok,))
            ps_ = nps()
            k.tr(ps_[0:ntok, 0:4], s4['Mefull'][0:4, 0:ntok], identf[0:4, 0:4], (s4['Mefull'], identf), (ps_,))
            k.cp(Metok[:], ps_[0:ntok, 0:4], (ps_,), (Metok,))
            k.tt(wend[0:ntok, :], atok[0:ntok, :], Metok[:], ALU.subtract, (atok, Metok), (wend,))
            k.act(wend[0:ntok, :], wend[0:ntok, :], AF.Exp, (wend,), (wend,))
            ps_ = nps()
            k.tr(ps_[0:16, 0:4], dec4[0:4, 0:16], identf[0:4, 0:4], (dec4, identf), (ps_,))
            k.cp(decT[:], ps_[0:16, 0:4], (ps_,), (decT,))
            k.tt(dd4[:, :].rearrange("p (a b) -> p a b", b=16), identf[0:4, 0:4].unsqueeze(2).to_broadcast([4, 4, 16]),
                 dec4[:, :].unsqueeze(1).to_broadcast([4, 4, 16]), ALU.mult, (identf, dec4), (dd4,))
            ps_ = nps()
            k.mm(ps_[:, 0:64], onesf[0:4, 0:128], dd4[:, :], True, True, (onesf, dd4), (ps_,))
            k.cp(decbc[:], ps_[:, 0:64], (ps_,), (decbc,))
            k.dma('sp', lambda e: e.dma_start(out=n0T[:], in_=I['mln'][l].rearrange("b h d -> d (b h)"), allow_slow_non_contiguous=True), (), (n0T,))
            for h in range(4):
                k.cp(n0sel[:, :, h, h], n0T[:, :].rearrange("p (b h) -> p b h", h=4)[:, :, h], (n0T,), (n0sel,))
            k.dma('sp', lambda e: e.dma_start(out=n0tok, in_=I['mln'][l].rearrange("b h d -> b (h d)")), (), (junkb,))
            for h in range(4):
                psn = nps()
                k.mm(psn[0:ntok, 0:ntok], selh[h][:, 0:ntok], g['negM'][:, 0:ntok], True, True, (selh[h], g['negM']), (psn,))
                psi = nps()
                k.mm(psi[:, 0:ntok], selh[h][:, :], s4['iw4'][:, 0:ntok], True, True, (selh[h], s4['iw4']), (psi,))
                k.tt(QpT[:, 0:ntok], mqT[:, h, 0:ntok], psi[:, 0:ntok], ALU.mult, (mqT, psi), (QpT,))
                k.ts(Et[0:ntok, 0:ntok], psn[0:ntok, 0:ntok], atok[0:ntok, h:h + 1], 0.0, ALU.add, ALU.min, (psn, atok), (Et,))
                k.act(Et[0:ntok, 0:ntok], Et[0:ntok, 0:ntok], AF.Exp, (Et,), (Et,))
                k.tt(Et[0:ntok, 0:ntok], Et[0:ntok, 0:ntok], blkF[:, :], ALU.mult, (Et, blkF), (Et,))
                pss = nps()
                k.mm(pss[0:ntok, 0:ntok], mkT[:, h, 0:ntok], mqT[:, h, 0:ntok], True, True, (mkT, mqT), (pss,))
                k.tt(SWb[0:ntok, 0:ntok], pss[0:ntok, 0:ntok], Et[0:ntok, 0:ntok], ALU.mult, (pss, Et), (SWb,))
                psnum = ps_acc[0]
                k.mm(psnum[:, 0:ntok], VM[0:ntok, h, 0:128], SWb[0:ntok, 0:ntok], True, False, (VM, SWb), (psnum,))
                k.mm(psden[0:4, 0:ntok], onesel[0:ntok, h, :], SWb[0:ntok, 0:ntok], h == 0, False, (onesel, SWb), (psden,))
                k.ts(Wm[:], ETf[:], wend[0:ntok, h:h + 1], None, ALU.mult, None, (ETf, wend), (Wm,))
                k.cp(Wmb[:], Wm[:], (Wm,), (Wmb,))
                psn2 = ps_acc[1]
                k.mm(psn2[0:16, 0:128], Wmb[:, :], mk_tok[0:ntok, h * 128:(h + 1) * 128], True, True, (Wmb, mk_tok), (psn2,))
                k.stt(n0tok[:, h * 128:(h + 1) * 128], n0tok[:, h * 128:(h + 1) * 128], decT[:, h:h + 1], psn2[0:16, 0:128], ALU.mult, ALU.add,
                      (junkb, decT, psn2), (junkb,))
                for b in range(SB):
                    k.dma('sp', lambda e: e.dma_start(out=C0, in_=I['mlC'][l, b, h]), (), (CsTb,))
                    pt2 = nps()
                    k.tr(pt2[:, 0:128], C0, identf[:, :], (CsTb, identf), (pt2,))
                    k.cp(C0T, pt2[:, 0:128], (pt2,), (CsTb,), e='act')
                    cs = slice(ST * b, ST * b + ST)
                    k.mm(psnum[:, cs], C0T, QpT[:, cs], False, b == SB - 1, (CsTb, QpT), (psnum,))
                    k.mm(psden[0:4, cs], n0sel[:, b, h, :], QpT[:, cs], False, (h == 3 and b == SB - 1), (n0sel, QpT), (psden,))
                    k.ts(Vw[:], VM[0:ntok, h, 0:128], Wm[:, b:b + 1], None, ALU.mult, None, (VM, Wm), (Vw,))
                    psd = nps()
                    k.mm(psd[:, 0:128], Vw[:, :], mk_tok[0:ntok, h * 128:(h + 1) * 128], True, True, (Vw, mk_tok), (psd,))
                    k.stt(C0, C0, decbc[:, h * 16 + b:h * 16 + b + 1], psd[:, 0:128], ALU.mult, ALU.add, (CsTb, decbc, psd), (CsTb,))
                    k.dma('sp', lambda e: e.dma_start(out=O['smlC'][l, b, h], in_=C0), (CsTb,), (), True)
                k.cp(arena_f[:, h * TT:h * TT + ntok], psnum[:, 0:ntok], (psnum,), (arena,), e='act')
            mlstm_finish(ntok)
            k.dma('sp', lambda e: e.dma_start(out=O['smln'][l].rearrange("b h d -> b (h d)"), in_=n0tok), (junkb,), (), True)

        def cross_attn_sample(l):
            qT = cross_q(l, P, ntok)
            for b in range(SB):
                G = GB[b % 2]
                for mc in range(2):
                    k.dma('sp', lambda e: e.dma_start(out=G['gK'], in_=I['cmk'][l, b, mc * 128:(mc + 1) * 128, :]), (), (G['tok'],))
                    k.cp(tmpb[:, 0:512], G['gK'], (G['tok'],), (tmpb,), e='act')
                    transpose_to_T(tmpb, 128, 4, memK, mc * 128, lambda c: tmpb[:, c * 128:(c + 1) * 128])
                    k.dma('sp', lambda e: e.dma_start(out=G['gV'], in_=I['cmv'][l, b, mc * 128:(mc + 1) * 128, :]), (), (G['tok'],))
                    k.cp(memV[:, mc, :], G['gV'], (G['tok'],), (memV,))
                cs = slice(ST * b, ST * b + ST)
                ps_ = nps()
                for h in range(4):
                    for mc in range(2):
                        c0 = (h * 2 + mc) * 4
                        k.mm(ps_[:, c0:c0 + 4], memK[:, h, mc * 128:(mc + 1) * 128], qT[:, h, cs], True, True, (memK, qT), (ps_,))
                k.act(PT2.rearrange("p h q -> p (h q)"), ps_[:, 0:32], AF.Exp, (ps_,), (vbuf[1],), bias=-8.0)
                pso = nps()
                for h in range(4):
                    for mc in range(2):
                        k.mm(pso[:, h * 4:(h + 1) * 4], memV[:, mc, h * 128:(h + 1) * 128], PT2[:, h * 2 + mc, :], mc == 0, mc == 1, (memV, vbuf[1]), (pso,))
                k.cp(arena_f[:, 0:4 * TT].rearrange("p (h t) -> p h t", t=TT)[:, :, cs], pso[:, 0:16].rearrange("p (h q) -> p h q", q=ST), (pso,), (arena,), e='act')
                for h in range(4):
                    for mc in range(2):
                        k.mm(psden[0:4, cs], onesel[:, h, :], PT2[:, h * 2 + mc, :], h == 0 and mc == 0, h == 3 and mc == 1, (onesel, vbuf[1]), (psden,))
            cross_finish(l, P, ntok)

        for l in range(DEPTH):
            bcast_load(gvb['fgq'], W['fox_gq'][l], 64)
            bcast_load(gvb['fgk'], W['fox_gk'][l], 64)
            bcast_load(bfb, W['fox_bf'][l], 8)
            s5_setup(l)
            mlstm_setup(l)
            bcast_load(gb, W['g_mix'][l], D)
            rmsnorm_T(xres, P, 1, gb, hT, tmpb)
            fox_sample(l)
            s5_sample(l)
            mlstm_sample(l)
            merge_out(l, P, ntok)
            cross_attn_sample(l)
            mlp(l, P, ntok)
        k.dma('sp', lambda g: g.dma_start(out=O['ys'][:, :], in_=xres[0:P, 0, :]), (xres,), (), True)
        k.memset(G4['zero'], G4['zero'][:], 0.0); k.memset(G4['one'], G4['one'][:], 1.0)

    def prompt_group():
        for l in range(DEPTH):
            memory_kv(l)
            bcast_load(gvb['fgq'], W['fox_gq'][l], 64)
            bcast_load(gvb['fgk'], W['fox_gk'][l], 64)
            bcast_load(bfb, W['fox_bf'][l], 8)
            k.memset(Fcarry, Fcarry[:], 0.0)
            s5_setup(l)
            mlstm_setup(l)
            mlstm_zero_state()
            for ti in range(NT):
                r0 = ti * TT
                if l == 0:
                    k.dma('sp', lambda g: g.dma_start(out=xres[:, 0, :], in_=I['xp'][r0:r0 + 128, :]), (), (xres,))
                else:
                    k.dma('sp', lambda g: g.dma_start(out=xres[:, 0, :], in_=xscr[r0:r0 + 128, :]), (XSCR,), (xres,))
                bcast_load(gb, W['g_mix'][l], D)
                rmsnorm_T(xres, 128, 1, gb, hT, tmpb)
                fox_prompt_tile(l, ti)
                s5_inproj(l, TT)
                s5_tile_prompt(l, TT)
                mlstm_inproj(l, 128, TT)
                mlstm_prompt_tile(l)
                merge_out(l, 128, TT)
                cross_attn_prompt(l)
                mlp(l, 128, TT)
                if l == 0:
                    k.dma('sp', lambda g: g.dma_start(out=xscr[r0:r0 + 128, :], in_=xres[:, 0, :]), (xres,), (XSCR,))
                else:
                    k.dma('sp', lambda g: g.dma_start(out=O['yp'][r0:r0 + 128, :], in_=xres[:, 0, :]), (xres,), (), True)
            s5_final_state_out(l)
            mlstm_prompt_out(l)

    if full:
        sample_group()
    prompt_group()
    k.finish()
    return nc


_NC_CACHE = {}


def kernel(**inp):
    f = lambda a: np.ascontiguousarray(np.asarray(a))
    if 'nc' not in _NC_CACHE:
        _NC_CACHE['nc'] = build()
    nc = _NC_CACHE['nc']
    wnames = ['g_mix', 'w_in', 's5_a_re', 's5_a_im', 's5_log_step', 's5_b_re', 's5_b_im', 's5_c_re', 's5_c_im', 's5_d',
              's5_w_glu', 's5_b_glu', 'fox_gq', 'fox_gk', 'fox_bf', 'ml_bi', 'ml_bf', 'ml_gn', 'w_br_s5', 'w_br_fox',
              'w_br_ml', 'w_out', 'g_cross', 'w_cq', 'cross_gq', 'g_mem', 'w_mk', 'w_mv', 'cross_gk', 'w_co', 'g_mlp',
              'w_up', 'w_down']
    shared = {n: f(inp[n]) for n in wnames}
    shared['ck'] = f(inp['cache_fox_k']).reshape(DEPTH, NPOOL * PAGE, 512)
    shared['cv'] = f(inp['cache_fox_v']).reshape(DEPTH, NPOOL * PAGE, 512)
    shared['clf'] = f(inp['cache_fox_logf']).reshape(DEPTH, NPOOL * PAGE, 8)
    in_maps = []
    for c in range(NCORE):
        b = c % 4
        sl = slice(SB * c, SB * (c + 1))
        m = dict(shared)
        m['xp'] = f(inp['x_prompt'][b]); m['xs'] = f(inp['x_sample'][sl]).reshape(NS, D); m['memp'] = f(inp['mem_prompt'][b])
        m['pt'] = f(inp['page_table'][sl]).reshape(1, SB * NPAGE).astype(np.int32)
        m['s5re'] = f(inp['state_s5_re'][:, sl]); m['s5im'] = f(inp['state_s5_im'][:, sl])
        m['mlC'] = f(inp['state_mlstm_C'][:, sl]); m['mln'] = f(inp['state_mlstm_n'][:, sl]); m['mlm'] = f(inp['state_mlstm_m'][:, sl])
        m['cmk'] = f(inp['cache_mem_k'][:, sl]).reshape(DEPTH, SB, 256, 512); m['cmv'] = f(inp['cache_mem_v'][:, sl]).reshape(DEPTH, SB, 256, 512)
        in_maps.append(m)
    res = run_bass_kernel_spmd(nc, in_maps, core_ids=list(range(NCORE))).results
    P4 = range(4)
    cat_p = lambda key, shp: np.stack([res[c][key] for c in P4], axis=1).reshape(shp)
    cat_s = lambda key, shp: np.concatenate([res[c][key] for c in range(NCORE)], axis=1).reshape(shp)
    yp = np.stack([res[c]['yp'] for c in P4], axis=0)
    ys = np.concatenate([res[c]['ys'].reshape(SB, ST, D) for c in range(NCORE)], axis=0)
    outs = (
        yp, ys,
        cat_p('pfk', (DEPTH, 4, SEQ, 8, 64)), cat_p('pfv', (DEPTH, 4, SEQ, 8, 64)), cat_p('pflf', (DEPTH, 4, SEQ, 8)),
        cat_p('ps5re', (DEPTH, 4, 32, 64)), cat_p('ps5im', (DEPTH, 4, 32, 64)),
        cat_p('pmlC', (DEPTH, 4, 4, 128, 128)), cat_p('pmln', (DEPTH, 4, 4, 128)), cat_p('pmlm', (DEPTH, 4, 4)),
        cat_p('pmemk', (DEPTH, 4, 256, 4, 128)), cat_p('pmemv', (DEPTH, 4, 256, 4, 128)),
        np.concatenate([res[c]['sfk'].reshape(DEPTH, SB, ST, 8, 64) for c in range(NCORE)], axis=1),
        np.concatenate([res[c]['sfv'].reshape(DEPTH, SB, ST, 8, 64) for c in range(NCORE)], axis=1),
        np.concatenate([res[c]['sflf'].reshape(DEPTH, SB, ST, 8) for c in range(NCORE)], axis=1),
        cat_s('ss5re', (DEPTH, 128, 32, 64)), cat_s('ss5im', (DEPTH, 128, 32, 64)),
        cat_s('smlC', (DEPTH, 128, 4, 128, 128)), cat_s('smln', (DEPTH, 128, 4, 128)), cat_s('smlm', (DEPTH, 128, 4)),
    )
    return tuple(np.ascontiguousarray(o, dtype=np.float32) for o in outs)
```

```python
import numpy as np
import concourse.bass as bass
import concourse.mybir as mybir
from concourse.bass_utils import run_bass_kernel_spmd

F32 = mybir.dt.float32
BF16 = mybir.dt.bfloat16
I32 = mybir.dt.int32
AF = mybir.ActivationFunctionType
ALU = mybir.AluOpType

D = 1024
SEQ = 4096
DEPTH = 2
NCORE = 8
SB = 16
ST = 4
NS = SB * ST
NPAGE = 16
PAGE = 128
NPOOL = 2560
D_IN = 7184
EPS = 1e-6
C_S5, C_FQ, C_FK, C_FV, C_FF, C_MQ, C_MK, C_MV, C_MI, C_MF, C_MO, C_G = (
    0, 512, 1024, 1536, 2048, 2056, 2568, 3080, 3592, 3596, 3600, 4112)


class Buf:
    def __init__(self, t):
        self.t = t
        self.w = []
        self.r = {}

    def __getitem__(self, idx):
        return self.t[idx]


class K:
    def __init__(self, nc):
        self.nc = nc
        self.eng = {'pe': nc.tensor, 'act': nc.scalar, 'dve': nc.vector, 'pool': nc.gpsimd, 'sp': nc.sync}
        self.sem = {e: nc.alloc_semaphore('sem_' + e) for e in self.eng}
        self.cnt = {e: 0 for e in self.eng}
        self.seen = {e: {} for e in self.eng}
        self.semobj = dict(self.sem)
        self.ndma = 40
        self.dsem = [nc.alloc_semaphore('dsem%d' % i) for i in range(self.ndma)]
        for i, s in enumerate(self.dsem):
            self.semobj['d%d' % i] = s
        self.dcnt = [0] * self.ndma
        self.dnext = 0
        self.out_waits = []
        self.nbuf = 0

    def sb(self, shape, dt=F32, name=None):
        self.nbuf += 1
        return Buf(self.nc.alloc_sbuf_tensor(name or ('sb%d' % self.nbuf), list(shape), dt))

    def ps(self, shape, dt=F32, name=None):
        self.nbuf += 1
        return Buf(self.nc.alloc_psum_tensor(name or ('ps%d' % self.nbuf), list(shape), dt))

    def _wait(self, e, key, val):
        if self.seen[e].get(key, 0) >= val:
            return
        self.seen[e][key] = val
        self.eng[e].wait_ge(self.semobj[key], val)

    def _deps(self, e, reads, writes):
        need = {}
        for b in reads:
            for (k, v) in b.w:
                need[k] = max(need.get(k, 0), v)
        for b in writes:
            for (k, v) in b.w:
                need[k] = max(need.get(k, 0), v)
            for k, v in b.r.items():
                need[k] = max(need.get(k, 0), v)
        for k, v in need.items():
            if k == e and e == 'pe':
                continue
            self._wait(e, k, v)

    def _mark(self, key, val, reads, writes):
        for b in reads:
            b.r[key] = max(b.r.get(key, 0), val)
        for b in writes:
            b.w = [(key, val)]
            b.r = {}

    def op(self, e, fn, reads=(), writes=()):
        reads = [b for b in reads if b is not None]
        writes = [b for b in writes if b is not None]
        self._deps(e, reads, writes)
        inst = fn(self.eng[e])
        self.cnt[e] += 1
        inst.then_inc(self.sem[e], 1)
        self._mark(e, self.cnt[e], reads, writes)

    def dma(self, e, fn, reads=(), writes=(), is_output=False):
        reads = [b for b in reads if b is not None]
        writes = [b for b in writes if b is not None]
        self._deps(e, reads, writes)
        i = self.dnext
        self.dnext = (self.dnext + 1) % self.ndma
        key = 'd%d' % i
        if self.dcnt[i] > 0:
            self._wait(e, key, 16 * self.dcnt[i])
        inst = fn(self.eng[e])
        self.dcnt[i] += 1
        inst.then_inc(self.dsem[i], 16)
        self._mark(key, 16 * self.dcnt[i], reads, writes)
        if is_output:
            self.out_waits.append((key, 16 * self.dcnt[i]))

    def finish(self):
        for (k, v) in self.out_waits:
            self._wait('sp', k, v)
        for e in ('pe', 'act', 'dve', 'pool'):
            if self.cnt[e] > 0:
                self._wait('sp', e, self.cnt[e])

    def mm(self, out_ap, lhsT, rhs, start, stop, reads, writes):
        self.op('pe', lambda g: g.matmul(out_ap, lhsT, rhs, start=start, stop=stop), reads, writes)

    def tr(self, out_ap, in_ap, ident_ap, reads, writes):
        self.op('pe', lambda g: g.transpose(out_ap, in_ap, ident_ap), reads, writes)

    def act(self, out_ap, in_ap, func, reads, writes, bias=None, scale=None, accum_out=None, e='act'):
        kw = {}
        if bias is not None:
            kw['bias'] = bias
        if scale is not None:
            kw['scale'] = scale
        if accum_out is not None:
            kw['accum_out'] = accum_out
        self.op('act', lambda g: g.activation(out=out_ap, in_=in_ap, func=func, **kw), reads, writes)

    def tt(self, out_ap, a, b, op, reads, writes, e='dve'):
        self.op(e, lambda g: g.tensor_tensor(out=out_ap, in0=a, in1=b, op=op), reads, writes)

    def ts(self, out_ap, a, s1, s2, op0, op1, reads, writes, e='dve'):
        if op1 is None:
            self.op(e, lambda g: g.tensor_scalar(out=out_ap, in0=a, scalar1=s1, scalar2=None, op0=op0), reads, writes)
        else:
            self.op(e, lambda g: g.tensor_scalar(out=out_ap, in0=a, scalar1=s1, scalar2=s2, op0=op0, op1=op1), reads, writes)

    def stt(self, out_ap, a, s, b, op0, op1, reads, writes):
        self.op('dve', lambda g: g.scalar_tensor_tensor(out=out_ap, in0=a, scalar=s, in1=b, op0=op0, op1=op1), reads, writes)

    def cp(self, out_ap, in_ap, reads, writes, e='dve'):
        if e == 'act':
            self.op('act', lambda g: g.copy(out=out_ap, in_=in_ap), reads, writes)
        else:
            self.op(e, lambda g: g.tensor_copy(out=out_ap, in_=in_ap), reads, writes)

    def memset(self, buf, ap, val, e='pool'):
        self.op(e, lambda g: g.memset(ap, val), (), (buf,))


TT = 128
NT = SEQ // TT


def build(mode='full'):
    import math
    nc = bass.Bass("TRN2", target_bir_lowering=False)
    k = K(nc)
    full = (mode == 'full')

    def din(name, shape, dt=F32):
        return nc.dram_tensor(name, list(shape), dt, kind="ExternalInput").ap()

    def dout(name, shape, dt=F32):
        return nc.dram_tensor(name, list(shape), dt, kind="ExternalOutput").ap()

    I = {}
    I['xp'] = din('xp', [SEQ, D]); I['memp'] = din('memp', [256, D])
    if full:
        I['xs'] = din('xs', [NS, D])
        I['ck'] = din('ck', [DEPTH, NPOOL * PAGE, 512]); I['cv'] = din('cv', [DEPTH, NPOOL * PAGE, 512])
        I['clf'] = din('clf', [DEPTH, NPOOL * PAGE, 8]); I['pt'] = din('pt', [1, SB * NPAGE], I32)
        I['s5re'] = din('s5re', [DEPTH, SB, 32, 64]); I['s5im'] = din('s5im', [DEPTH, SB, 32, 64])
        I['mlC'] = din('mlC', [DEPTH, SB, 4, 128, 128]); I['mln'] = din('mln', [DEPTH, SB, 4, 128])
        I['mlm'] = din('mlm', [DEPTH, SB, 4]); I['cmk'] = din('cmk', [DEPTH, SB, 256, 512]); I['cmv'] = din('cmv', [DEPTH, SB, 256, 512])
    wshapes = dict(g_mix=[DEPTH, D], w_in=[DEPTH, D, D_IN], s5_a_re=[DEPTH, 32, 64], s5_a_im=[DEPTH, 32, 64],
                   s5_log_step=[DEPTH, 32], s5_b_re=[DEPTH, 32, 64, 16], s5_b_im=[DEPTH, 32, 64, 16],
                   s5_c_re=[DEPTH, 32, 16, 64], s5_c_im=[DEPTH, 32, 16, 64], s5_d=[DEPTH, 512],
                   s5_w_glu=[DEPTH, 512, 512], s5_b_glu=[DEPTH, 512], fox_gq=[DEPTH, 64], fox_gk=[DEPTH, 64],
                   fox_bf=[DEPTH, 8], ml_bi=[DEPTH, 4], ml_bf=[DEPTH, 4], ml_gn=[DEPTH, 128],
                   w_br_s5=[DEPTH, 512, D], w_br_fox=[DEPTH, 512, D], w_br_ml=[DEPTH, 512, D], w_out=[DEPTH, D, D],
                   g_cross=[DEPTH, D], w_cq=[DEPTH, D, 512], cross_gq=[DEPTH, 128], g_mem=[DEPTH, D],
                   w_mk=[DEPTH, D, 512], w_mv=[DEPTH, D, 512], cross_gk=[DEPTH, 128], w_co=[DEPTH, 512, D],
                   g_mlp=[DEPTH, D], w_up=[DEPTH, D, 4096], w_down=[DEPTH, 4096, D])
    W = {n: din(n, s) for n, s in wshapes.items()}
    O = {}
    O['yp'] = dout('yp', [SEQ, D])
    O['pfk'] = dout('pfk', [DEPTH, SEQ, 512]); O['pfv'] = dout('pfv', [DEPTH, SEQ, 512]); O['pflf'] = dout('pflf', [DEPTH, SEQ, 8])
    O['ps5re'] = dout('ps5re', [DEPTH, 32, 64]); O['ps5im'] = dout('ps5im', [DEPTH, 32, 64])
    O['pmlC'] = dout('pmlC', [DEPTH, 4, 128, 128]); O['pmln'] = dout('pmln', [DEPTH, 4, 128]); O['pmlm'] = dout('pmlm', [DEPTH, 4])
    O['pmemk'] = dout('pmemk', [DEPTH, 256, 512]); O['pmemv'] = dout('pmemv', [DEPTH, 256, 512])
    if full:
        O['ys'] = dout('ys', [NS, D])
        O['sfk'] = dout('sfk', [DEPTH, NS, 512]); O['sfv'] = dout('sfv', [DEPTH, NS, 512]); O['sflf'] = dout('sflf', [DEPTH, NS, 8])
        O['ss5re'] = dout('ss5re', [DEPTH, SB, 32, 64]); O['ss5im'] = dout('ss5im', [DEPTH, SB, 32, 64])
        O['smlC'] = dout('smlC', [DEPTH, SB, 4, 128, 128]); O['smln'] = dout('smln', [DEPTH, SB, 4, 128]); O['smlm'] = dout('smlm', [DEPTH, SB, 4])
    xscr = nc.dram_tensor('xscr', [SEQ, D], F32, kind="Internal").ap()
    ktscr = nc.dram_tensor('ktscr', [8, 96, SEQ], BF16, kind="Internal").ap()
    vscr = nc.dram_tensor('vscr', [8, 128, SEQ // 128, 65], BF16, kind="Internal").ap()
    XSCR = Buf(None); KSCR = Buf(None); VSCR = Buf(None)

    identf = k.sb([128, 128], F32, 'identf'); k.memset(identf, identf[:], 1.0)
    k.op('pool', lambda g: g.affine_select(out=identf[:], in_=identf[:], pattern=[[-1, 128]], compare_op=ALU.is_equal,
                                           fill=0.0, base=0, channel_multiplier=1), (identf,), (identf,))
    identb = k.sb([128, 128], BF16, 'identb'); k.cp(identb[:], identf[:], (identf,), (identb,))
    trif = k.sb([128, 128], F32, 'trif'); k.memset(trif, trif[:], 1.0)
    k.op('pool', lambda g: g.affine_select(out=trif[:], in_=trif[:], pattern=[[1, 128]], compare_op=ALU.is_ge,
                                           fill=0.0, base=0, channel_multiplier=-1), (trif,), (trif,))
    trib = k.sb([128, 128], BF16, 'trib'); k.cp(trib[:], trif[:], (trif,), (trib,))
    onesf = k.sb([128, 128], F32, 'onesf'); k.memset(onesf, onesf[:], 1.0)
    onesb = k.sb([128, 128], BF16, 'onesb'); k.memset(onesb, onesb[:], 1.0)
    triR = k.sb([128, 128], F32, 'triR')
    k.tt(triR[:], onesf[:], trif[:], ALU.subtract, (onesf, trif), (triR,))

    psb = [k.ps([128, 512], F32, 'psb%d' % i) for i in range(3)]
    pst = [k.ps([128, 1024], BF16, 'pst%d' % i) for i in range(2)]
    ps_acc = [k.ps([128, 512], F32, 'psacc%d' % i) for i in range(2)]
    psden = k.ps([128, 512], F32, 'psden')
    st = {'ps': 0, 'pt': 0, 'w': 0}

    def nps():
        st['ps'] = (st['ps'] + 1) % len(psb)
        return psb[st['ps']]

    def npt():
        st['pt'] = (st['pt'] + 1) % len(pst)
        return pst[st['pt']]

    wbufs = [k.sb([128, 8, 512], BF16, 'wbuf%d' % i) for i in range(3)]

    def nwb():
        st['w'] = (st['w'] + 1) % len(wbufs)
        return wbufs[st['w']]

    wcache = {}

    def _wscratch(key, p, n):
        if key not in wcache:
            scr = nc.dram_tensor('ws%d' % len(wcache), [p, n], BF16, kind="Internal").ap()
            wcache[key] = [scr, Buf(None), False]
        return wcache[key]

    def wload(wap, r0, nr, c0, ncol, p=128):
        wb = nwb()
        kc = nr // p
        ent = _wscratch((wap.tensor.name, wap.offset, r0, nr, c0, ncol, p), p, kc * ncol)
        scr3 = ent[0].rearrange("p (k n) -> p k n", n=ncol)
        if not ent[2]:
            src = wap[r0:r0 + nr, c0:c0 + ncol].rearrange("(kc p) n -> p kc n", p=p)
            k.dma('pool', lambda g: g.dma_start(out=wb[0:p, 0:kc, 0:ncol], in_=src), (), (wb,))
            k.dma('sp', lambda g: g.dma_start(out=scr3, in_=wb[0:p, 0:kc, 0:ncol]), (wb,), (ent[1],))
            ent[2] = True
        else:
            k.dma('pool', lambda g: g.dma_start(out=wb[0:p, 0:kc, 0:ncol], in_=scr3), (ent[1],), (wb,))
        return wb

    def wload96(wap, c0, ncol):
        wb = nwb()
        ent = _wscratch((wap.tensor.name, wap.offset, c0, ncol, '96'), 96, 6 * ncol)
        scr3 = ent[0].rearrange("p (k n) -> p k n", n=ncol)
        if not ent[2]:
            k.dma('pool', lambda g: g.dma_start(out=wb[0:96, 0:5, 0:ncol], in_=wap[0:480, c0:c0 + ncol].rearrange("(c p) n -> p c n", p=96)), (), (wb,))
            k.dma('pool', lambda g: g.dma_start(out=wb[0:32, 5, 0:ncol], in_=wap[480:512, c0:c0 + ncol]), (), (wb,))
            k.dma('sp', lambda g: g.dma_start(out=scr3, in_=wb[0:96, 0:6, 0:ncol]), (wb,), (ent[1],))
            ent[2] = True
        else:
            k.dma('pool', lambda g: g.dma_start(out=wb[0:96, 0:6, 0:ncol], in_=scr3), (ent[1],), (wb,))
        return wb

    def bcast_load(dst, ap_row, n):
        k.dma('sp', lambda g: g.dma_start(out=dst[:, 0:n], in_=ap_row.partition_broadcast(128)), (), (dst,))

    def transpose_to_T(src, P, nchunks, dstT, t0, srcsl):
        for c0 in range(0, nchunks, 8):
            nch = min(8, nchunks - c0)
            pt_ = npt()
            for c in range(nch):
                k.tr(pt_[:, c * 128:c * 128 + P], srcsl(c0 + c), identb[0:P, 0:P], (src, identb), (pt_,))
            k.cp(dstT[:, c0:c0 + nch, t0:t0 + P],
                 pt_[:, 0:nch * 128].rearrange("p (c t) -> p c t", t=128)[:, :, 0:P], (pt_,), (dstT,), e='act')

    def transpose_heads(src, P, dst, t0, srcsl, nrows=64, dst_ap=None, prow=0):
        pt_ = npt()
        for c in range(4):
            k.tr(pt_[:, c * 128:c * 128 + P], srcsl(c), identb[0:P, 0:P], (src, identb), (pt_,))
        v = pt_[:, 0:512].rearrange("p (c t) -> p c t", t=128)
        d_ = dst_ap if dst_ap is not None else dst.t
        k.cp(d_[prow:prow + nrows, 0:8:2, t0:t0 + P], v[0:nrows, :, 0:P], (pt_,), (dst,), e='act')
        k.cp(d_[prow:prow + nrows, 1:8:2, t0:t0 + P], v[64:64 + nrows, :, 0:P], (pt_,), (dst,), e='dve')

    junkb = k.sb([128, D], BF16, 'junkb'); ss1 = k.sb([128, 1], F32, 'ss1')

    def rmsnorm_T(x, P, nsub, gb, hT, tmpb):
        for s in range(nsub):
            k.act(junkb[0:P, :], x[0:P, s, :], AF.Square, (x,), (junkb, ss1), accum_out=ss1[0:P, :])
            k.ts(ss1[0:P, :], ss1[0:P, :], 1.0 / D, EPS, ALU.mult, ALU.add, (ss1,), (ss1,))
            k.act(ss1[0:P, :], ss1[0:P, :], AF.Sqrt, (ss1,), (ss1,))
            k.op('dve', lambda g: g.reciprocal(out=ss1[0:P, :], in_=ss1[0:P, :]), (ss1,), (ss1,))
            k.stt(tmpb[0:P, :], x[0:P, s, :], ss1[0:P, :], gb[0:P, 0:D], ALU.mult, ALU.mult, (x, ss1, gb), (tmpb,))
            transpose_to_T(tmpb, P, 8, hT, s * 128, lambda c: tmpb[0:P, c * 128:(c + 1) * 128])

    def proj_tok(hT, kchunks, P, nsub, wap, r0, c0, ncols, consumer):
        for cb in range(0, ncols, 512):
            nc_ = min(512, ncols - cb)
            wb = wload(wap, r0, kchunks * 128, c0 + cb, nc_)
            for s in range(nsub):
                ps_ = nps()
                for kc in range(kchunks):
                    k.mm(ps_[0:P, 0:nc_], hT[:, kc, s * 128:s * 128 + P], wb[:, kc, 0:nc_], kc == 0, kc == kchunks - 1,
                         (hT, wb), (ps_,))
                consumer(ps_, s, cb, nc_)

    def proj_feat(hT, kchunks, ntok, wap, r0, c0, ncols, consumer, chunk=128):
        for cb in range(0, ncols, 512):
            nc_ = min(512, ncols - cb)
            wb = wload(wap, r0, kchunks * 128, c0 + cb, nc_)
            for j in range(0, nc_, chunk):
                m = min(chunk, nc_ - j)
                ps_ = nps()
                for kc in range(kchunks):
                    k.mm(ps_[0:m, 0:ntok], wb[:, kc, j:j + m], hT[:, kc, 0:ntok], kc == 0, kc == kchunks - 1, (hT, wb), (ps_,))
                consumer(ps_, (cb + j) // chunk, m)

    NSUBX = 1
    xres = k.sb([128, NSUBX, D], F32, 'xres')
    gb = k.sb([128, D], F32, 'gb')
    tmpb = k.sb([128, D], BF16, 'tmpb')
    hT = k.sb([128, 8, 128], BF16, 'hT')
    arena = k.sb([128, 32, TT], BF16, 'arena')
    arena_f = arena.t.bitcast(F32).reshape([128, 16 * TT])
    mrgf = k.sb([128, D], F32, 'mrgf')
    yT = {b: k.sb([128, 4, TT], BF16, 'yT_' + b) for b in ('s5', 'fox', 'ml')}
    memK = k.sb([128, 4, 256], BF16, 'memKT')
    memV = k.sb([128, 2, 512], BF16, 'memV')
    rowt = k.sb([128, 512], F32, 'rowt')
    kout = k.sb([128, 512], F32, 'kout')
    sm8 = k.sb([128, 8], F32, 'sm8')
    pT = k.sb([128, 2, TT], BF16, 'pT')
    den4 = k.sb([4, TT], F32, 'den4')
    ones4 = k.sb([4, 128], F32, 'ones4'); k.memset(ones4, ones4[:], 1.0)
    selh = []
    for h in range(4):
        s_ = k.sb([4, 128], F32, 'selh%d' % h)
        k.op('pool', lambda g, s_=s_, h=h: g.affine_select(out=s_[:], in_=ones4[:], pattern=[[0, 128]], compare_op=ALU.is_equal,
                                                       fill=0.0, base=-h, channel_multiplier=1), (ones4,), (s_,))
        selh.append(s_)
    onesel = k.sb([128, 4, 4], BF16, 'onesel'); k.memset(onesel, onesel[:], 0.0)
    for h in range(4):
        k.memset(onesel, onesel[:, h, h:h + 1], 1.0)
    gvb = {n_: k.sb([128, 128], F32, 'gvb_' + n_) for n_ in ('cgq', 'cgk', 'fgq', 'fgk')}

    def head_rms(ps_, P, nh, hd, gvec_b, out_buf, out_ap, scale):
        n = nh * hd
        k.act(rowt[0:P, 0:n], ps_[0:P, 0:n], AF.Square, (ps_,), (rowt,))
        k.op('dve', lambda g: g.tensor_reduce(out=sm8[0:P, 0:nh], in_=rowt[0:P, 0:n].rearrange("p (h d) -> p h d", d=hd),
                                              axis=mybir.AxisListType.X, op=ALU.add), (rowt,), (sm8,))
        k.ts(sm8[0:P, 0:nh], sm8[0:P, 0:nh], 1.0 / hd, EPS, ALU.mult, ALU.add, (sm8,), (sm8,))
        k.act(sm8[0:P, 0:nh], sm8[0:P, 0:nh], AF.Sqrt, (sm8,), (sm8,))
        k.op('dve', lambda g: g.reciprocal(out=sm8[0:P, 0:nh], in_=sm8[0:P, 0:nh]), (sm8,), (sm8,))
        k.tt(rowt[0:P, 0:n].rearrange("p (h d) -> p h d", d=hd), ps_[0:P, 0:n].rearrange("p (h d) -> p h d", d=hd),
             sm8[0:P, 0:nh].unsqueeze(2).to_broadcast([P, nh, hd]), ALU.mult, (ps_, sm8), (rowt,))
        k.stt(out_ap, rowt[0:P, 0:n].rearrange("p (h d) -> p h d", d=hd), float(scale),
              gvec_b[0:P, 0:hd].unsqueeze(1).to_broadcast([P, nh, hd]), ALU.mult, ALU.mult, (rowt, gvec_b), (out_buf,))

    def memory_kv(l):
        bcast_load(gb, W['g_mem'][l], D)
        bcast_load(gvb['cgk'], W['cross_gk'][l], 128)
        for s in range(2):
            k.dma('sp', lambda g: g.dma_start(out=xres[:, 0, :], in_=I['memp'][s * 128:(s + 1) * 128, :]), (), (xres,))
            rmsnorm_T(xres, 128, 1, gb, hT, tmpb)

            def cons_k(ps_, s_, cb, nc_, s=s):
                head_rms(ps_, 128, 4, 128, gvb['cgk'], kout, kout[:, 0:512].rearrange("p (h d) -> p h d", d=128), 1.0)
                k.dma('sp', lambda g: g.dma_start(out=O['pmemk'][l, s * 128:(s + 1) * 128, :], in_=kout[:, 0:512]), (kout,), (), True)
                k.cp(tmpb[:, 0:512], kout[:, 0:512], (kout,), (tmpb,))
                transpose_to_T(tmpb, 128, 4, memK, s * 128, lambda c: tmpb[:, c * 128:(c + 1) * 128])
            proj_tok(hT, 8, 128, 1, W['w_mk'][l], 0, 0, 512, cons_k)

            def cons_v(ps_, s_, cb, nc_, s=s):
                k.cp(rowt[:, 0:512], ps_[:, 0:512], (ps_,), (rowt,), e='act')
                k.dma('sp', lambda g: g.dma_start(out=O['pmemv'][l, s * 128:(s + 1) * 128, :], in_=rowt[:, 0:512]), (rowt,), (), True)
                k.cp(memV[:, s, :], rowt[:, 0:512], (rowt,), (memV,))
            proj_tok(hT, 8, 128, 1, W['w_mv'][l], 0, 0, 512, cons_v)

    def cross_q(l, P, ntok):
        bcast_load(gb, W['g_cross'][l], D)
        bcast_load(gvb['cgq'], W['cross_gq'][l], 128)
        rmsnorm_T(xres, P, 1, gb, hT, tmpb)
        qT = yT['s5']

        def cons_q(ps_, s, cb, nc_):
            head_rms(ps_, P, 4, 128, gvb['cgq'], tmpb, tmpb[0:P, 0:512].rearrange("p (h d) -> p h d", d=128), 128 ** -0.5)
            transpose_to_T(tmpb, P, 4, qT, 0, lambda c: tmpb[0:P, c * 128:(c + 1) * 128])
        proj_tok(hT, 8, P, 1, W['w_cq'][l], 0, 0, 512, cons_q)
        return qT

    def cross_finish(l, P, ntok):
        ocT = yT['fox']
        k.op('dve', lambda g: g.reciprocal(out=den4[:, 0:ntok], in_=psden[0:4, 0:ntok]), (psden,), (den4,))
        for h in range(4):
            psr = nps()
            k.mm(psr[:, 0:ntok], selh[h][:], den4[:, 0:ntok], True, True, (selh[h], den4), (psr,))
            k.tt(ocT[:, h, 0:ntok], arena_f[:, h * TT:h * TT + ntok], psr[:, 0:ntok], ALU.mult, (arena, psr), (ocT,))

        def cons_o(ps_, s, cb, nc_):
            k.tt(xres[0:P, 0, cb:cb + nc_], xres[0:P, 0, cb:cb + nc_], ps_[0:P, 0:nc_], ALU.add, (xres, ps_), (xres,))
        proj_tok(ocT, 4, P, 1, W['w_co'][l], 0, 0, D, cons_o)

    def cross_attn_prompt(l):
        P, ntok = 128, TT
        qT = cross_q(l, P, ntok)
        for h in range(4):
            for mc in range(2):
                ps_ = nps()
                k.mm(ps_[:, 0:ntok], memK[:, h, mc * 128:(mc + 1) * 128], qT[:, h, 0:ntok], True, True, (memK, qT), (ps_,))
                k.act(pT[:, mc, 0:ntok], ps_[:, 0:ntok], AF.Exp, (ps_,), (pT,), bias=-8.0)
            pso = nps()
            for mc in range(2):
                k.mm(pso[:, 0:ntok], memV[:, mc, h * 128:(h + 1) * 128], pT[:, mc, 0:ntok], mc == 0, mc == 1, (memV, pT), (pso,))
            k.cp(arena_f[:, h * TT:h * TT + ntok], pso[:, 0:ntok], (pso,), (arena,), e='act')
            for mc in range(2):
                k.mm(psden[0:4, 0:ntok], onesel[:, h, :], pT[:, mc, 0:ntok], h == 0 and mc == 0, h == 3 and mc == 1, (onesel, pT), (psden,))
        cross_finish(l, P, ntok)

    def mlp(l, P, ntok):
        bcast_load(gb, W['g_mlp'][l], D)
        rmsnorm_T(xres, P, 1, gb, hT, tmpb)

        def cons_up(ps_, j, m):
            k.act(rowt[:, 0:ntok], ps_[:, 0:ntok], AF.Relu, (ps_,), (rowt,))
            k.tt(arena[:, j, 0:ntok], rowt[:, 0:ntok], rowt[:, 0:ntok], ALU.mult, (rowt,), (arena,))
        proj_feat(hT, 8, ntok, W['w_up'][l], 0, 0, 4096, cons_up)
        for cb in range(2):
            ps_ = ps_acc[cb]
            for kg in range(4):
                wb = wload(W['w_down'][l], kg * 1024, 1024, cb * 512, 512)
                for kc in range(8):
                    k.mm(ps_[0:P, :], arena[:, kg * 8 + kc, 0:P], wb[:, kc, :], kg == 0 and kc == 0, kg == 3 and kc == 7, (arena, wb), (ps_,))
            k.tt(xres[0:P, 0, cb * 512:(cb + 1) * 512], xres[0:P, 0, cb * 512:(cb + 1) * 512], ps_[0:P, :], ALU.add, (xres, ps_), (xres,))

    S5T = Buf(None)
    def raw(name, shape, dt=F32):
        return nc.alloc_sbuf_tensor('s5_' + name, list(shape), dt)
    are = raw('are', [128, 16]); aim = raw('aim', [128, 16]); lsb = raw('lsb', [128, 16]); dtt = raw('dtt', [128, 16])
    mag = raw('mag', [128, 16]); cs_c = raw('cs_c', [128, 16]); cs_s = raw('cs_s', [128, 16])
    u1 = raw('u1', [128, 16]); u2 = raw('u2', [128, 16]); u3 = raw('u3', [128, 16]); u4 = raw('u4', [128, 16])
    Bre = raw('Bre', [128, 16, 16]); Bim = raw('Bim', [128, 16, 16]); Cre = raw('Cre', [128, 16, 16]); Cim = raw('Cim', [128, 16, 16])
    Bbr = raw('Bbr', [128, 16, 16]); Bbi = raw('Bbi', [128, 16, 16])
    pwr = raw('pwr', [128, 9, 16]); pwi = raw('pwi', [128, 9, 16])
    V1 = raw('V1', [128, 16, 16]); V2 = raw('V2', [128, 16, 16]); V3 = raw('V3', [128, 16, 16]); V4 = raw('V4', [128, 16, 16])
    QBD = raw('QBD', [128, 2, 16, 9, 32], BF16); BBD = raw('BBD', [128, 2, 16, 32], BF16)
    BDt = raw('BDt', [128, 96], BF16)
    W1 = raw('W1', [96, 8, 2, 6, 128], BF16); Kt = raw('Kt', [96, 6, 8, 32], BF16)
    NJ = TT // 8
    MUFr = raw('MUFr', [128, 16, NJ]); MUFi = raw('MUFi', [128, 16, NJ]); MUIr = raw('MUIr', [128, 16, NJ]); MUIi = raw('MUIi', [128, 16, NJ])
    M1 = raw('M1', [128, 16, NJ]); M2 = raw('M2', [128, 16, NJ]); M3 = raw('M3', [128, 16, NJ]); M4 = raw('M4', [128, 16, NJ])
    rmask = raw('rmask', [128, 16, NJ]); dcol = raw('dcol', [96, 6]); bgcol = raw('bgcol', [96, 6])
    Xr = raw('Xr', [128, 16, NJ + 1]); Xi = raw('Xi', [128, 16, NJ + 1])
    Xbr = raw('Xbr', [128, 16, NJ + 1], BF16); Xbi = raw('Xbi', [128, 16, NJ + 1], BF16)
    uT = k.sb([96, 6, TT], BF16, 'uT'); zt = k.sb([96, TT], F32, 'zt'); ygT = k.sb([96, 6, TT], BF16, 'ygT')
    ysb = k.sb([16, 8, 3, 32], BF16, 'ysb')
    ys5T = k.sb([96, 6, TT], BF16, 'ys5T')
    for t_ in (QBD, BBD, BDt):
        k.op('pool', lambda g, t_=t_: g.memset(t_[:], 0.0), (), (S5T,))
    k.op('pool', lambda g: g.memset(rmask[:], 1.0), (), (S5T,))
    k.op('pool', lambda g: g.memset(rmask[:, :, 0:1], 0.0), (), (S5T,))

    def sop(fn, e='dve'):
        k.op(e, fn, (S5T,), (S5T,))

    def s_tt(o, a, b_, op, e='dve'):
        sop(lambda g: g.tensor_tensor(out=o, in0=a, in1=b_, op=op), e)

    def s_cmul(o_re, o_im, a_re, a_im, b_re, b_im, t1, t2):
        s_tt(t1, a_re, b_re, ALU.mult); s_tt(t2, a_im, b_im, ALU.mult)
        s_tt(t1, t1, t2, ALU.subtract)
        s_tt(t2, a_re, b_im, ALU.mult); s_tt(o_im, a_im, b_re, ALU.mult)
        s_tt(o_im, o_im, t2, ALU.add)
        sop(lambda g: g.tensor_copy(out=o_re, in_=t1))

    def s5_setup(l):
        def ld(dst, src, eng='sp'):
            k.dma(eng, lambda g: g.dma_start(out=dst, in_=src, allow_slow_non_contiguous=True), (S5T,), (S5T,))
        ld(are[:], W['s5_a_re'][l].rearrange("(q r) p -> (r p) q", r=2))
        ld(aim[:], W['s5_a_im'][l].rearrange("(q r) p -> (r p) q", r=2))
        lsv = W['s5_log_step'][l].rearrange("(q r) -> r q", r=2)
        for r in range(2):
            ld(lsb[r * 64:(r + 1) * 64, :], lsv[r].partition_broadcast(64))
        ld(Bre[:], W['s5_b_re'][l].rearrange("(q r) p c -> (r p) q c", r=2))
        ld(Bim[:], W['s5_b_im'][l].rearrange("(q r) p c -> (r p) q c", r=2))
        for r in range(2):
            for q in range(16):
                ld(Cre[r * 64:(r + 1) * 64, q, :], W['s5_c_re'][l][2 * q + r].rearrange("c p -> p c"))
                ld(Cim[r * 64:(r + 1) * 64, q, :], W['s5_c_im'][l][2 * q + r].rearrange("c p -> p c"))
        for (dst, src) in ((dcol, W['s5_d'][l]), (bgcol, W['s5_b_glu'][l])):
            ld(dst[:, 0:5], src[0:480].rearrange("(c p) -> p c", p=96))
            ld(dst[0:32, 5:6], src[480:512].rearrange("(c p) -> p c", p=32))
        sop(lambda g: g.activation(out=dtt[:], in_=lsb[:], func=AF.Exp), 'act')
        s_tt(u1[:], are[:], dtt[:], ALU.mult)
        sop(lambda g: g.activation(out=mag[:], in_=u1[:], func=AF.Exp), 'act')
        s_tt(u1[:], aim[:], dtt[:], ALU.mult)
        sop(lambda g: g.activation(out=cs_s[:], in_=u1[:], func=AF.Sin, scale=1.0 / 16), 'act')
        sop(lambda g: g.tensor_scalar(out=u2[:], in0=u1[:], scalar1=1.0 / 16, scalar2=math.pi / 2, op0=ALU.mult, op1=ALU.add))
        sop(lambda g: g.activation(out=cs_c[:], in_=u2[:], func=AF.Sin), 'act')
        for _ in range(4):
            s_tt(u2[:], cs_c[:], cs_c[:], ALU.mult); s_tt(u3[:], cs_s[:], cs_s[:], ALU.mult)
            s_tt(u4[:], cs_c[:], cs_s[:], ALU.mult)
            s_tt(cs_c[:], u2[:], u3[:], ALU.subtract)
            sop(lambda g: g.tensor_scalar(out=cs_s[:], in0=u4[:], scalar1=2.0, scalar2=None, op0=ALU.mult))
        sop(lambda g: g.memset(pwr[:, 0, :], 1.0), 'pool'); sop(lambda g: g.memset(pwi[:, 0, :], 0.0), 'pool')
        s_tt(pwr[:, 1, :], mag[:], cs_c[:], ALU.mult); s_tt(pwi[:, 1, :], mag[:], cs_s[:], ALU.mult)
        for kk in range(2, 9):
            s_cmul(pwr[:, kk, :], pwi[:, kk, :], pwr[:, kk - 1, :], pwi[:, kk - 1, :], pwr[:, 1, :], pwi[:, 1, :], u1[:], u2[:])
        sop(lambda g: g.tensor_scalar(out=u1[:], in0=pwr[:, 1, :], scalar1=-1.0, scalar2=None, op0=ALU.add))
        s_tt(u2[:], are[:], are[:], ALU.mult); s_tt(u3[:], aim[:], aim[:], ALU.mult); s_tt(u2[:], u2[:], u3[:], ALU.add)
        sop(lambda g: g.reciprocal(out=u2[:], in_=u2[:]))
        s_tt(u3[:], u1[:], are[:], ALU.mult); s_tt(u4[:], pwi[:, 1, :], aim[:], ALU.mult); s_tt(u3[:], u3[:], u4[:], ALU.add)
        s_tt(u3[:], u3[:], u2[:], ALU.mult)
        s_tt(u4[:], pwi[:, 1, :], are[:], ALU.mult); s_tt(u1[:], u1[:], aim[:], ALU.mult); s_tt(u4[:], u4[:], u1[:], ALU.subtract)
        s_tt(u4[:], u4[:], u2[:], ALU.mult)
        bc = lambda a: a.unsqueeze(2).to_broadcast([128, 16, 16])
        s_cmul(Bbr[:], Bbi[:], bc(u3[:]), bc(u4[:]), Bre[:], Bim[:], V1[:], V2[:])
        for r in range(2):
            sop(lambda g, r=r: g.tensor_copy(out=BBD[r * 64:(r + 1) * 64, 0, :, r * 16:(r + 1) * 16], in_=Bbr[r * 64:(r + 1) * 64, :, :]))
            sop(lambda g, r=r: g.tensor_copy(out=BBD[r * 64:(r + 1) * 64, 1, :, r * 16:(r + 1) * 16], in_=Bbi[r * 64:(r + 1) * 64, :, :]))
        for kk in range(9):
            s_cmul(V3[:], V4[:], Cre[:], Cim[:], bc(pwr[:, kk, :]), bc(pwi[:, kk, :]), V1[:], V2[:])
            for r in range(2):
                sop(lambda g, r=r, kk=kk: g.tensor_copy(out=QBD[r * 64:(r + 1) * 64, 0, :, kk, r * 16:(r + 1) * 16], in_=V3[r * 64:(r + 1) * 64, :, :]))
                sop(lambda g, r=r, kk=kk: g.tensor_scalar(out=QBD[r * 64:(r + 1) * 64, 1, :, kk, r * 16:(r + 1) * 16], in0=V4[r * 64:(r + 1) * 64, :, :],
                                                          scalar1=-1.0, scalar2=None, op0=ALU.mult))
        for kk in range(8):
            s_cmul(V3[:], V4[:], Bbr[:], Bbi[:], bc(pwr[:, kk, :]), bc(pwi[:, kk, :]), V1[:], V2[:])
            for ri, Vx in enumerate((V3, V4)):
                for tq in range(6):
                    npair = 3 if tq < 5 else 1
                    bdv = BDt[:, :].rearrange("p (a r c) -> p a r c", r=2, c=16)
                    for r in range(2):
                        sop(lambda g, r=r, tq=tq, npair=npair, Vx=Vx: g.tensor_copy(out=bdv[r * 64:(r + 1) * 64, 0:npair, r, :],
                                                                               in_=Vx[r * 64:(r + 1) * 64, tq * 3:tq * 3 + npair, :]))
                    pt_ = npt()
                    k.op('pe', lambda g, pt_=pt_, npair=npair: g.transpose(pt_[0:32 * npair, 0:128], BDt[:, 0:32 * npair], identb[:, :]), (S5T, identb), (pt_,))
                    k.op('act', lambda g, pt_=pt_, npair=npair, kk=kk, ri=ri, tq=tq: g.copy(out=W1[0:32 * npair, kk, ri, tq, :], in_=pt_[0:32 * npair, 0:128]), (pt_, S5T), (S5T,))
        for q in range(16):
            po, tq = 32 * (q % 3), q // 3
            ps_ = nps()
            for tau in range(8):
                k.op('pe', lambda g, tau=tau, q=q, ps_=ps_, po=po: g.matmul(ps_[po:po + 32, tau * 32:(tau + 1) * 32], BBD[:, 0, q, :], QBD[:, 0, q, tau, :], start=True, stop=False), (S5T,), (ps_,))
                k.op('pe', lambda g, tau=tau, q=q, ps_=ps_, po=po: g.matmul(ps_[po:po + 32, tau * 32:(tau + 1) * 32], BBD[:, 1, q, :], QBD[:, 1, q, tau, :], start=False, stop=True), (S5T,), (ps_,))
            k.op('act', lambda g, ps_=ps_, po=po, tq=tq: g.copy(out=Kt[po:po + 32, tq, :, :], in_=ps_[po:po + 32, 0:256].rearrange("p (a b) -> p a b", b=32)), (ps_, S5T), (S5T,))
        sop(lambda g: g.tensor_copy(out=MUFr[:, :, 0], in_=pwr[:, 8, :])); sop(lambda g: g.tensor_copy(out=MUFi[:, :, 0], in_=pwi[:, 8, :]))
        for j in range(1, NJ):
            s_cmul(MUFr[:, :, j], MUFi[:, :, j], MUFr[:, :, j - 1], MUFi[:, :, j - 1], pwr[:, 8, :], pwi[:, 8, :], u1[:], u2[:])
        s_tt(M1[:], MUFr[:], MUFr[:], ALU.mult); s_tt(M2[:], MUFi[:], MUFi[:], ALU.mult); s_tt(M1[:], M1[:], M2[:], ALU.add)
        sop(lambda g: g.reciprocal(out=M1[:], in_=M1[:]))
        s_tt(MUIr[:], MUFr[:], M1[:], ALU.mult); s_tt(MUIi[:], MUFi[:], M1[:], ALU.mult)
        sop(lambda g: g.tensor_scalar(out=MUIi[:], in0=MUIi[:], scalar1=-1.0, scalar2=None, op0=ALU.mult))
        sop(lambda g: g.memset(Xr[:], 0.0), 'pool'); sop(lambda g: g.memset(Xi[:], 0.0), 'pool')

    def s5_tile_prompt(l, ntok):
        L, nj = 8, ntok // 8
        sop(lambda g: g.tensor_copy(out=Xr[:, :, 0], in_=Xr[:, :, nj])); sop(lambda g: g.tensor_copy(out=Xi[:, :, 0], in_=Xi[:, :, nj]))
        psS = [nps(), nps()]
        for q in range(16):
            po, tq = 32 * (q % 3), q // 3
            for ri in range(2):
                for sp in range(L):
                    k.op('pe', lambda g, q=q, ri=ri, sp=sp, po=po, tq=tq: g.matmul(psS[ri][:, q * nj:(q + 1) * nj], W1[po:po + 32, L - 1 - sp, ri, tq, :],
                                                                                uT[po:po + 32, tq, sp:ntok:L], start=(sp == 0), stop=(sp == L - 1)), (S5T, uT), (psS[ri],))
        Sr = psS[0][:, 0:16 * nj].rearrange("p (q j) -> p q j", j=nj); Si = psS[1][:, 0:16 * nj].rearrange("p (q j) -> p q j", j=nj)
        rd = (S5T, psS[0], psS[1])
        def mop(fn):
            k.op('dve', fn, rd, (S5T,))
        mop(lambda g: g.tensor_tensor(out=M1[:], in0=Sr, in1=MUIr[:], op=ALU.mult)); mop(lambda g: g.tensor_tensor(out=M2[:], in0=Si, in1=MUIi[:], op=ALU.mult))
        mop(lambda g: g.tensor_tensor(out=M1[:], in0=M1[:], in1=M2[:], op=ALU.subtract))
        mop(lambda g: g.tensor_tensor(out=M2[:], in0=Sr, in1=MUIi[:], op=ALU.mult)); mop(lambda g: g.tensor_tensor(out=M3[:], in0=Si, in1=MUIr[:], op=ALU.mult))
        mop(lambda g: g.tensor_tensor(out=M2[:], in0=M2[:], in1=M3[:], op=ALU.add))
        fl = lambda a: a[:, :, :].rearrange("p q j -> p (q j)")
        mop(lambda g: g.tensor_tensor_scan(out=fl(M3), data0=fl(rmask), data1=fl(M1), initial=0.0, op0=ALU.mult, op1=ALU.add))
        mop(lambda g: g.tensor_tensor_scan(out=fl(M4), data0=fl(rmask), data1=fl(M2), initial=0.0, op0=ALU.mult, op1=ALU.add))
        mop(lambda g: g.tensor_tensor(out=M3[:], in0=M3[:], in1=Xr[:, :, 0:1].to_broadcast([128, 16, nj]), op=ALU.add))
        mop(lambda g: g.tensor_tensor(out=M4[:], in0=M4[:], in1=Xi[:, :, 0:1].to_broadcast([128, 16, nj]), op=ALU.add))
        s_cmul(Xr[:, :, 1:nj + 1], Xi[:, :, 1:nj + 1], M3[:], M4[:], MUFr[:], MUFi[:], M1[:], M2[:])
        sop(lambda g: g.tensor_copy(out=Xbr[:], in_=Xr[:])); sop(lambda g: g.tensor_copy(out=Xbi[:], in_=Xi[:]))
        s5_outputs(l, ntok, L, nj, Xbr, Xbi, lambda q: (slice(0, nj)))

    def s5_outputs(l, ntok, L, nj, Xb_r, Xb_i, xsl, xtok=None):
        xrd = (S5T,) if xtok is None else (S5T, xtok)
        W_ = L * 32
        for tq in range(6):
            npair = 3 if tq < 5 else 1
            nr = 32 * npair
            psY = [nps(), nps()]
            for qq in range(npair):
                q = tq * 3 + qq
                po = 32 * qq
                bank = psY[qq // 2]
                c0 = (qq % 2) * 256
                k.op('pe', lambda g: g.matmul(bank[0:nj, c0:c0 + W_], Xb_r[:, q, 0:nj], QBD[:, 0, q, 1:1 + L, :].rearrange("p r c -> p (r c)"), start=True, stop=False), xrd, (bank,))
                k.op('pe', lambda g: g.matmul(bank[0:nj, c0:c0 + W_], Xb_i[:, q, 0:nj], QBD[:, 1, q, 1:1 + L, :].rearrange("p r c -> p (r c)"), start=False, stop=False), xrd, (bank,))
                for sp in range(L):
                    k.op('pe', lambda g: g.matmul(bank[0:nj, c0 + sp * 32:c0 + W_], uT[po:po + 32, tq, sp:ntok:L], Kt[po:po + 32, tq, 0:L - sp, :].rearrange("p a b -> p (a b)"),
                                                  start=False, stop=(sp == L - 1)), (S5T, uT), (bank,))
                k.cp(ysb[0:nj, 0:L, qq, :], bank[0:nj, c0:c0 + W_].rearrange("p (r c) -> p r c", c=32), (bank,), (ysb,), e='act')
            psy = npt()
            for r in range(L):
                k.tr(psy[0:nr, r * nj:(r + 1) * nj], ysb[0:nj, r, 0:npair, :].rearrange("p a c -> p (a c)"), identb[0:nj, 0:nj], (ysb, identb), (psy,))
            k.op('dve', lambda g, tq=tq, nr=nr, psy=psy: g.scalar_tensor_tensor(
                out=zt[0:nr, 0:ntok].rearrange("p (j r) -> p r j", r=L), in0=uT[0:nr, tq, 0:ntok].rearrange("p (j r) -> p r j", r=L),
                scalar=dcol[0:nr, tq:tq + 1], in1=psy[0:nr, 0:ntok].rearrange("p (r j) -> p r j", j=nj), op0=ALU.mult, op1=ALU.add), (uT, S5T, psy), (zt,))
            k.act(ygT[0:nr, tq, 0:ntok], zt[0:nr, 0:ntok], AF.Gelu, (zt,), (ygT,))
        wgl = wload96(W['s5_w_glu'][l], 0, 512)
        for to in range(6):
            nro = 96 if to < 5 else 32
            psg = nps()
            for ti_ in range(6):
                nri = 96 if ti_ < 5 else 32
                k.op('pe', lambda g, to=to, ti_=ti_, nro=nro, nri=nri, psg=psg: g.matmul(psg[0:nro, 0:ntok], wgl[0:nri, ti_, to * 96:to * 96 + nro], ygT[0:nri, ti_, 0:ntok],
                                                                                   start=(ti_ == 0), stop=(ti_ == 5)), (wgl, ygT), (psg,))
            k.op('act', lambda g, to=to, nro=nro, psg=psg: g.activation(out=zt[0:nro, 0:ntok], in_=psg[0:nro, 0:ntok], func=AF.Sigmoid, bias=bgcol[0:nro, to:to + 1]), (psg, S5T), (zt,))
            k.tt(ys5T[0:nro, to, 0:ntok], ygT[0:nro, to, 0:ntok], zt[0:nro, 0:ntok], ALU.mult, (ygT, zt), (ys5T,))

    def s5_final_state_out(l):
        k.dma('sp', lambda g: g.dma_start(out=O['ps5re'][l].rearrange("(q r) p -> (r p) q", r=2), in_=Xr[:, :, NJ], allow_slow_non_contiguous=True), (S5T,), (), True)
        k.dma('sp', lambda g: g.dma_start(out=O['ps5im'][l].rearrange("(q r) p -> (r p) q", r=2), in_=Xi[:, :, NJ], allow_slow_non_contiguous=True), (S5T,), (), True)

    NB = SEQ // 128
    Fcarry = k.sb([128, 8], F32, 'Fcarry')
    QTh = k.sb([96, 8, TT], BF16, 'QTh')
    AQ = k.sb([128, 512], BF16, 'AQ'); k.memset(AQ, AQ[:], 0.0)
    k.memset(AQ, AQ[:, :].rearrange("p (h c) -> p h c", c=64)[:, :, 2:4], 1.0)
    AQTh = k.sb([32, 8, NS], BF16, 'AQTh')
    kst = k.sb([96, 8, TT], BF16, 'kst')
    AK = k.sb([128, 512], BF16, 'AK'); k.memset(AK, AK[:], 0.0); k.memset(AK, AK[:, :].rearrange("p (h c) -> p h c", c=64)[:, :, 0:2], 1.0)
    vst = k.sb([128, 8, 65], BF16, 'vst'); k.memset(vst, vst[:], 1.0)
    kbuf = [k.sb([128, SEQ], BF16, 'kbuf%d' % i) for i in range(2)]
    vbuf = [k.sb([128, NB, 65], BF16, 'vbuf%d' % i) for i in range(2)]
    bfb = k.sb([128, 8], F32, 'bfb')
    lf8 = k.sb([128, 8], F32, 'lf8'); F8 = k.sb([128, 8], F32, 'F8'); t8 = k.sb([128, 8], F32, 't8')
    pTf = k.sb([128, 2, 4 * TT], BF16, 'pTf')
    oTf = k.sb([65, TT], F32, 'oTf')
    recf = k.sb([64, TT], F32, 'recf')
    yfx = k.sb([64, 8, TT], BF16, 'yfx')
    sel65 = k.sb([65, 64], F32, 'sel65'); k.memset(sel65, sel65[:], 0.0); k.memset(sel65, sel65[64:65, :], 1.0)

    def fox_gates(ps_, P, blk_negF, out_lf_ap):
        k.tt(t8[0:P, :], ps_[0:P, 0:8], bfb[0:P, :], ALU.add, (ps_, bfb), (t8,))
        k.act(t8[0:P, :], t8[0:P, :], AF.Exp, (t8,), (t8,), scale=-1.0)
        k.act(t8[0:P, :], t8[0:P, :], AF.Ln, (t8,), (t8,), bias=1.0)
        k.ts(lf8[0:P, :], t8[0:P, :], -1.0, None, ALU.mult, None, (t8,), (lf8,))
        k.dma('sp', lambda g: g.dma_start(out=out_lf_ap, in_=lf8[0:P, :]), (lf8,), (), True)

    def fox_prompt_tile(l, ti):
        P, ntok = 128, TT
        t0 = ti * TT
        Win = W['w_in'][l]

        def cons_ff(ps_, s, cb, nc_):
            fox_gates(ps_, P, None, O['pflf'][l, t0:t0 + P, :])
            p1 = nps()
            k.mm(p1[:, 0:8], trif[:], lf8[:], True, True, (trif, lf8), (p1,))
            k.tt(F8[:], p1[:, 0:8], Fcarry[:], ALU.add, (p1, Fcarry), (F8,))
            p2 = nps()
            k.mm(p2[:, 0:8], onesf[:], lf8[:], True, True, (onesf, lf8), (p2,))
            k.tt(Fcarry[:], Fcarry[:], p2[:, 0:8], ALU.add, (Fcarry, p2), (Fcarry,))
            aqv = AQ[:, :].rearrange("p (h c) -> p h c", c=64)
            akv = AK[:, :].rearrange("p (h c) -> p h c", c=64)
            k.cp(aqv[:, :, 0], F8[:], (F8,), (AQ,))
            k.cp(t8[:], aqv[:, :, 0], (AQ,), (t8,))
            k.tt(aqv[:, :, 1], F8[:], t8[:], ALU.subtract, (F8, t8), (AQ,))
            k.ts(akv[:, :, 2], aqv[:, :, 0], -1.0, None, ALU.mult, None, (AQ,), (AK,))
            k.ts(akv[:, :, 3], aqv[:, :, 1], -1.0, None, ALU.mult, None, (AQ,), (AK,))
            transpose_heads(AQ, P, QTh, 0, lambda c: AQ[0:P, c * 128:(c + 1) * 128], nrows=32, prow=64)
            transpose_heads(AK, P, kst, 0, lambda c: AK[0:P, c * 128:(c + 1) * 128], nrows=32, prow=64)
        proj_tok(hT, 8, P, 1, Win, 0, C_FF, 8, cons_ff)

        def cons_fq(ps_, s, cb, nc_):
            head_rms(ps_, P, 8, 64, gvb['fgq'], tmpb, tmpb[0:P, 0:512].rearrange("p (h d) -> p h d", d=64), 0.125)
            transpose_heads(tmpb, P, QTh, 0, lambda c: tmpb[0:P, c * 128:(c + 1) * 128])
        proj_tok(hT, 8, P, 1, Win, 0, C_FQ, 512, cons_fq)

        def cons_fk(ps_, s, cb, nc_):
            head_rms(ps_, P, 8, 64, gvb['fgk'], kout, kout[0:P, 0:512].rearrange("p (h d) -> p h d", d=64), 1.0)
            k.dma('sp', lambda g: g.dma_start(out=O['pfk'][l, t0:t0 + P, :], in_=kout[0:P, 0:512]), (kout,), (), True)
            k.cp(tmpb[0:P, 0:512], kout[0:P, 0:512], (kout,), (tmpb,))
            transpose_heads(tmpb, P, kst, 0, lambda c: tmpb[0:P, c * 128:(c + 1) * 128])
            k.dma('sp', lambda g: g.dma_start(out=ktscr.rearrange("h d t -> d h t")[:, :, t0:t0 + P], in_=kst[:, :, 0:P]), (kst,), (KSCR,))
        proj_tok(hT, 8, P, 1, Win, 0, C_FK, 512, cons_fk)

        def cons_fv(ps_, s, cb, nc_):
            k.cp(rowt[0:P, 0:512], ps_[0:P, 0:512], (ps_,), (rowt,), e='act')
            k.dma('sp', lambda g: g.dma_start(out=O['pfv'][l, t0:t0 + P, :], in_=rowt[0:P, 0:512]), (rowt,), (), True)
            k.cp(vst[:, :, 0:64], rowt[:, 0:512].rearrange("p (h d) -> p h d", d=64), (rowt,), (vst,))
            k.dma('sp', lambda g: g.dma_start(out=vscr.rearrange("h p b c -> p h b c")[:, :, ti, :], in_=vst[:, :, :]), (vst,), (VSCR,))
        proj_tok(hT, 8, P, 1, Win, 0, C_FV, 512, cons_fv)

        nkb = ti + 1
        for h in range(8):
            kb_, vb_ = kbuf[h % 2], vbuf[h % 2]
            k.dma('sp', lambda g: g.dma_start(out=kb_[0:96, 0:nkb * 128], in_=ktscr[h, :, 0:nkb * 128]), (KSCR,), (kb_,))
            k.dma('sp', lambda g: g.dma_start(out=vb_[:, 0:nkb, :], in_=vscr[h, :, 0:nkb, :]), (VSCR,), (vb_,))
            pso = ps_acc[h % 2]
            for g0 in range(0, nkb, 4):
                ng = min(4, nkb - g0)
                ps_ = nps()
                par = (g0 // 4) % 2
                for i in range(ng):
                    kb = g0 + i
                    k.mm(ps_[:, i * ntok:(i + 1) * ntok], kb_[0:96, kb * 128:(kb + 1) * 128], QTh[0:96, h, 0:ntok], True, True, (kb_, QTh), (ps_,))
                k.act(pTf[:, par, 0:ng * ntok], ps_[:, 0:ng * ntok], AF.Exp, (ps_,), (pTf,), bias=-8.0)
                if g0 + ng == nkb:
                    i = ng - 1
                    k.tt(pTf[:, par, i * ntok:(i + 1) * ntok], pTf[:, par, i * ntok:(i + 1) * ntok], trib[:, 0:ntok], ALU.mult, (pTf, trib), (pTf,))
                for i in range(ng):
                    kb = g0 + i
                    k.mm(pso[0:65, 0:ntok], vb_[:, kb, :], pTf[:, par, i * ntok:(i + 1) * ntok], kb == 0, kb == nkb - 1, (vb_, pTf), (pso,))
            fox_finish_head(h, pso, ntok)

    def fox_finish_head(h, pso, ntok):
        if pso is not None:
            k.cp(oTf[:, 0:ntok], pso[0:65, 0:ntok], (pso,), (oTf,), e='act')
        psr = nps()
        k.mm(psr[0:64, 0:ntok], sel65[:], oTf[:, 0:ntok], True, True, (sel65, oTf), (psr,))
        k.op('dve', lambda g: g.reciprocal(out=recf[:, 0:ntok], in_=psr[0:64, 0:ntok]), (psr,), (recf,))
        k.tt(yfx[:, h, 0:ntok], oTf[0:64, 0:ntok], recf[:, 0:ntok], ALU.mult, (oTf, recf), (yfx,))

    Cst = k.sb([128, 4, 129], F32, 'Cst'); CsTb = k.sb([128, 4, 128], BF16, 'CsTb'); nselb = k.sb([128, 4, 4], BF16, 'nselb')
    Fm = k.sb([4, 1], F32, 'Fm'); Mm = k.sb([4, 1], F32, 'Mm'); Mpe_bc = k.sb([128, 4], F32, 'Mpe_bc'); Me_bc = k.sb([128, 4], F32, 'Me_bc')
    G4 = {n_: k.sb([4, TT], F32, 'g4_' + n_) for n_ in ('mi', 'mf', 'lf', 'F', 'a', 'M', 'negM', 'emt', 'zero', 'one', 'rden')}
    k.memset(G4['zero'], G4['zero'][:], 0.0); k.memset(G4['one'], G4['one'][:], 1.0)
    negbf = k.sb([4, 1], F32, 'negbf'); bicol = k.sb([4, 1], F32, 'bicol'); gncol = k.sb([128, 1], F32, 'gncol')
    mqT = k.sb([128, 4, TT], BF16, 'mqT'); mkT = k.sb([128, 4, TT], BF16, 'mkT'); so4 = k.sb([128, 4, TT], BF16, 'so4')
    mk_tok = k.sb([128, 512], BF16, 'mk_tok'); VM = k.sb([128, 4, 129], BF16, 'VM'); k.memset(VM, VM[:], 1.0)
    atok = k.sb([128, 4], F32, 'atok'); diag4 = k.sb([4, 4], F32, 'diag4'); dec_bc = k.sb([128, 4], F32, 'dec_bc'); wend = k.sb([128, 4], F32, 'wend')
    iwb = k.sb([128, TT], F32, 'iwb'); QpT = k.sb([128, TT], BF16, 'QpT'); Et = k.sb([128, TT], F32, 'Et'); SWb = k.sb([128, TT], BF16, 'SWb')
    Kw = k.sb([128, 128], BF16, 'Kw'); hh = k.sb([128, TT], F32, 'hh'); sqb = k.sb([128, TT], BF16, 'sqb'); rst = k.sb([128, TT], F32, 'rst')

    def mlstm_setup(l):
        k.dma('sp', lambda g: g.dma_start(out=negbf[:], in_=W['ml_bf'][l].rearrange("(h o) -> h o", o=1)), (), (negbf,))
        k.ts(negbf[:], negbf[:], -1.0, None, ALU.mult, None, (negbf,), (negbf,))
        k.dma('sp', lambda g: g.dma_start(out=bicol[:], in_=W['ml_bi'][l].rearrange("(h o) -> h o", o=1)), (), (bicol,))
        k.dma('sp', lambda g: g.dma_start(out=gncol[:], in_=W['ml_gn'][l].rearrange("(p o) -> p o", o=1)), (), (gncol,))

    def mlstm_zero_state():
        k.memset(Cst, Cst[:], 0.0); k.memset(CsTb, CsTb[:], 0.0); k.memset(nselb, nselb[:], 0.0)
        k.memset(Fm, Fm[:], 0.0); k.memset(Mm, Mm[:], 0.0); k.memset(Mpe_bc, Mpe_bc[:], 0.0)

    def mlstm_inproj(l, P, ntok):
        Win = W['w_in'][l]

        def c_mq(ps_, j, m):
            k.cp(mqT[:, j, 0:ntok], ps_[:, 0:ntok], (ps_,), (mqT,), e='act')
        proj_feat(hT, 8, ntok, Win, 0, C_MQ, 512, c_mq)

        def c_mk(ps_, j, m):
            k.op('act', lambda g: g.mul(out=mkT[:, j, 0:ntok], in_=ps_[:, 0:ntok], mul=128 ** -0.5), (ps_,), (mkT,))
        proj_feat(hT, 8, ntok, Win, 0, C_MK, 512, c_mk)

        def c_mkt(ps_, s, cb, nc_):
            k.op('act', lambda g: g.mul(out=mk_tok[0:P, :], in_=ps_[0:P, 0:512], mul=128 ** -0.5), (ps_,), (mk_tok,))
        proj_tok(hT, 8, P, 1, Win, 0, C_MK, 512, c_mkt)

        def c_mvt(ps_, s, cb, nc_):
            k.cp(VM[0:P, :, 0:128], ps_[0:P, 0:512].rearrange("p (h d) -> p h d", d=128), (ps_,), (VM,))
        proj_tok(hT, 8, P, 1, Win, 0, C_MV, 512, c_mvt)
        wb = wload(Win, 0, 1024, C_MI, 8)
        for gi, nm in enumerate(('mi', 'mf')):
            ps_ = nps()
            for kc in range(8):
                k.mm(ps_[0:4, 0:ntok], wb[:, kc, 4 * gi:4 * gi + 4], hT[:, kc, 0:ntok], kc == 0, kc == 7, (hT, wb), (ps_,))
            k.cp(G4[nm][:, 0:ntok], ps_[0:4, 0:ntok], (ps_,), (G4[nm],))

        def c_mo(ps_, j, m):
            k.act(so4[:, j, 0:ntok], ps_[:, 0:ntok], AF.Sigmoid, (ps_,), (so4,))
        proj_feat(hT, 8, ntok, Win, 0, C_MO, 512, c_mo)

    def mlstm_gates(ntok, scan_mask=None, a_override=None):
        g = G4
        k.act(g['lf'][:, 0:ntok], g['mf'][:, 0:ntok], AF.Exp, (g['mf'], negbf), (g['lf'],), bias=negbf[:], scale=-1.0)
        k.act(g['lf'][:, 0:ntok], g['lf'][:, 0:ntok], AF.Ln, (g['lf'],), (g['lf'],), bias=1.0)
        k.ts(g['lf'][:, 0:ntok], g['lf'][:, 0:ntok], -1.0, None, ALU.mult, None, (g['lf'],), (g['lf'],))
        k.ts(g['mi'][:, 0:ntok], g['mi'][:, 0:ntok], bicol[:], None, ALU.add, None, (g['mi'], bicol), (g['mi'],))

    def mlstm_prompt_tile(l):
        ntok = TT
        g = G4
        mlstm_gates(ntok)
        k.op('dve', lambda e: e.tensor_tensor_scan(out=g['F'][:, 0:ntok], data0=g['one'][:, 0:ntok], data1=g['lf'][:, 0:ntok], initial=Fm[:, 0:1],
                                                   op0=ALU.mult, op1=ALU.add), (g['one'], g['lf'], Fm), (g['F'],))
        k.tt(g['a'][:, 0:ntok], g['mi'][:, 0:ntok], g['F'][:, 0:ntok], ALU.subtract, (g['mi'], g['F']), (g['a'],))
        k.op('dve', lambda e: e.tensor_tensor_scan(out=g['M'][:, 0:ntok], data0=g['zero'][:, 0:ntok], data1=g['a'][:, 0:ntok], initial=Mm[:, 0:1],
                                                   op0=ALU.add, op1=ALU.max), (g['zero'], g['a'], Mm), (g['M'],))
        k.tt(g['emt'][:, 0:ntok], g['F'][:, 0:ntok], g['M'][:, 0:ntok], ALU.add, (g['F'], g['M']), (g['emt'],))
        k.act(g['emt'][:, 0:ntok], g['emt'][:, 0:ntok], AF.Exp, (g['emt'],), (g['emt'],), scale=-1.0)
        k.ts(g['negM'][:, 0:ntok], g['M'][:, 0:ntok], -1.0, None, ALU.mult, None, (g['M'],), (g['negM'],))
        ps_ = nps()
        k.tr(ps_[:, 0:4], g['a'][0:4, 0:ntok], identf[0:4, 0:4], (g['a'], identf), (ps_,))
        k.cp(atok[:], ps_[:, 0:4], (ps_,), (atok,))
        k.ts(diag4[:], identf[0:4, 0:4], g['M'][:, ntok - 1:ntok], None, ALU.mult, None, (identf, g['M']), (diag4,))
        ps_ = nps()
        k.mm(ps_[:, 0:4], onesf[0:4, 0:128], diag4[:], True, True, (onesf, diag4), (ps_,))
        k.cp(Me_bc[:], ps_[:, 0:4], (ps_,), (Me_bc,))
        k.tt(dec_bc[:], Mpe_bc[:], Me_bc[:], ALU.subtract, (Mpe_bc, Me_bc), (dec_bc,))
        k.act(dec_bc[:], dec_bc[:], AF.Exp, (dec_bc,), (dec_bc,))
        k.tt(wend[:], atok[:], Me_bc[:], ALU.subtract, (atok, Me_bc), (wend,))
        k.act(wend[:], wend[:], AF.Exp, (wend,), (wend,))
        for h in range(4):
            psn = nps()
            k.mm(psn[:, 0:ntok], selh[h][:], g['negM'][:, 0:ntok], True, True, (selh[h], g['negM']), (psn,))
            k.act(iwb[:, 0:ntok], psn[:, 0:ntok], AF.Exp, (psn, Mpe_bc), (iwb,), bias=Mpe_bc[:, h:h + 1])
            k.tt(QpT[:, 0:ntok], mqT[:, h, 0:ntok], iwb[:, 0:ntok], ALU.mult, (mqT, iwb), (QpT,))
            k.ts(Et[:, 0:ntok], psn[:, 0:ntok], atok[:, h:h + 1], 0.0, ALU.add, ALU.min, (psn, atok), (Et,))
            k.act(Et[:, 0:ntok], Et[:, 0:ntok], AF.Exp, (Et,), (Et,))
            k.tt(Et[:, 0:ntok], Et[:, 0:ntok], trif[:, 0:ntok], ALU.mult, (Et, trif), (Et,))
            pss = nps()
            k.mm(pss[:, 0:ntok], mkT[:, h, 0:ntok], mqT[:, h, 0:ntok], True, True, (mkT, mqT), (pss,))
            k.tt(SWb[:, 0:ntok], pss[:, 0:ntok], Et[:, 0:ntok], ALU.mult, (pss, Et), (SWb,))
            psnum = nps()
            k.mm(psnum[:, 0:ntok], VM[:, h, 0:128], SWb[:, 0:ntok], True, False, (VM, SWb), (psnum,))
            k.mm(psnum[:, 0:ntok], CsTb[:, h, :], QpT[:, 0:ntok], False, True, (CsTb, QpT), (psnum,))
            k.cp(arena_f[:, h * TT:h * TT + ntok], psnum[:, 0:ntok], (psnum,), (arena,), e='act')
            k.mm(psden[0:4, 0:ntok], onesel[:, h, :], SWb[:, 0:ntok], h == 0, False, (onesel, SWb), (psden,))
            k.mm(psden[0:4, 0:ntok], nselb[:, h, :], QpT[:, 0:ntok], False, h == 3, (nselb, QpT), (psden,))
            k.ts(Kw[:], mk_tok[:, h * 128:(h + 1) * 128], wend[:, h:h + 1], None, ALU.mult, None, (mk_tok, wend), (Kw,))
            psd = nps()
            k.mm(psd[:, 0:129], Kw[:], VM[:, h, :], True, True, (Kw, VM), (psd,))
            k.stt(Cst[:, h, :], Cst[:, h, :], dec_bc[:, h:h + 1], psd[:, 0:129], ALU.mult, ALU.add, (Cst, dec_bc, psd), (Cst,))
            k.cp(CsTb[:, h, :], Cst[:, h, 0:128], (Cst,), (CsTb,))
            k.cp(nselb[:, h, h:h + 1], Cst[:, h, 128:129], (Cst,), (nselb,))
        mlstm_finish(ntok)
        k.cp(Fm[:], g['F'][:, ntok - 1:ntok], (g['F'],), (Fm,))
        k.cp(Mm[:], g['M'][:, ntok - 1:ntok], (g['M'],), (Mm,))
        k.cp(Mpe_bc[:], Me_bc[:], (Me_bc,), (Mpe_bc,))

    def mlstm_finish(ntok):
        g = G4
        k.act(g['rden'][:, 0:ntok], psden[0:4, 0:ntok], AF.Abs, (psden,), (g['rden'],))
        k.tt(g['rden'][:, 0:ntok], g['rden'][:, 0:ntok], g['emt'][:, 0:ntok], ALU.max, (g['rden'], g['emt']), (g['rden'],))
        k.op('dve', lambda e: e.reciprocal(out=g['rden'][:, 0:ntok], in_=g['rden'][:, 0:ntok]), (g['rden'],), (g['rden'],))
        for h in range(4):
            psr = nps()
            k.mm(psr[:, 0:ntok], selh[h][:], g['rden'][:, 0:ntok], True, True, (selh[h], g['rden']), (psr,))
            k.tt(hh[:, 0:ntok], arena_f[:, h * TT:h * TT + ntok], psr[:, 0:ntok], ALU.mult, (arena, psr), (hh,))
            k.act(sqb[:, 0:ntok], hh[:, 0:ntok], AF.Square, (hh,), (sqb,))
            ps2 = nps()
            k.mm(ps2[:, 0:ntok], onesb[:, :], sqb[:, 0:ntok], True, True, (onesb, sqb), (ps2,))
            k.ts(rst[:, 0:ntok], ps2[:, 0:ntok], 1.0 / 128, EPS, ALU.mult, ALU.add, (ps2,), (rst,))
            k.act(rst[:, 0:ntok], rst[:, 0:ntok], AF.Sqrt, (rst,), (rst,))
            k.op('dve', lambda e: e.reciprocal(out=rst[:, 0:ntok], in_=rst[:, 0:ntok]), (rst,), (rst,))
            k.tt(hh[:, 0:ntok], hh[:, 0:ntok], rst[:, 0:ntok], ALU.mult, (hh, rst), (hh,))
            k.stt(yT['ml'][:, h, 0:ntok], hh[:, 0:ntok], gncol[:, 0:1], so4[:, h, 0:ntok], ALU.mult, ALU.mult, (hh, gncol, so4), (yT['ml'],))

    def mlstm_prompt_out(l):
        for h in range(4):
            ps_ = nps()
            k.tr(ps_[:, 0:128], Cst[:, h, 0:128], identf[:, :], (Cst, identf), (ps_,))
            k.cp(rowt[:, 0:128], ps_[:, 0:128], (ps_,), (rowt,), e='act')
            k.dma('sp', lambda g: g.dma_start(out=O['pmlC'][l, h], in_=rowt[:, 0:128]), (rowt,), (), True)
            k.dma('sp', lambda g: g.dma_start(out=O['pmln'][l, h].rearrange("(p o) -> p o", o=1), in_=Cst[:, h, 128:129]), (Cst,), (), True)
        k.tt(G4['rden'][:, 0:1], Fm[:], Mm[:], ALU.add, (Fm, Mm), (G4['rden'],))
        k.dma('sp', lambda g: g.dma_start(out=O['pmlm'][l].rearrange("(h o) -> h o", o=1), in_=G4['rden'][:, 0:1]), (G4['rden'],), (), True)

    def merge_out(l, P, ntok):
        Win = W['w_in'][l]
        Gt = arena[:, :, :].rearrange("p a t -> p (a t)")

        def cons_g(ps_, s, cb, nc_):
            k.act(Gt[0:P, cb:cb + nc_], ps_[0:P, 0:nc_], AF.Sigmoid, (ps_,), (arena,))
        proj_tok(hT, 8, P, 1, Win, 0, C_G, 3072, cons_g)
        for cb in range(2):
            cs = slice(cb * 512, (cb + 1) * 512)
            wb = wload96(W['w_br_s5'][l], cb * 512, 512)
            ps_ = nps()
            for tq in range(6):
                nr = 96 if tq < 5 else 32
                k.mm(ps_[0:P, :], ys5T[0:nr, tq, 0:P], wb[0:nr, tq, :], tq == 0, tq == 5, (ys5T, wb), (ps_,))
            k.tt(mrgf[0:P, cs], ps_[0:P, :], Gt[0:P, cb * 512:(cb + 1) * 512], ALU.mult, (ps_, arena), (mrgf,))
            wb = wload(W['w_br_fox'][l], 0, 512, cb * 512, 512, p=64)
            ps_ = nps()
            for h in range(8):
                k.mm(ps_[0:P, :], yfx[:, h, 0:P], wb[0:64, h, :], h == 0, h == 7, (yfx, wb), (ps_,))
            k.tt(rowt[0:P, :], ps_[0:P, :], Gt[0:P, 1024 + cb * 512:1024 + (cb + 1) * 512], ALU.mult, (ps_, arena), (rowt,))
            k.tt(mrgf[0:P, cs], mrgf[0:P, cs], rowt[0:P, :], ALU.add, (mrgf, rowt), (mrgf,))
            wb = wload(W['w_br_ml'][l], 0, 512, cb * 512, 512)
            ps_ = nps()
            for h in range(4):
                k.mm(ps_[0:P, :], yT['ml'][:, h, 0:P], wb[:, h, :], h == 0, h == 3, (yT['ml'], wb), (ps_,))
            k.tt(rowt[0:P, :], ps_[0:P, :], Gt[0:P, 2048 + cb * 512:2048 + (cb + 1) * 512], ALU.mult, (ps_, arena), (rowt,))
            k.tt(mrgf[0:P, cs], mrgf[0:P, cs], rowt[0:P, :], ALU.add, (mrgf, rowt), (mrgf,))
        k.cp(tmpb[0:P, :], mrgf[0:P, :], (mrgf,), (tmpb,))
        transpose_to_T(tmpb, P, 8, hT, 0, lambda c: tmpb[0:P, c * 128:(c + 1) * 128])

        def cons_o(ps_, s, cb, nc_):
            k.tt(xres[0:P, 0, cb:cb + nc_], xres[0:P, 0, cb:cb + nc_], ps_[0:P, 0:nc_], ALU.add, (xres, ps_), (xres,))
        proj_tok(hT, 8, P, 1, W['w_out'][l], 0, 0, D, cons_o)

    def s5_inproj(l, ntok):
        def c_u(ps_, j, m):
            k.cp(uT[0:m, j, 0:ntok], ps_[0:m, 0:ntok], (ps_,), (uT,), e='act')
        proj_feat(hT, 8, ntok, W['w_in'][l], 0, C_S5, 512, c_u, chunk=96)

    BIG = 1.0e30

    def carve(parent, off, shape, dt):
        pt_ = parent.t
        pshape = list(pt_.shape)
        n0 = 1
        for d_ in pshape[1:]:
            n0 *= d_
        flat = pt_.reshape([pshape[0], n0]) if len(pshape) > 2 else pt_
        esz = mybir.dt.size(pt_.dtype)
        n = 1
        for d_ in shape[1:]:
            n *= d_
        nbytes = n * mybir.dt.size(dt)
        assert off % esz == 0 and nbytes % esz == 0 and (off + nbytes) <= n0 * esz and shape[0] <= pshape[0]
        ap = flat[0:shape[0], off // esz:(off + nbytes) // esz]
        if dt != pt_.dtype:
            ap = ap.bitcast(dt)
        if len(shape) == 3:
            ap = ap.rearrange("p (a b) -> p a b", b=shape[2])
        elif len(shape) == 4:
            ap = ap.rearrange("p (a b c) -> p a b c", b=shape[2], c=shape[3])
        return ap

    def sample_group():
        P = ntok = NS
        GB = []
        for par in range(2):
            GB.append(dict(tok=kbuf[par], gK=carve(kbuf[par], 0, [128, 512], F32), gV=carve(kbuf[par], 2048, [128, 512], F32),
                           gL=carve(kbuf[par], 4096, [128, 8], F32)))
        kpg = carve(vbuf[0], 0, [64, 8, 128], BF16); vpg = carve(vbuf[0], 2048, [128, 8, 65], BF16)
        k.op('pool', lambda g: g.memset(vpg, 1.0), (), (vbuf[0],))
        Et2 = carve(vbuf[1], 0, [128, 32], F32); PT2 = carve(vbuf[1], 128, [128, 8, 4], BF16)
        nb8 = carve(vbuf[1], 256, [128, 8], F32); Rc = carve(vbuf[1], 288, [128, 8], F32)
        Oacc = carve(Cst, 0, [65, 8, NS], F32)
        C0 = carve(CsTb, 0, [128, 128], F32); C0T = carve(CsTb, 512, [128, 128], BF16)
        X0r = carve(mrgf, 0, [128, 16, 16], F32); X0i = carve(mrgf, 1024, [128, 16, 16], F32)
        X0br = carve(mrgf, 2048, [128, 16, 16], BF16); X0bi = carve(mrgf, 2560, [128, 16, 16], BF16)
        n0tok = carve(junkb, 0, [16, 512], F32)
        ptb = carve(rowt, 0, [128, SB * NPAGE], I32)
        idx_all = k.sb([128, SB * NPAGE], I32, 'idx_all'); iot = k.sb([128, 1], I32, 'iot')
        E16 = k.sb([16, NS], F32, 'E16'); ETf = k.sb([NS, 16], F32, 'ETf'); blkF = k.sb([NS, NS], F32, 'blkF'); blkB = k.sb([NS, NS], BF16, 'blkB')
        ones8 = k.sb([8, 128], F32, 'ones8'); k.memset(ones8, ones8[:], 1.0)
        sel8buf = k.sb([8, 128], F32, 'sel8buf')
        eD = k.sb([8, NS], F32, 'eD'); negDk = k.sb([NS, 8], F32, 'negDk')
        S4 = {n_: k.sb([4, NS], F32, 's4_' + n_) for n_ in ('m0full', 'Mefull', 'iw4', 'a2')}
        m0T = k.sb([4, 16], F32, 'm0T'); dec4 = k.sb([4, 16], F32, 'dec4'); mnew4 = k.sb([4, 16], F32, 'mnew4'); dd4 = k.sb([4, 64], F32, 'dd4')
        Metok = k.sb([NS, 4], F32, 'Metok'); decT = k.sb([16, 4], F32, 'decT'); decbc = k.sb([128, 64], F32, 'decbc')
        n0T = k.sb([128, 64], F32, 'n0T'); n0sel = k.sb([128, 16, 4, 4], BF16, 'n0sel'); k.memset(n0sel, n0sel[:], 0.0)
        Wm = k.sb([NS, 16], F32, 'Wm'); Wmb = k.sb([NS, 16], BF16, 'Wmb'); Vw = k.sb([NS, 128], BF16, 'Vw')
        rmaskF, reset4 = G4['one'], G4['zero']

        k.memset(E16, E16[:], 1.0)
        k.op('pool', lambda g: g.affine_select(out=E16[:], in_=E16[:], pattern=[[1, NS]], compare_op=ALU.is_ge, fill=0.0, base=0, channel_multiplier=-ST), (E16,), (E16,))
        k.op('pool', lambda g: g.affine_select(out=E16[:], in_=E16[:], pattern=[[-1, NS]], compare_op=ALU.is_ge, fill=0.0, base=ST - 1, channel_multiplier=ST), (E16,), (E16,))
        ps_ = nps()
        k.mm(ps_[0:NS, 0:NS], E16[:, :], E16[:, :], True, True, (E16,), (ps_,))
        k.tt(blkF[:], ps_[0:NS, 0:NS], trif[0:NS, 0:NS], ALU.mult, (ps_, trif), (blkF,))
        k.cp(blkB[:], blkF[:], (blkF,), (blkB,))
        ps_ = nps()
        k.tr(ps_[0:NS, 0:16], E16[:, :], identf[0:16, 0:16], (E16, identf), (ps_,))
        k.cp(ETf[:], ps_[0:NS, 0:16], (ps_,), (ETf,))
        k.dma('sp', lambda g: g.dma_start(out=ptb, in_=I['pt'][0].partition_broadcast(128)), (), (rowt,))
        k.op('pool', lambda g: g.iota(iot[:], pattern=[[0, 1]], base=0, channel_multiplier=1), (), (iot,))
        k.ts(idx_all[:], ptb, PAGE, iot[:], ALU.mult, ALU.add, (rowt, iot), (idx_all,))
        k.memset(rmaskF, rmaskF[:], 1.0); k.memset(rmaskF, rmaskF[:, 0:NS].rearrange("p (b t) -> p b t", t=ST)[:, :, 0], 0.0)
        k.memset(reset4, reset4[:], BIG); k.memset(reset4, reset4[:, 0:NS].rearrange("p (b t) -> p b t", t=ST)[:, :, 0], -BIG)

        k.dma('sp', lambda g: g.dma_start(out=xres[0:NS, 0, :], in_=I['xs'][:, :]), (), (xres,))

        def fox_sample(l):
            Win = W['w_in'][l]

            def cons_ff(ps_, s, cb, nc_):
                fox_gates(ps_, P, None, O['sflf'][l, :, :])
                p1 = nps()
                k.mm(p1[0:P, 0:8], blkF[:, :], lf8[0:P, :], True, True, (blkF, lf8), (p1,))
                k.cp(F8[0:P, :], p1[0:P, 0:8], (p1,), (F8,))
                k.ts(negDk[:], F8[0:P, :], -1.0, -8.0, ALU.mult, ALU.add, (F8,), (negDk,))
                aqv = AQ[:, :].rearrange("p (h c) -> p h c", c=64)
                k.cp(aqv[0:P, :, 0], F8[0:P, :], (F8,), (AQ,))
                k.cp(t8[0:P, :], aqv[0:P, :, 0], (AQ,), (t8,))
                k.tt(aqv[0:P, :, 1], F8[0:P, :], t8[0:P, :], ALU.subtract, (F8, t8), (AQ,))
                transpose_heads(AQ, P, AQTh, 0, lambda c: AQ[0:P, c * 128:(c + 1) * 128], nrows=32)
                p2 = nps()
                k.tr(p2[0:8, 0:P], F8[0:P, 0:8], identf[0:P, 0:P], (F8, identf), (p2,))
                k.act(eD[:, 0:P], p2[0:8, 0:P], AF.Exp, (p2,), (eD,))
            proj_tok(hT, 8, P, 1, Win, 0, C_FF, 8, cons_ff)

            def cons_fq(ps_, s, cb, nc_):
                head_rms(ps_, P, 8, 64, gvb['fgq'], tmpb, tmpb[0:P, 0:512].rearrange("p (h d) -> p h d", d=64), 0.125)
                transpose_heads(tmpb, P, QTh, 0, lambda c: tmpb[0:P, c * 128:(c + 1) * 128])
            proj_tok(hT, 8, P, 1, Win, 0, C_FQ, 512, cons_fq)

            def cons_fk(ps_, s, cb, nc_):
                head_rms(ps_, P, 8, 64, gvb['fgk'], kout, kout[0:P, 0:512].rearrange("p (h d) -> p h d", d=64), 1.0)
                k.dma('sp', lambda g: g.dma_start(out=O['sfk'][l, :, :], in_=kout[0:P, 0:512]), (kout,), (), True)
                k.cp(tmpb[0:P, 0:512], kout[0:P, 0:512], (kout,), (tmpb,))
                transpose_heads(tmpb, P, kst, 0, lambda c: tmpb[0:P, c * 128:(c + 1) * 128])
            proj_tok(hT, 8, P, 1, Win, 0, C_FK, 512, cons_fk)

            def cons_fv(ps_, s, cb, nc_):
                k.cp(rowt[0:P, 0:512], ps_[0:P, 0:512], (ps_,), (rowt,), e='act')
                k.dma('sp', lambda g: g.dma_start(out=O['sfv'][l, :, :], in_=rowt[0:P, 0:512]), (rowt,), (), True)
                k.cp(vst[0:P, :, 0:64], rowt[0:P, 0:512].rearrange("p (h d) -> p h d", d=64), (rowt,), (vst,))
            proj_tok(hT, 8, P, 1, Win, 0, C_FV, 512, cons_fv)

            if l > 0:
                k.ts(idx_all[:], idx_all[:], NPOOL * PAGE, None, ALU.add, None, (idx_all,), (idx_all,))
            ckf = I['ck'].rearrange("l n c -> (l n) c"); cvf = I['cv'].rearrange("l n c -> (l n) c"); clff = I['clf'].rearrange("l n c -> (l n) c")
            k.op('pool', lambda g: g.memset(Oacc, 0.0), (), (Cst,))
            cnt = 0
            for b in range(SB):
                k.op('pool', lambda g: g.memset(Rc, 0.0), (), (vbuf[1],))
                for j in range(NPAGE - 1, -1, -1):
                    G = GB[cnt % 2]; cnt += 1
                    col = b * NPAGE + j
                    off = bass.IndirectOffsetOnAxis(ap=idx_all[:, col:col + 1], axis=0)
                    k.dma('pool', lambda g: g.indirect_dma_start(out=G['gK'], out_offset=None, in_=ckf, in_offset=off), (idx_all,), (G['tok'],))
                    k.dma('pool', lambda g: g.indirect_dma_start(out=G['gV'], out_offset=None, in_=cvf, in_offset=off), (idx_all,), (G['tok'],))
                    k.dma('pool', lambda g: g.indirect_dma_start(out=G['gL'], out_offset=None, in_=clff, in_offset=off), (idx_all,), (G['tok'],))
                    p1 = nps()
                    k.mm(p1[:, 0:8], triR[:, :], G['gL'], True, True, (triR, G['tok']), (p1,))
                    k.stt(nb8, p1[:, 0:8], 1.0, Rc, ALU.mult, ALU.add, (p1, vbuf[1]), (vbuf[1],))
                    k.ts(nb8, nb8, -8.0, None, ALU.add, None, (vbuf[1],), (vbuf[1],))
                    p2 = nps()
                    k.mm(p2[:, 0:8], onesf[:, :], G['gL'], True, True, (onesf, G['tok']), (p2,))
                    k.tt(Rc, Rc, p2[:, 0:8], ALU.add, (vbuf[1], p2), (vbuf[1],))
                    k.cp(tmpb[:, 0:512], G['gK'], (G['tok'],), (tmpb,), e='act')
                    transpose_heads(tmpb, 128, vbuf[0], 0, lambda c: tmpb[:, c * 128:(c + 1) * 128], dst_ap=kpg)
                    k.cp(vpg[:, :, 0:64], G['gV'].rearrange("p (h d) -> p h d", d=64), (G['tok'],), (vbuf[0],))
                    ps_ = nps()
                    for h in range(8):
                        k.mm(ps_[:, h * 4:(h + 1) * 4], kpg[0:64, h, :], QTh[0:64, h, ST * b:ST * b + ST], True, True, (vbuf[0], QTh), (ps_,))
                    k.tt(Et2.rearrange("p (h q) -> p h q", q=ST), ps_[:, 0:32].rearrange("p (h q) -> p h q", q=ST),
                         nb8.unsqueeze(2).to_broadcast([128, 8, ST]), ALU.add, (ps_, vbuf[1]), (vbuf[1],))
                    k.act(PT2.rearrange("p h q -> p (h q)"), Et2, AF.Exp, (vbuf[1],), (vbuf[1],))
                    pso = nps()
                    for h in range(8):
                        k.mm(pso[0:65, h * 4:(h + 1) * 4], vpg[:, h, :], PT2[:, h, :], True, True, (vbuf[0], vbuf[1]), (pso,))
                    k.tt(Oacc[:, :, ST * b:ST * b + ST], Oacc[:, :, ST * b:ST * b + ST], pso[0:65, 0:32].rearrange("p (h q) -> p h q", q=ST),
                         ALU.add, (Cst, pso), (Cst,))
            for h in range(8):
                ps_ = nps()
                k.mm(ps_[0:P, 0:P], kst[0:64, h, 0:P], QTh[0:64, h, 0:P], True, False, (kst, QTh), (ps_,))
                k.mm(ps_[0:P, 0:P], onesb[0:2, 0:P], AQTh[0:2, h, 0:P], False, True, (onesb, AQTh), (ps_,))
                k.act(pTf[0:P, 0, 0:P], ps_[0:P, 0:P], AF.Exp, (ps_, negDk), (pTf,), bias=negDk[:, h:h + 1])
                k.tt(pTf[0:P, 0, 0:P], pTf[0:P, 0, 0:P], blkB[:, :], ALU.mult, (pTf, blkB), (pTf,))
                pso = ps_acc[h % 2]
                k.mm(pso[0:65, 0:P], vst[0:P, h, :], pTf[0:P, 0, 0:P], True, True, (vst, pTf), (pso,))
                psr = nps()
                k.op('pool', lambda g: g.affine_select(out=sel8buf[:], in_=ones8[:], pattern=[[0, 128]], compare_op=ALU.is_equal,
                                                       fill=0.0, base=-h, channel_multiplier=1), (ones8,), (sel8buf,))
                k.mm(psr[0:65, 0:P], sel8buf[:, 0:65], eD[:, 0:P], True, True, (sel8buf, eD), (psr,))
                k.tt(oTf[:, 0:P], Oacc[:, h, :], psr[0:65, 0:P], ALU.mult, (Cst, psr), (oTf,))
                k.tt(oTf[:, 0:P], oTf[:, 0:P], pso[0:65, 0:P], ALU.add, (oTf, pso), (oTf,))
                fox_finish_head(h, None, P)

        def s5_sample(l):
            L, nj = ST, SB
            s5_inproj(l, ntok)
            for q in range(16):
                for r in range(2):
                    k.dma('sp', lambda g: g.dma_start(out=X0r[r * 64:(r + 1) * 64, q, :], in_=I['s5re'][l][:, 2 * q + r, :].rearrange("b p -> p b"),
                                                      allow_slow_non_contiguous=True), (), (mrgf,))
                    k.dma('sp', lambda g: g.dma_start(out=X0i[r * 64:(r + 1) * 64, q, :], in_=I['s5im'][l][:, 2 * q + r, :].rearrange("b p -> p b"),
                                                      allow_slow_non_contiguous=True), (), (mrgf,))
            k.cp(X0br, X0r, (mrgf,), (mrgf,)); k.cp(X0bi, X0i, (mrgf,), (mrgf,))
            psS = [nps(), nps()]
            for q in range(16):
                po, tq = 32 * (q % 3), q // 3
                for ri in range(2):
                    for sp in range(L):
                        k.op('pe', lambda g: g.matmul(psS[ri][:, q * nj:(q + 1) * nj], W1[po:po + 32, L - 1 - sp, ri, tq, :],
                                                      uT[po:po + 32, tq, sp:ntok:L], start=(sp == 0), stop=(sp == L - 1)), (S5T, uT), (psS[ri],))
            bc16 = lambda a: a.unsqueeze(2).to_broadcast([128, 16, 16])
            rd = (S5T, mrgf, psS[0], psS[1])
            def mop(fn):
                k.op('dve', fn, rd, (S5T,))
            mop(lambda g: g.tensor_tensor(out=M1[:], in0=X0r, in1=bc16(pwr[:, 4, :]), op=ALU.mult))
            mop(lambda g: g.tensor_tensor(out=M3[:], in0=X0i, in1=bc16(pwi[:, 4, :]), op=ALU.mult))
            mop(lambda g: g.tensor_tensor(out=M1[:], in0=M1[:], in1=M3[:], op=ALU.subtract))
            mop(lambda g: g.tensor_tensor(out=M2[:], in0=X0r, in1=bc16(pwi[:, 4, :]), op=ALU.mult))
            mop(lambda g: g.tensor_tensor(out=M3[:], in0=X0i, in1=bc16(pwr[:, 4, :]), op=ALU.mult))
            mop(lambda g: g.tensor_tensor(out=M2[:], in0=M2[:], in1=M3[:], op=ALU.add))
            mop(lambda g: g.tensor_tensor(out=M1[:], in0=M1[:], in1=psS[0][:, 0:256].rearrange("p (q j) -> p q j", j=nj), op=ALU.add))
            mop(lambda g: g.tensor_tensor(out=M2[:], in0=M2[:], in1=psS[1][:, 0:256].rearrange("p (q j) -> p q j", j=nj), op=ALU.add))
            for q in range(16):
                for r in range(2):
                    k.dma('sp', lambda g: g.dma_start(out=O['ss5re'][l][:, 2 * q + r, :].rearrange("b p -> p b"), in_=M1[r * 64:(r + 1) * 64, q, :],
                                                      allow_slow_non_contiguous=True), (S5T,), (), True)
                    k.dma('sp', lambda g: g.dma_start(out=O['ss5im'][l][:, 2 * q + r, :].rearrange("b p -> p b"), in_=M2[r * 64:(r + 1) * 64, q, :],
                                                      allow_slow_non_contiguous=True), (S5T,), (), True)
            s5_outputs(l, ntok, L, nj, X0br, X0bi, None, xtok=mrgf)

        def mlstm_sample(l):
            g = G4
            s4 = S4
            mlstm_inproj(l, P, ntok)
            mlstm_gates(ntok)
            k.dma('sp', lambda e: e.dma_start(out=m0T[:], in_=I['mlm'][l].rearrange("b h -> h b"), allow_slow_non_contiguous=True), (), (m0T,))
            v3 = lambda a: a[:, 0:ntok].rearrange("p (b t) -> p b t", t=ST)
            k.cp(v3(s4['m0full']), m0T[:, :].unsqueeze(2).to_broadcast([4, SB, ST]), (m0T,), (s4['m0full'],))
            k.op('dve', lambda e: e.tensor_tensor_scan(out=g['F'][:, 0:ntok], data0=rmaskF[:, 0:ntok], data1=g['lf'][:, 0:ntok], initial=0.0,
                                                       op0=ALU.mult, op1=ALU.add), (rmaskF, g['lf']), (g['F'],))
            k.tt(g['a'][:, 0:ntok], g['mi'][:, 0:ntok], g['F'][:, 0:ntok], ALU.subtract, (g['mi'], g['F']), (g['a'],))
            k.tt(s4['a2'][:, 0:ntok], g['a'][:, 0:ntok], s4['m0full'][:, 0:ntok], ALU.max, (g['a'], s4['m0full']), (s4['a2'],))
            k.op('dve', lambda e: e.tensor_tensor_scan(out=g['M'][:, 0:ntok], data0=reset4[:, 0:ntok], data1=s4['a2'][:, 0:ntok], initial=-BIG,
                                                       op0=ALU.min, op1=ALU.max), (reset4, s4['a2']), (g['M'],))
            k.tt(g['emt'][:, 0:ntok], g['F'][:, 0:ntok], g['M'][:, 0:ntok], ALU.add, (g['F'], g['M']), (g['emt'],))
            k.cp(mnew4[:], v3(g['emt'])[:, :, ST - 1], (g['emt'],), (mnew4,))
            k.dma('sp', lambda e: e.dma_start(out=O['smlm'][l].rearrange("b h -> h b"), in_=mnew4[:], allow_slow_non_contiguous=True), (mnew4,), (), True)
            k.act(g['emt'][:, 0:ntok], g['emt'][:, 0:ntok], AF.Exp, (g['emt'],), (g['emt'],), scale=-1.0)
            k.ts(g['negM'][:, 0:ntok], g['M'][:, 0:ntok], -1.0, None, ALU.mult, None, (g['M'],), (g['negM'],))
            k.cp(v3(s4['Mefull']), v3(g['M'])[:, :, ST - 1:ST].to_broadcast([4, SB, ST]), (g['M'],), (s4['Mefull'],))
            k.tt(s4['iw4'][:, 0:ntok], s4['m0full'][:, 0:ntok], g['M'][:, 0:ntok], ALU.subtract, (s4['m0full'], g['M']), (s4['iw4'],))
            k.act(s4['iw4'][:, 0:ntok], s4['iw4'][:, 0:ntok], AF.Exp, (s4['iw4'],), (s4['iw4'],))
            k.tt(dec4[:], m0T[:], v3(g['M'])[:, :, ST - 1], ALU.subtract, (m0T, g['M']), (dec4,))
            k.act(dec4[:], dec4[:], AF.Exp, (dec4,), (dec4,))
            ps_ = nps()
            k.tr(ps_[0:ntok, 0:4], g['a'][0:4, 0:ntok], identf[0:4, 0:4], (g['a'], identf), (ps_,))
            k.cp(atok[0:ntok, :], ps_[0:ntok, 0:4], (ps_,), (atok,))
            ps_ = nps()
            k.tr(ps_[0:ntok, 0:4], s4['Mefull'][0:4, 0:ntok], identf[0:4, 0:4], (s4['Mefull'], identf), (ps_,))
            k.cp(Metok[:], ps_[0:ntok, 0:4], (ps_,), (Metok,))
            k.tt(wend[0:ntok, :], atok[0:ntok, :], Metok[:], ALU.subtract, (atok, Metok), (wend,))
            k.act(wend[0:ntok, :], wend[0:ntok, :], AF.Exp, (wend,), (wend,))
            ps_ = nps()
            k.tr(ps_[0:16, 0:4], dec4[0:4, 0:16], identf[0:4, 0:4], (dec4, identf), (ps_,))
            k.cp(decT[:], ps_[0:16, 0:4], (ps_,), (decT,))
            k.tt(dd4[:, :].rearrange("p (a b) -> p a b", b=16), identf[0:4, 0:4].unsqueeze(2).to_broadcast([4, 4, 16]),
                 dec4[:, :].unsqueeze(1).to_broadcast([4, 4, 16]), ALU.mult, (identf, dec4), (dd4,))
            ps_ = nps()
            k.mm(ps_[:, 0:64], onesf[0:4, 0:128], dd4[:, :], True, True, (onesf, dd4), (ps_,))
            k.cp(decbc[:], ps_[:, 0:64], (ps_,), (decbc,))
            k.dma('sp', lambda e: e.dma_start(out=n0T[:], in_=I['mln'][l].rearrange("b h d -> d (b h)"), allow_slow_non_contiguous=True), (), (n0T,))
            for h in range(4):
                k.cp(n0sel[:, :, h, h], n0T[:, :].rearrange("p (b h) -> p b h", h=4)[:, :, h], (n0T,), (n0sel,))
            k.dma('sp', lambda e: e.dma_start(out=n0tok, in_=I['mln'][l].rearrange("b h d -> b (h d)")), (), (junkb,))
            for h in range(4):
                psn = nps()
                k.mm(psn[0:ntok, 0:ntok], selh[h][:, 0:ntok], g['negM'][:, 0:ntok], True, True, (selh[h], g['negM']), (psn,))
                psi = nps()
                k.mm(psi[:, 0:ntok], selh[h][:, :], s4['iw4'][:, 0:ntok], True, True, (selh[h], s4['iw4']), (psi,))
                k.tt(QpT[:, 0:ntok], mqT[:, h, 0:ntok], psi[:, 0:ntok], ALU.mult, (mqT, psi), (QpT,))
                k.ts(Et[0:ntok, 0:ntok], psn[0:ntok, 0:ntok], atok[0:ntok, h:h + 1], 0.0, ALU.add, ALU.min, (psn, atok), (Et,))
                k.act(Et[0:ntok, 0:ntok], Et[0:ntok, 0:ntok], AF.Exp, (Et,), (Et,))
                k.tt(Et[0:ntok, 0:ntok], Et[0:ntok, 0:ntok], blkF[:, :], ALU.mult, (Et, blkF), (Et,))
                pss = nps()
                k.mm(pss[0:ntok, 0:ntok], mkT[:, h, 0:ntok], mqT[:, h, 0:ntok], True, True, (mkT, mqT), (pss,))
                k.tt(SWb[0:ntok, 0:ntok], pss[0:ntok, 0:ntok], Et[0:ntok, 0:ntok], ALU.mult, (pss, Et), (SWb,))
                psnum = ps_acc[0]
                k.mm(psnum[:, 0:ntok], VM[0:ntok, h, 0:128], SWb[0:ntok, 0:ntok], True, False, (VM, SWb), (psnum,))
                k.mm(psden[0:4, 0:ntok], onesel[0:ntok, h, :], SWb[0:ntok, 0:ntok], h == 0, False, (onesel, SWb), (psden,))
                k.ts(Wm[:], ETf[:], wend[0:ntok, h:h + 1], None, ALU.mult, None, (ETf, wend), (Wm,))
                k.cp(Wmb[:], Wm[:], (Wm,), (Wmb,))
                psn2 = ps_acc[1]
                k.mm(psn2[0:16, 0:128], Wmb[:, :], mk_tok[0:ntok, h * 128:(h + 1) * 128], True, True, (Wmb, mk_tok), (psn2,))
                k.stt(n0tok[:, h * 128:(h + 1) * 128], n0tok[:, h * 128:(h + 1) * 128], decT[:, h:h + 1], psn2[0:16, 0:128], ALU.mult, ALU.add,
                      (junkb, decT, psn2), (junkb,))
                for b in range(SB):
                    k.dma('sp', lambda e: e.dma_start(out=C0, in_=I['mlC'][l, b, h]), (), (CsTb,))
                    pt2 = nps()
                    k.tr(pt2[:, 0:128], C0, identf[:, :], (CsTb, identf), (pt2,))
                    k.cp(C0T, pt2[:, 0:128], (pt2,), (CsTb,), e='act')
                    cs = slice(ST * b, ST * b + ST)
                    k.mm(psnum[:, cs], C0T, QpT[:, cs], False, b == SB - 1, (CsTb, QpT), (psnum,))
                    k.mm(psden[0:4, cs], n0sel[:, b, h, :], QpT[:, cs], False, (h == 3 and b == SB - 1), (n0sel, QpT), (psden,))
                    k.ts(Vw[:], VM[0:ntok, h, 0:128], Wm[:, b:b + 1], None, ALU.mult, None, (VM, Wm), (Vw,))
                    psd = nps()
                    k.mm(psd[:, 0:128], Vw[:, :], mk_tok[0:ntok, h * 128:(h + 1) * 128], True, True, (Vw, mk_tok), (psd,))
                    k.stt(C0, C0, decbc[:, h * 16 + b:h * 16 + b + 1], psd[:, 0:128], ALU.mult, ALU.add, (CsTb, decbc, psd), (CsTb,))
                    k.dma('sp', lambda e: e.dma_start(out=O['smlC'][l, b, h], in_=C0), (CsTb,), (), True)
                k.cp(arena_f[:, h * TT:h * TT + ntok], psnum[:, 0:ntok], (psnum,), (arena,), e='act')
            mlstm_finish(ntok)
            k.dma('sp', lambda e: e.dma_start(out=O['smln'][l].rearrange("b h d -> b (h d)"), in_=n0tok), (junkb,), (), True)

        def cross_attn_sample(l):
            qT = cross_q(l, P, ntok)
            for b in range(SB):
                G = GB[b % 2]
                for mc in range(2):
                    k.dma('sp', lambda e: e.dma_start(out=G['gK'], in_=I['cmk'][l, b, mc * 128:(mc + 1) * 128, :]), (), (G['tok'],))
                    k.cp(tmpb[:, 0:512], G['gK'], (G['tok'],), (tmpb,), e='act')
                    transpose_to_T(tmpb, 128, 4, memK, mc * 128, lambda c: tmpb[:, c * 128:(c + 1) * 128])
                    k.dma('sp', lambda e: e.dma_start(out=G['gV'], in_=I['cmv'][l, b, mc * 128:(mc + 1) * 128, :]), (), (G['tok'],))
                    k.cp(memV[:, mc, :], G['gV'], (G['tok'],), (memV,))
                cs = slice(ST * b, ST * b + ST)
                ps_ = nps()
                for h in range(4):
                    for mc in range(2):
                        c0 = (h * 2 + mc) * 4
                        k.mm(ps_[:, c0:c0 + 4], memK[:, h, mc * 128:(mc + 1) * 128], qT[:, h, cs], True, True, (memK, qT), (ps_,))
                k.act(PT2.rearrange("p h q -> p (h q)"), ps_[:, 0:32], AF.Exp, (ps_,), (vbuf[1],), bias=-8.0)
                pso = nps()
                for h in range(4):
                    for mc in range(2):
                        k.mm(pso[:, h * 4:(h + 1) * 4], memV[:, mc, h * 128:(h + 1) * 128], PT2[:, h * 2 + mc, :], mc == 0, mc == 1, (memV, vbuf[1]), (pso,))
                k.cp(arena_f[:, 0:4 * TT].rearrange("p (h t) -> p h t", t=TT)[:, :, cs], pso[:, 0:16].rearrange("p (h q) -> p h q", q=ST), (pso,), (arena,), e='act')
                for h in range(4):
                    for mc in range(2):
                        k.mm(psden[0:4, cs], onesel[:, h, :], PT2[:, h * 2 + mc, :], h == 0 and mc == 0, h == 3 and mc == 1, (onesel, vbuf[1]), (psden,))
            cross_finish(l, P, ntok)

        for l in range(DEPTH):
            bcast_load(gvb['fgq'], W['fox_gq'][l], 64)
            bcast_load(gvb['fgk'], W['fox_gk'][l], 64)
            bcast_load(bfb, W['fox_bf'][l], 8)
            s5_setup(l)
            mlstm_setup(l)
            bcast_load(gb, W['g_mix'][l], D)
            rmsnorm_T(xres, P, 1, gb, hT, tmpb)
            fox_sample(l)
            s5_sample(l)
            mlstm_sample(l)
            merge_out(l, P, ntok)
            cross_attn_sample(l)
            mlp(l, P, ntok)
        k.dma('sp', lambda g: g.dma_start(out=O['ys'][:, :], in_=xres[0:P, 0, :]), (xres,), (), True)
        k.memset(G4['zero'], G4['zero'][:], 0.0); k.memset(G4['one'], G4['one'][:], 1.0)

    def prompt_group():
        for l in range(DEPTH):
            memory_kv(l)
            bcast_load(gvb['fgq'], W['fox_gq'][l], 64)
            bcast_load(gvb['fgk'], W['fox_gk'][l], 64)
            bcast_load(bfb, W['fox_bf'][l], 8)
            k.memset(Fcarry, Fcarry[:], 0.0)
            s5_setup(l)
            mlstm_setup(l)
            mlstm_zero_state()
            for ti in range(NT):
                r0 = ti * TT
                if l == 0:
                    k.dma('sp', lambda g: g.dma_start(out=xres[:, 0, :], in_=I['xp'][r0:r0 + 128, :]), (), (xres,))
                else:
                    k.dma('sp', lambda g: g.dma_start(out=xres[:, 0, :], in_=xscr[r0:r0 + 128, :]), (XSCR,), (xres,))
                bcast_load(gb, W['g_mix'][l], D)
                rmsnorm_T(xres, 128, 1, gb, hT, tmpb)
                fox_prompt_tile(l, ti)
                s5_inproj(l, TT)
                s5_tile_prompt(l, TT)
                mlstm_inproj(l, 128, TT)
                mlstm_prompt_tile(l)
                merge_out(l, 128, TT)
                cross_attn_prompt(l)
                mlp(l, 128, TT)
                if l == 0:
                    k.dma('sp', lambda g: g.dma_start(out=xscr[r0:r0 + 128, :], in_=xres[:, 0, :]), (xres,), (XSCR,))
                else:
                    k.dma('sp', lambda g: g.dma_start(out=O['yp'][r0:r0 + 128, :], in_=xres[:, 0, :]), (xres,), (), True)
            s5_final_state_out(l)
            mlstm_prompt_out(l)

    if full:
        sample_group()
    prompt_group()
    k.finish()
    return nc


_NC_CACHE = {}


def kernel(**inp):
    f = lambda a: np.ascontiguousarray(np.asarray(a))
    if 'nc' not in _NC_CACHE:
        _NC_CACHE['nc'] = build()
    nc = _NC_CACHE['nc']
    wnames = ['g_mix', 'w_in', 's5_a_re', 's5_a_im', 's5_log_step', 's5_b_re', 's5_b_im', 's5_c_re', 's5_c_im', 's5_d',
              's5_w_glu', 's5_b_glu', 'fox_gq', 'fox_gk', 'fox_bf', 'ml_bi', 'ml_bf', 'ml_gn', 'w_br_s5', 'w_br_fox',
              'w_br_ml', 'w_out', 'g_cross', 'w_cq', 'cross_gq', 'g_mem', 'w_mk', 'w_mv', 'cross_gk', 'w_co', 'g_mlp',
              'w_up', 'w_down']
    shared = {n: f(inp[n]) for n in wnames}
    shared['ck'] = f(inp['cache_fox_k']).reshape(DEPTH, NPOOL * PAGE, 512)
    shared['cv'] = f(inp['cache_fox_v']).reshape(DEPTH, NPOOL * PAGE, 512)
    shared['clf'] = f(inp['cache_fox_logf']).reshape(DEPTH, NPOOL * PAGE, 8)
    in_maps = []
    for c in range(NCORE):
        b = c % 4
        sl = slice(SB * c, SB * (c + 1))
        m = dict(shared)
        m['xp'] = f(inp['x_prompt'][b]); m['xs'] = f(inp['x_sample'][sl]).reshape(NS, D); m['memp'] = f(inp['mem_prompt'][b])
        m['pt'] = f(inp['page_table'][sl]).reshape(1, SB * NPAGE).astype(np.int32)
        m['s5re'] = f(inp['state_s5_re'][:, sl]); m['s5im'] = f(inp['state_s5_im'][:, sl])
        m['mlC'] = f(inp['state_mlstm_C'][:, sl]); m['mln'] = f(inp['state_mlstm_n'][:, sl]); m['mlm'] = f(inp['state_mlstm_m'][:, sl])
        m['cmk'] = f(inp['cache_mem_k'][:, sl]).reshape(DEPTH, SB, 256, 512); m['cmv'] = f(inp['cache_mem_v'][:, sl]).reshape(DEPTH, SB, 256, 512)
        in_maps.append(m)
    res = run_bass_kernel_spmd(nc, in_maps, core_ids=list(range(NCORE))).results
    P4 = range(4)
    cat_p = lambda key, shp: np.stack([res[c][key] for c in P4], axis=1).reshape(shp)
    cat_s = lambda key, shp: np.concatenate([res[c][key] for c in range(NCORE)], axis=1).reshape(shp)
    yp = np.stack([res[c]['yp'] for c in P4], axis=0)
    ys = np.concatenate([res[c]['ys'].reshape(SB, ST, D) for c in range(NCORE)], axis=0)
    outs = (
        yp, ys,
        cat_p('pfk', (DEPTH, 4, SEQ, 8, 64)), cat_p('pfv', (DEPTH, 4, SEQ, 8, 64)), cat_p('pflf', (DEPTH, 4, SEQ, 8)),
        cat_p('ps5re', (DEPTH, 4, 32, 64)), cat_p('ps5im', (DEPTH, 4, 32, 64)),
        cat_p('pmlC', (DEPTH, 4, 4, 128, 128)), cat_p('pmln', (DEPTH, 4, 4, 128)), cat_p('pmlm', (DEPTH, 4, 4)),
        cat_p('pmemk', (DEPTH, 4, 256, 4, 128)), cat_p('pmemv', (DEPTH, 4, 256, 4, 128)),
        np.concatenate([res[c]['sfk'].reshape(DEPTH, SB, ST, 8, 64) for c in range(NCORE)], axis=1),
        np.concatenate([res[c]['sfv'].reshape(DEPTH, SB, ST, 8, 64) for c in range(NCORE)], axis=1),
        np.concatenate([res[c]['sflf'].reshape(DEPTH, SB, ST, 8) for c in range(NCORE)], axis=1),
        cat_s('ss5re', (DEPTH, 128, 32, 64)), cat_s('ss5im', (DEPTH, 128, 32, 64)),
        cat_s('smlC', (DEPTH, 128, 4, 128, 128)), cat_s('smln', (DEPTH, 128, 4, 128)), cat_s('smlm', (DEPTH, 128, 4)),
    )
    return tuple(np.ascontiguousarray(o, dtype=np.float32) for o in outs)
```

```python
import numpy as np
import concourse.bass as bass
import concourse.mybir as mybir
from concourse.bass_utils import run_bass_kernel_spmd

F32 = mybir.dt.float32
BF16 = mybir.dt.bfloat16
I32 = mybir.dt.int32
AF = mybir.ActivationFunctionType
ALU = mybir.AluOpType

D = 1024
SEQ = 4096
DEPTH = 2
NCORE = 8
SB = 16
ST = 4
NS = SB * ST
NPAGE = 16
PAGE = 128
NPOOL = 2560
D_IN = 7184
EPS = 1e-6
C_S5, C_FQ, C_FK, C_FV, C_FF, C_MQ, C_MK, C_MV, C_MI, C_MF, C_MO, C_G = (
    0, 512, 1024, 1536, 2048, 2056, 2568, 3080, 3592, 3596, 3600, 4112)


class Buf:
    def __init__(self, t):
        self.t = t
        self.w = []
        self.r = {}

    def __getitem__(self, idx):
        return self.t[idx]


class K:
    def __init__(self, nc):
        self.nc = nc
        self.eng = {'pe': nc.tensor, 'act': nc.scalar, 'dve': nc.vector, 'pool': nc.gpsimd, 'sp': nc.sync}
        self.sem = {e: nc.alloc_semaphore('sem_' + e) for e in self.eng}
        self.cnt = {e: 0 for e in self.eng}
        self.seen = {e: {} for e in self.eng}
        self.semobj = dict(self.sem)
        self.ndma = 40
        self.dsem = [nc.alloc_semaphore('dsem%d' % i) for i in range(self.ndma)]
        for i, s in enumerate(self.dsem):
            self.semobj['d%d' % i] = s
        self.dcnt = [0] * self.ndma
        self.dnext = 0
        self.out_waits = []
        self.nbuf = 0

    def sb(self, shape, dt=F32, name=None):
        self.nbuf += 1
        return Buf(self.nc.alloc_sbuf_tensor(name or ('sb%d' % self.nbuf), list(shape), dt))

    def ps(self, shape, dt=F32, name=None):
        self.nbuf += 1
        return Buf(self.nc.alloc_psum_tensor(name or ('ps%d' % self.nbuf), list(shape), dt))

    def _wait(self, e, key, val):
        if self.seen[e].get(key, 0) >= val:
            return
        self.seen[e][key] = val
        self.eng[e].wait_ge(self.semobj[key], val)

    def _deps(self, e, reads, writes):
        need = {}
        for b in reads:
            for (k, v) in b.w:
                need[k] = max(need.get(k, 0), v)
        for b in writes:
            for (k, v) in b.w:
                need[k] = max(need.get(k, 0), v)
            for k, v in b.r.items():
                need[k] = max(need.get(k, 0), v)
        for k, v in need.items():
            if k == e and e == 'pe':
                continue
            self._wait(e, k, v)

    def _mark(self, key, val, reads, writes):
        for b in reads:
            b.r[key] = max(b.r.get(key, 0), val)
        for b in writes:
            b.w = [(key, val)]
            b.r = {}

    def op(self, e, fn, reads=(), writes=()):
        reads = [b for b in reads if b is not None]
        writes = [b for b in writes if b is not None]
        self._deps(e, reads, writes)
        inst = fn(self.eng[e])
        self.cnt[e] += 1
        inst.then_inc(self.sem[e], 1)
        self._mark(e, self.cnt[e], reads, writes)

    def dma(self, e, fn, reads=(), writes=(), is_output=False):
        reads = [b for b in reads if b is not None]
        writes = [b for b in writes if b is not None]
        self._deps(e, reads, writes)
        i = self.dnext
        self.dnext = (self.dnext + 1) % self.ndma
        key = 'd%d' % i
        if self.dcnt[i] > 0:
            self._wait(e, key, 16 * self.dcnt[i])
        inst = fn(self.eng[e])
        self.dcnt[i] += 1
        inst.then_inc(self.dsem[i], 16)
        self._mark(key, 16 * self.dcnt[i], reads, writes)
        if is_output:
            self.out_waits.append((key, 16 * self.dcnt[i]))

    def finish(self):
        for (k, v) in self.out_waits:
            self._wait('sp', k, v)
        for e in ('pe', 'act', 'dve', 'pool'):
            if self.cnt[e] > 0:
                self._wait('sp', e, self.cnt[e])

    def mm(self, out_ap, lhsT, rhs, start, stop, reads, writes):
        self.op('pe', lambda g: g.matmul(out_ap, lhsT, rhs, start=start, stop=stop), reads, writes)

    def tr(self, out_ap, in_ap, ident_ap, reads, writes):
        self.op('pe', lambda g: g.transpose(out_ap, in_ap, ident_ap), reads, writes)

    def act(self, out_ap, in_ap, func, reads, writes, bias=None, scale=None, accum_out=None, e='act'):
        kw = {}
        if bias is not None:
            kw['bias'] = bias
        if scale is not None:
            kw['scale'] = scale
        if accum_out is not None:
            kw['accum_out'] = accum_out
        self.op('act', lambda g: g.activation(out=out_ap, in_=in_ap, func=func, **kw), reads, writes)

    def tt(self, out_ap, a, b, op, reads, writes, e='dve'):
        self.op(e, lambda g: g.tensor_tensor(out=out_ap, in0=a, in1=b, op=op), reads, writes)

    def ts(self, out_ap, a, s1, s2, op0, op1, reads, writes, e='dve'):
        if op1 is None:
            self.op(e, lambda g: g.tensor_scalar(out=out_ap, in0=a, scalar1=s1, scalar2=None, op0=op0), reads, writes)
        else:
            self.op(e, lambda g: g.tensor_scalar(out=out_ap, in0=a, scalar1=s1, scalar2=s2, op0=op0, op1=op1), reads, writes)

    def stt(self, out_ap, a, s, b, op0, op1, reads, writes):
        self.op('dve', lambda g: g.scalar_tensor_tensor(out=out_ap, in0=a, scalar=s, in1=b, op0=op0, op1=op1), reads, writes)

    def cp(self, out_ap, in_ap, reads, writes, e='dve'):
        if e == 'act':
            self.op('act', lambda g: g.copy(out=out_ap, in_=in_ap), reads, writes)
        else:
            self.op(e, lambda g: g.tensor_copy(out=out_ap, in_=in_ap), reads, writes)

    def memset(self, buf, ap, val, e='pool'):
        self.op(e, lambda g: g.memset(ap, val), (), (buf,))


TT = 128
NT = SEQ // TT


def build(mode='full'):
    import math
    nc = bass.Bass("TRN2", target_bir_lowering=False)
    k = K(nc)
    full = (mode == 'full')

    def din(name, shape, dt=F32):
        return nc.dram_tensor(name, list(shape), dt, kind="ExternalInput").ap()

    def dout(name, shape, dt=F32):
        return nc.dram_tensor(name, list(shape), dt, kind="ExternalOutput").ap()

    I = {}
    I['xp'] = din('xp', [SEQ, D]); I['memp'] = din('memp', [256, D])
    if full:
        I['xs'] = din('xs', [NS, D])
        I['ck'] = din('ck', [DEPTH, NPOOL * PAGE, 512]); I['cv'] = din('cv', [DEPTH, NPOOL * PAGE, 512])
        I['clf'] = din('clf', [DEPTH, NPOOL * PAGE, 8]); I['pt'] = din('pt', [1, SB * NPAGE], I32)
        I['s5re'] = din('s5re', [DEPTH, SB, 32, 64]); I['s5im'] = din('s5im', [DEPTH, SB, 32, 64])
        I['mlC'] = din('mlC', [DEPTH, SB, 4, 128, 128]); I['mln'] = din('mln', [DEPTH, SB, 4, 128])
        I['mlm'] = din('mlm', [DEPTH, SB, 4]); I['cmk'] = din('cmk', [DEPTH, SB, 256, 512]); I['cmv'] = din('cmv', [DEPTH, SB, 256, 512])
    wshapes = dict(g_mix=[DEPTH, D], w_in=[DEPTH, D, D_IN], s5_a_re=[DEPTH, 32, 64], s5_a_im=[DEPTH, 32, 64],
                   s5_log_step=[DEPTH, 32], s5_b_re=[DEPTH, 32, 64, 16], s5_b_im=[DEPTH, 32, 64, 16],
                   s5_c_re=[DEPTH, 32, 16, 64], s5_c_im=[DEPTH, 32, 16, 64], s5_d=[DEPTH, 512],
                   s5_w_glu=[DEPTH, 512, 512], s5_b_glu=[DEPTH, 512], fox_gq=[DEPTH, 64], fox_gk=[DEPTH, 64],
                   fox_bf=[DEPTH, 8], ml_bi=[DEPTH, 4], ml_bf=[DEPTH, 4], ml_gn=[DEPTH, 128],
                   w_br_s5=[DEPTH, 512, D], w_br_fox=[DEPTH, 512, D], w_br_ml=[DEPTH, 512, D], w_out=[DEPTH, D, D],
                   g_cross=[DEPTH, D], w_cq=[DEPTH, D, 512], cross_gq=[DEPTH, 128], g_mem=[DEPTH, D],
                   w_mk=[DEPTH, D, 512], w_mv=[DEPTH, D, 512], cross_gk=[DEPTH, 128], w_co=[DEPTH, 512, D],
                   g_mlp=[DEPTH, D], w_up=[DEPTH, D, 4096], w_down=[DEPTH, 4096, D])
    W = {n: din(n, s) for n, s in wshapes.items()}
    O = {}
    O['yp'] = dout('yp', [SEQ, D])
    O['pfk'] = dout('pfk', [DEPTH, SEQ, 512]); O['pfv'] = dout('pfv', [DEPTH, SEQ, 512]); O['pflf'] = dout('pflf', [DEPTH, SEQ, 8])
    O['ps5re'] = dout('ps5re', [DEPTH, 32, 64]); O['ps5im'] = dout('ps5im', [DEPTH, 32, 64])
    O['pmlC'] = dout('pmlC', [DEPTH, 4, 128, 128]); O['pmln'] = dout('pmln', [DEPTH, 4, 128]); O['pmlm'] = dout('pmlm', [DEPTH, 4])
    O['pmemk'] = dout('pmemk', [DEPTH, 256, 512]); O['pmemv'] = dout('pmemv', [DEPTH, 256, 512])
    if full:
        O['ys'] = dout('ys', [NS, D])
        O['sfk'] = dout('sfk', [DEPTH, NS, 512]); O['sfv'] = dout('sfv', [DEPTH, NS, 512]); O['sflf'] = dout('sflf', [DEPTH, NS, 8])
        O['ss5re'] = dout('ss5re', [DEPTH, SB, 32, 64]); O['ss5im'] = dout('ss5im', [DEPTH, SB, 32, 64])
        O['smlC'] = dout('smlC', [DEPTH, SB, 4, 128, 128]); O['smln'] = dout('smln', [DEPTH, SB, 4, 128]); O['smlm'] = dout('smlm', [DEPTH, SB, 4])
    xscr = nc.dram_tensor('xscr', [SEQ, D], F32, kind="Internal").ap()
    ktscr = nc.dram_tensor('ktscr', [8, 96, SEQ], BF16, kind="Internal").ap()
    vscr = nc.dram_tensor('vscr', [8, 128, SEQ // 128, 65], BF16, kind="Internal").ap()
    XSCR = Buf(None); KSCR = Buf(None); VSCR = Buf(None)

    identf = k.sb([128, 128], F32, 'identf'); k.memset(identf, identf[:], 1.0)
    k.op('pool', lambda g: g.affine_select(out=identf[:], in_=identf[:], pattern=[[-1, 128]], compare_op=ALU.is_equal,
                                           fill=0.0, base=0, channel_multiplier=1), (identf,), (identf,))
    identb = k.sb([128, 128], BF16, 'identb'); k.cp(identb[:], identf[:], (identf,), (identb,))
    trif = k.sb([128, 128], F32, 'trif'); k.memset(trif, trif[:], 1.0)
    k.op('pool', lambda g: g.affine_select(out=trif[:], in_=trif[:], pattern=[[1, 128]], compare_op=ALU.is_ge,
                                           fill=0.0, base=0, channel_multiplier=-1), (trif,), (trif,))
    trib = k.sb([128, 128], BF16, 'trib'); k.cp(trib[:], trif[:], (trif,), (trib,))
    onesf = k.sb([128, 128], F32, 'onesf'); k.memset(onesf, onesf[:], 1.0)
    onesb = k.sb([128, 128], BF16, 'onesb'); k.memset(onesb, onesb[:], 1.0)
    triR = k.sb([128, 128], F32, 'triR')
    k.tt(triR[:], onesf[:], trif[:], ALU.subtract, (onesf, trif), (triR,))

    psb = [k.ps([128, 512], F32, 'psb%d' % i) for i in range(3)]
    pst = [k.ps([128, 1024], BF16, 'pst%d' % i) for i in range(2)]
    ps_acc = [k.ps([128, 512], F32, 'psacc%d' % i) for i in range(2)]
    psden = k.ps([128, 512], F32, 'psden')
    st = {'ps': 0, 'pt': 0, 'w': 0}

    def nps():
        st['ps'] = (st['ps'] + 1) % len(psb)
        return psb[st['ps']]

    def npt():
        st['pt'] = (st['pt'] + 1) % len(pst)
        return pst[st['pt']]

    wbufs = [k.sb([128, 8, 512], BF16, 'wbuf%d' % i) for i in range(3)]

    def nwb():
        st['w'] = (st['w'] + 1) % len(wbufs)
        return wbufs[st['w']]

    wcache = {}

    def _wscratch(key, p, n):
        if key not in wcache:
            scr = nc.dram_tensor('ws%d' % len(wcache), [p, n], BF16, kind="Internal").ap()
            wcache[key] = [scr, Buf(None), False]
        return wcache[key]

    def wload(wap, r0, nr, c0, ncol, p=128):
        wb = nwb()
        kc = nr // p
        ent = _wscratch((wap.tensor.name, wap.offset, r0, nr, c0, ncol, p), p, kc * ncol)
        scr3 = ent[0].rearrange("p (k n) -> p k n", n=ncol)
        if not ent[2]:
            src = wap[r0:r0 + nr, c0:c0 + ncol].rearrange("(kc p) n -> p kc n", p=p)
            k.dma('pool', lambda g: g.dma_start(out=wb[0:p, 0:kc, 0:ncol], in_=src), (), (wb,))
            k.dma('sp', lambda g: g.dma_start(out=scr3, in_=wb[0:p, 0:kc, 0:ncol]), (wb,), (ent[1],))
            ent[2] = True
        else:
            k.dma('pool', lambda g: g.dma_start(out=wb[0:p, 0:kc, 0:ncol], in_=scr3), (ent[1],), (wb,))
        return wb

    def wload96(wap, c0, ncol):
        wb = nwb()
        ent = _wscratch((wap.tensor.name, wap.offset, c0, ncol, '96'), 96, 6 * ncol)
        scr3 = ent[0].rearrange("p (k n) -> p k n", n=ncol)
        if not ent[2]:
            k.dma('pool', lambda g: g.dma_start(out=wb[0:96, 0:5, 0:ncol], in_=wap[0:480, c0:c0 + ncol].rearrange("(c p) n -> p c n", p=96)), (), (wb,))
            k.dma('pool', lambda g: g.dma_start(out=wb[0:32, 5, 0:ncol], in_=wap[480:512, c0:c0 + ncol]), (), (wb,))
            k.dma('sp', lambda g: g.dma_start(out=scr3, in_=wb[0:96, 0:6, 0:ncol]), (wb,), (ent[1],))
            ent[2] = True
        else:
            k.dma('pool', lambda g: g.dma_start(out=wb[0:96, 0:6, 0:ncol], in_=scr3), (ent[1],), (wb,))
        return wb

    def bcast_load(dst, ap_row, n):
        k.dma('sp', lambda g: g.dma_start(out=dst[:, 0:n], in_=ap_row.partition_broadcast(128)), (), (dst,))

    def transpose_to_T(src, P, nchunks, dstT, t0, srcsl):
        for c0 in range(0, nchunks, 8):
            nch = min(8, nchunks - c0)
            pt_ = npt()
            for c in range(nch):
                k.tr(pt_[:, c * 128:c * 128 + P], srcsl(c0 + c), identb[0:P, 0:P], (src, identb), (pt_,))
            k.cp(dstT[:, c0:c0 + nch, t0:t0 + P],
                 pt_[:, 0:nch * 128].rearrange("p (c t) -> p c t", t=128)[:, :, 0:P], (pt_,), (dstT,), e='act')

    def transpose_heads(src, P, dst, t0, srcsl, nrows=64, dst_ap=None, prow=0):
        pt_ = npt()
        for c in range(4):
            k.tr(pt_[:, c * 128:c * 128 + P], srcsl(c), identb[0:P, 0:P], (src, identb), (pt_,))
        v = pt_[:, 0:512].rearrange("p (c t) -> p c t", t=128)
        d_ = dst_ap if dst_ap is not None else dst.t
        k.cp(d_[prow:prow + nrows, 0:8:2, t0:t0 + P], v[0:nrows, :, 0:P], (pt_,), (dst,), e='act')
        k.cp(d_[prow:prow + nrows, 1:8:2, t0:t0 + P], v[64:64 + nrows, :, 0:P], (pt_,), (dst,), e='dve')

    junkb = k.sb([128, D], BF16, 'junkb'); ss1 = k.sb([128, 1], F32, 'ss1')

    def rmsnorm_T(x, P, nsub, gb, hT, tmpb):
        for s in range(nsub):
            k.act(junkb[0:P, :], x[0:P, s, :], AF.Square, (x,), (junkb, ss1), accum_out=ss1[0:P, :])
            k.ts(ss1[0:P, :], ss1[0:P, :], 1.0 / D, EPS, ALU.mult, ALU.add, (ss1,), (ss1,))
            k.act(ss1[0:P, :], ss1[0:P, :], AF.Sqrt, (ss1,), (ss1,))
            k.op('dve', lambda g: g.reciprocal(out=ss1[0:P, :], in_=ss1[0:P, :]), (ss1,), (ss1,))
            k.stt(tmpb[0:P, :], x[0:P, s, :], ss1[0:P, :], gb[0:P, 0:D], ALU.mult, ALU.mult, (x, ss1, gb), (tmpb,))
            transpose_to_T(tmpb, P, 8, hT, s * 128, lambda c: tmpb[0:P, c * 128:(c + 1) * 128])

    def proj_tok(hT, kchunks, P, nsub, wap, r0, c0, ncols, consumer):
        for cb in range(0, ncols, 512):
            nc_ = min(512, ncols - cb)
            wb = wload(wap, r0, kchunks * 128, c0 + cb, nc_)
            for s in range(nsub):
                ps_ = nps()
                for kc in range(kchunks):
                    k.mm(ps_[0:P, 0:nc_], hT[:, kc, s * 128:s * 128 + P], wb[:, kc, 0:nc_], kc == 0, kc == kchunks - 1,
                         (hT, wb), (ps_,))
                consumer(ps_, s, cb, nc_)

    def proj_feat(hT, kchunks, ntok, wap, r0, c0, ncols, consumer, chunk=128):
        for cb in range(0, ncols, 512):
            nc_ = min(512, ncols - cb)
            wb = wload(wap, r0, kchunks * 128, c0 + cb, nc_)
            for j in range(0, nc_, chunk):
                m = min(chunk, nc_ - j)
                ps_ = nps()
                for kc in range(kchunks):
                    k.mm(ps_[0:m, 0:ntok], wb[:, kc, j:j + m], hT[:, kc, 0:ntok], kc == 0, kc == kchunks - 1, (hT, wb), (ps_,))
                consumer(ps_, (cb + j) // chunk, m)

    NSUBX = 1
    xres = k.sb([128, NSUBX, D], F32, 'xres')
    gb = k.sb([128, D], F32, 'gb')
    tmpb = k.sb([128, D], BF16, 'tmpb')
    hT = k.sb([128, 8, 128], BF16, 'hT')
    arena = k.sb([128, 32, TT], BF16, 'arena')
    arena_f = arena.t.bitcast(F32).reshape([128, 16 * TT])
    mrgf = k.sb([128, D], F32, 'mrgf')
    yT = {b: k.sb([128, 4, TT], BF16, 'yT_' + b) for b in ('s5', 'fox', 'ml')}
    memK = k.sb([128, 4, 256], BF16, 'memKT')
    memV = k.sb([128, 2, 512], BF16, 'memV')
    rowt = k.sb([128, 512], F32, 'rowt')
    kout = k.sb([128, 512], F32, 'kout')
    sm8 = k.sb([128, 8], F32, 'sm8')
    pT = k.sb([128, 2, TT], BF16, 'pT')
    den4 = k.sb([4, TT], F32, 'den4')
    ones4 = k.sb([4, 128], F32, 'ones4'); k.memset(ones4, ones4[:], 1.0)
    selh = []
    for h in range(4):
        s_ = k.sb([4, 128], F32, 'selh%d' % h)
        k.op('pool', lambda g, s_=s_, h=h: g.affine_select(out=s_[:], in_=ones4[:], pattern=[[0, 128]], compare_op=ALU.is_equal,
                                                       fill=0.0, base=-h, channel_multiplier=1), (ones4,), (s_,))
        selh.append(s_)
    onesel = k.sb([128, 4, 4], BF16, 'onesel'); k.memset(onesel, onesel[:], 0.0)
    for h in range(4):
        k.memset(onesel, onesel[:, h, h:h + 1], 1.0)
    gvb = {n_: k.sb([128, 128], F32, 'gvb_' + n_) for n_ in ('cgq', 'cgk', 'fgq', 'fgk')}

    def head_rms(ps_, P, nh, hd, gvec_b, out_buf, out_ap, scale):
        n = nh * hd
        k.act(rowt[0:P, 0:n], ps_[0:P, 0:n], AF.Square, (ps_,), (rowt,))
        k.op('dve', lambda g: g.tensor_reduce(out=sm8[0:P, 0:nh], in_=rowt[0:P, 0:n].rearrange("p (h d) -> p h d", d=hd),
                                              axis=mybir.AxisListType.X, op=ALU.add), (rowt,), (sm8,))
        k.ts(sm8[0:P, 0:nh], sm8[0:P, 0:nh], 1.0 / hd, EPS, ALU.mult, ALU.add, (sm8,), (sm8,))
        k.act(sm8[0:P, 0:nh], sm8[0:P, 0:nh], AF.Sqrt, (sm8,), (sm8,))
        k.op('dve', lambda g: g.reciprocal(out=sm8[0:P, 0:nh], in_=sm8[0:P, 0:nh]), (sm8,), (sm8,))
        k.tt(rowt[0:P, 0:n].rearrange("p (h d) -> p h d", d=hd), ps_[0:P, 0:n].rearrange("p (h d) -> p h d", d=hd),
             sm8[0:P, 0:nh].unsqueeze(2).to_broadcast([P, nh, hd]), ALU.mult, (ps_, sm8), (rowt,))
        k.stt(out_ap, rowt[0:P, 0:n].rearrange("p (h d) -> p h d", d=hd), float(scale),
              gvec_b[0:P, 0:hd].unsqueeze(1).to_broadcast([P, nh, hd]), ALU.mult, ALU.mult, (rowt, gvec_b), (out_buf,))

    def memory_kv(l):
        bcast_load(gb, W['g_mem'][l], D)
        bcast_load(gvb['cgk'], W['cross_gk'][l], 128)
        for s in range(2):
            k.dma('sp', lambda g: g.dma_start(out=xres[:, 0, :], in_=I['memp'][s * 128:(s + 1) * 128, :]), (), (xres,))
            rmsnorm_T(xres, 128, 1, gb, hT, tmpb)

            def cons_k(ps_, s_, cb, nc_, s=s):
                head_rms(ps_, 128, 4, 128, gvb['cgk'], kout, kout[:, 0:512].rearrange("p (h d) -> p h d", d=128), 1.0)
                k.dma('sp', lambda g: g.dma_start(out=O['pmemk'][l, s * 128:(s + 1) * 128, :], in_=kout[:, 0:512]), (kout,), (), True)
                k.cp(tmpb[:, 0:512], kout[:, 0:512], (kout,), (tmpb,))
                transpose_to_T(tmpb, 128, 4, memK, s * 128, lambda c: tmpb[:, c * 128:(c + 1) * 128])
            proj_tok(hT, 8, 128, 1, W['w_mk'][l], 0, 0, 512, cons_k)

            def cons_v(ps_, s_, cb, nc_, s=s):
                k.cp(rowt[:, 0:512], ps_[:, 0:512], (ps_,), (rowt,), e='act')
                k.dma('sp', lambda g: g.dma_start(out=O['pmemv'][l, s * 128:(s + 1) * 128, :], in_=rowt[:, 0:512]), (rowt,), (), True)
                k.cp(memV[:, s, :], rowt[:, 0:512], (rowt,), (memV,))
            proj_tok(hT, 8, 128, 1, W['w_mv'][l], 0, 0, 512, cons_v)

    def cross_q(l, P, ntok):
        bcast_load(gb, W['g_cross'][l], D)
        bcast_load(gvb['cgq'], W['cross_gq'][l], 128)
        rmsnorm_T(xres, P, 1, gb, hT, tmpb)
        qT = yT['s5']

        def cons_q(ps_, s, cb, nc_):
            head_rms(ps_, P, 4, 128, gvb['cgq'], tmpb, tmpb[0:P, 0:512].rearrange("p (h d) -> p h d", d=128), 128 ** -0.5)
            transpose_to_T(tmpb, P, 4, qT, 0, lambda c: tmpb[0:P, c * 128:(c + 1) * 128])
        proj_tok(hT, 8, P, 1, W['w_cq'][l], 0, 0, 512, cons_q)
        return qT

    def cross_finish(l, P, ntok):
        ocT = yT['fox']
        k.op('dve', lambda g: g.reciprocal(out=den4[:, 0:ntok], in_=psden[0:4, 0:ntok]), (psden,), (den4,))
        for h in range(4):
            psr = nps()
            k.mm(psr[:, 0:ntok], selh[h][:], den4[:, 0:ntok], True, True, (selh[h], den4), (psr,))
            k.tt(ocT[:, h, 0:ntok], arena_f[:, h * TT:h * TT + ntok], psr[:, 0:ntok], ALU.mult, (arena, psr), (ocT,))

        def cons_o(ps_, s, cb, nc_):
            k.tt(xres[0:P, 0, cb:cb + nc_], xres[0:P, 0, cb:cb + nc_], ps_[0:P, 0:nc_], ALU.add, (xres, ps_), (xres,))
        proj_tok(ocT, 4, P, 1, W['w_co'][l], 0, 0, D, cons_o)

    def cross_attn_prompt(l):
        P, ntok = 128, TT
        qT = cross_q(l, P, ntok)
        for h in range(4):
            for mc in range(2):
                ps_ = nps()
                k.mm(ps_[:, 0:ntok], memK[:, h, mc * 128:(mc + 1) * 128], qT[:, h, 0:ntok], True, True, (memK, qT), (ps_,))
                k.act(pT[:, mc, 0:ntok], ps_[:, 0:ntok], AF.Exp, (ps_,), (pT,), bias=-8.0)
            pso = nps()
            for mc in range(2):
                k.mm(pso[:, 0:ntok], memV[:, mc, h * 128:(h + 1) * 128], pT[:, mc, 0:ntok], mc == 0, mc == 1, (memV, pT), (pso,))
            k.cp(arena_f[:, h * TT:h * TT + ntok], pso[:, 0:ntok], (pso,), (arena,), e='act')
            for mc in range(2):
                k.mm(psden[0:4, 0:ntok], onesel[:, h, :], pT[:, mc, 0:ntok], h == 0 and mc == 0, h == 3 and mc == 1, (onesel, pT), (psden,))
        cross_finish(l, P, ntok)

    def mlp(l, P, ntok):
        bcast_load(gb, W['g_mlp'][l], D)
        rmsnorm_T(xres, P, 1, gb, hT, tmpb)

        def cons_up(ps_, j, m):
            k.act(rowt[:, 0:ntok], ps_[:, 0:ntok], AF.Relu, (ps_,), (rowt,))
            k.tt(arena[:, j, 0:ntok], rowt[:, 0:ntok], rowt[:, 0:ntok], ALU.mult, (rowt,), (arena,))
        proj_feat(hT, 8, ntok, W['w_up'][l], 0, 0, 4096, cons_up)
        for cb in range(2):
            ps_ = ps_acc[cb]
            for kg in range(4):
                wb = wload(W['w_down'][l], kg * 1024, 1024, cb * 512, 512)
                for kc in range(8):
                    k.mm(ps_[0:P, :], arena[:, kg * 8 + kc, 0:P], wb[:, kc, :], kg == 0 and kc == 0, kg == 3 and kc == 7, (arena, wb), (ps_,))
            k.tt(xres[0:P, 0, cb * 512:(cb + 1) * 512], xres[0:P, 0, cb * 512:(cb + 1) * 512], ps_[0:P, :], ALU.add, (xres, ps_), (xres,))

    S5T = Buf(None)
    def raw(name, shape, dt=F32):
        return nc.alloc_sbuf_tensor('s5_' + name, list(shape), dt)
    are = raw('are', [128, 16]); aim = raw('aim', [128, 16]); lsb = raw('lsb', [128, 16]); dtt = raw('dtt', [128, 16])
    mag = raw('mag', [128, 16]); cs_c = raw('cs_c', [128, 16]); cs_s = raw('cs_s', [128, 16])
    u1 = raw('u1', [128, 16]); u2 = raw('u2', [128, 16]); u3 = raw('u3', [128, 16]); u4 = raw('u4', [128, 16])
    Bre = raw('Bre', [128, 16, 16]); Bim = raw('Bim', [128, 16, 16]); Cre = raw('Cre', [128, 16, 16]); Cim = raw('Cim', [128, 16, 16])
    Bbr = raw('Bbr', [128, 16, 16]); Bbi = raw('Bbi', [128, 16, 16])
    pwr = raw('pwr', [128, 9, 16]); pwi = raw('pwi', [128, 9, 16])
    V1 = raw('V1', [128, 16, 16]); V2 = raw('V2', [128, 16, 16]); V3 = raw('V3', [128, 16, 16]); V4 = raw('V4', [128, 16, 16])
    QBD = raw('QBD', [128, 2, 16, 9, 32], BF16); BBD = raw('BBD', [128, 2, 16, 32], BF16)
    BDt = raw('BDt', [128, 96], BF16)
    W1 = raw('W1', [96, 8, 2, 6, 128], BF16); Kt = raw('Kt', [96, 6, 8, 32], BF16)
    NJ = TT // 8
    MUFr = raw('MUFr', [128, 16, NJ]); MUFi = raw('MUFi', [128, 16, NJ]); MUIr = raw('MUIr', [128, 16, NJ]); MUIi = raw('MUIi', [128, 16, NJ])
    M1 = raw('M1', [128, 16, NJ]); M2 = raw('M2', [128, 16, NJ]); M3 = raw('M3', [128, 16, NJ]); M4 = raw('M4', [128, 16, NJ])
    rmask = raw('rmask', [128, 16, NJ]); dcol = raw('dcol', [96, 6]); bgcol = raw('bgcol', [96, 6])
    Xr = raw('Xr', [128, 16, NJ + 1]); Xi = raw('Xi', [128, 16, NJ + 1])
    Xbr = raw('Xbr', [128, 16, NJ + 1], BF16); Xbi = raw('Xbi', [128, 16, NJ + 1], BF16)
    uT = k.sb([96, 6, TT], BF16, 'uT'); zt = k.sb([96, TT], F32, 'zt'); ygT = k.sb([96, 6, TT], BF16, 'ygT')
    ysb = k.sb([16, 8, 3, 32], BF16, 'ysb')
    ys5T = k.sb([96, 6, TT], BF16, 'ys5T')
    for t_ in (QBD, BBD, BDt):
        k.op('pool', lambda g, t_=t_: g.memset(t_[:], 0.0), (), (S5T,))
    k.op('pool', lambda g: g.memset(rmask[:], 1.0), (), (S5T,))
    k.op('pool', lambda g: g.memset(rmask[:, :, 0:1], 0.0), (), (S5T,))

    def sop(fn, e='dve'):
        k.op(e, fn, (S5T,), (S5T,))

    def s_tt(o, a, b_, op, e='dve'):
        sop(lambda g: g.tensor_tensor(out=o, in0=a, in1=b_, op=op), e)

    def s_cmul(o_re, o_im, a_re, a_im, b_re, b_im, t1, t2):
        s_tt(t1, a_re, b_re, ALU.mult); s_tt(t2, a_im, b_im, ALU.mult)
        s_tt(t1, t1, t2, ALU.subtract)
        s_tt(t2, a_re, b_im, ALU.mult); s_tt(o_im, a_im, b_re, ALU.mult)
        s_tt(o_im, o_im, t2, ALU.add)
        sop(lambda g: g.tensor_copy(out=o_re, in_=t1))

    def s5_setup(l):
        def ld(dst, src, eng='sp'):
            k.dma(eng, lambda g: g.dma_start(out=dst, in_=src, allow_slow_non_contiguous=True), (S5T,), (S5T,))
        ld(are[:], W['s5_a_re'][l].rearrange("(q r) p -> (r p) q", r=2))
        ld(aim[:], W['s5_a_im'][l].rearrange("(q r) p -> (r p) q", r=2))
        lsv = W['s5_log_step'][l].rearrange("(q r) -> r q", r=2)
        for r in range(2):
            ld(lsb[r * 64:(r + 1) * 64, :], lsv[r].partition_broadcast(64))
        ld(Bre[:], W['s5_b_re'][l].rearrange("(q r) p c -> (r p) q c", r=2))
        ld(Bim[:], W['s5_b_im'][l].rearrange("(q r) p c -> (r p) q c", r=2))
        cstb = nwb()
        cstg = carve(cstb, 0, [128, 4, 64], F32)
        for (srcw, Cx) in ((W['s5_c_re'][l], Cre), (W['s5_c_im'][l], Cim)):
            k.dma('sp', lambda g: g.dma_start(out=cstg, in_=srcw.rearrange("(t g) c p -> (g c) t p", t=4)), (), (cstb,))
            pt_ = nps()
            for t_ in range(4):
                k.tr(pt_[0:64, t_ * 128:(t_ + 1) * 128], cstg[:, t_, :], identf[:, :], (cstb, identf), (pt_,))
            tv = pt_[0:64, 0:512].rearrange("p (q r c) -> p q r c", r=2, c=16)
            k.op('act', lambda g: g.copy(out=Cx[0:64, :, :], in_=tv[:, :, 0, :]), (pt_, S5T), (S5T,))
            k.op('dve', lambda g: g.tensor_copy(out=Cx[64:128, :, :], in_=tv[:, :, 1, :]), (pt_, S5T), (S5T,))
        for (dst, src) in ((dcol, W['s5_d'][l]), (bgcol, W['s5_b_glu'][l])):
            ld(dst[:, 0:5], src[0:480].rearrange("(c p) -> p c", p=96))
            ld(dst[0:32, 5:6], src[480:512].rearrange("(c p) -> p c", p=32))
        sop(lambda g: g.activation(out=dtt[:], in_=lsb[:], func=AF.Exp), 'act')
        s_tt(u1[:], are[:], dtt[:], ALU.mult)
        sop(lambda g: g.activation(out=mag[:], in_=u1[:], func=AF.Exp), 'act')
        s_tt(u1[:], aim[:], dtt[:], ALU.mult)
        sop(lambda g: g.activation(out=cs_s[:], in_=u1[:], func=AF.Sin, scale=1.0 / 16), 'act')
        sop(lambda g: g.tensor_scalar(out=u2[:], in0=u1[:], scalar1=1.0 / 16, scalar2=math.pi / 2, op0=ALU.mult, op1=ALU.add))
        sop(lambda g: g.activation(out=cs_c[:], in_=u2[:], func=AF.Sin), 'act')
        for _ in range(4):
            s_tt(u2[:], cs_c[:], cs_c[:], ALU.mult); s_tt(u3[:], cs_s[:], cs_s[:], ALU.mult)
            s_tt(u4[:], cs_c[:], cs_s[:], ALU.mult)
            s_tt(cs_c[:], u2[:], u3[:], ALU.subtract)
            sop(lambda g: g.tensor_scalar(out=cs_s[:], in0=u4[:], scalar1=2.0, scalar2=None, op0=ALU.mult))
        sop(lambda g: g.memset(pwr[:, 0, :], 1.0), 'pool'); sop(lambda g: g.memset(pwi[:, 0, :], 0.0), 'pool')
        s_tt(pwr[:, 1, :], mag[:], cs_c[:], ALU.mult); s_tt(pwi[:, 1, :], mag[:], cs_s[:], ALU.mult)
        for kk in range(2, 9):
            s_cmul(pwr[:, kk, :], pwi[:, kk, :], pwr[:, kk - 1, :], pwi[:, kk - 1, :], pwr[:, 1, :], pwi[:, 1, :], u1[:], u2[:])
        sop(lambda g: g.tensor_scalar(out=u1[:], in0=pwr[:, 1, :], scalar1=-1.0, scalar2=None, op0=ALU.add))
        s_tt(u2[:], are[:], are[:], ALU.mult); s_tt(u3[:], aim[:], aim[:], ALU.mult); s_tt(u2[:], u2[:], u3[:], ALU.add)
        sop(lambda g: g.reciprocal(out=u2[:], in_=u2[:]))
        s_tt(u3[:], u1[:], are[:], ALU.mult); s_tt(u4[:], pwi[:, 1, :], aim[:], ALU.mult); s_tt(u3[:], u3[:], u4[:], ALU.add)
        s_tt(u3[:], u3[:], u2[:], ALU.mult)
        s_tt(u4[:], pwi[:, 1, :], are[:], ALU.mult); s_tt(u1[:], u1[:], aim[:], ALU.mult); s_tt(u4[:], u4[:], u1[:], ALU.subtract)
        s_tt(u4[:], u4[:], u2[:], ALU.mult)
        bc = lambda a: a.unsqueeze(2).to_broadcast([128, 16, 16])
        s_cmul(Bbr[:], Bbi[:], bc(u3[:]), bc(u4[:]), Bre[:], Bim[:], V1[:], V2[:])
        for r in range(2):
            sop(lambda g, r=r: g.tensor_copy(out=BBD[r * 64:(r + 1) * 64, 0, :, r * 16:(r + 1) * 16], in_=Bbr[r * 64:(r + 1) * 64, :, :]))
            sop(lambda g, r=r: g.tensor_copy(out=BBD[r * 64:(r + 1) * 64, 1, :, r * 16:(r + 1) * 16], in_=Bbi[r * 64:(r + 1) * 64, :, :]))
        for kk in range(9):
            s_cmul(V3[:], V4[:], Cre[:], Cim[:], bc(pwr[:, kk, :]), bc(pwi[:, kk, :]), V1[:], V2[:])
            for r in range(2):
                sop(lambda g, r=r, kk=kk: g.tensor_copy(out=QBD[r * 64:(r + 1) * 64, 0, :, kk, r * 16:(r + 1) * 16], in_=V3[r * 64:(r + 1) * 64, :, :]))
                sop(lambda g, r=r, kk=kk: g.tensor_scalar(out=QBD[r * 64:(r + 1) * 64, 1, :, kk, r * 16:(r + 1) * 16], in0=V4[r * 64:(r + 1) * 64, :, :],
                                                          scalar1=-1.0, scalar2=None, op0=ALU.mult))
        for kk in range(8):
            s_cmul(V3[:], V4[:], Bbr[:], Bbi[:], bc(pwr[:, kk, :]), bc(pwi[:, kk, :]), V1[:], V2[:])
            for ri, Vx in enumerate((V3, V4)):
                for tq in range(6):
                    npair = 3 if tq < 5 else 1
                    bdv = BDt[:, :].rearrange("p (a r c) -> p a r c", r=2, c=16)
                    for r in range(2):
                        sop(lambda g, r=r, tq=tq, npair=npair, Vx=Vx: g.tensor_copy(out=bdv[r * 64:(r + 1) * 64, 0:npair, r, :],
                                                                               in_=Vx[r * 64:(r + 1) * 64, tq * 3:tq * 3 + npair, :]))
                    pt_ = npt()
                    k.op('pe', lambda g, pt_=pt_, npair=npair: g.transpose(pt_[0:32 * npair, 0:128], BDt[:, 0:32 * npair], identb[:, :]), (S5T, identb), (pt_,))
                    k.op('act', lambda g, pt_=pt_, npair=npair, kk=kk, ri=ri, tq=tq: g.copy(out=W1[0:32 * npair, kk, ri, tq, :], in_=pt_[0:32 * npair, 0:128]), (pt_, S5T), (S5T,))
        for q in range(16):
            po, tq = 32 * (q % 3), q // 3
            ps_ = nps()
            for tau in range(8):
                k.op('pe', lambda g, tau=tau, q=q, ps_=ps_, po=po: g.matmul(ps_[po:po + 32, tau * 32:(tau + 1) * 32], BBD[:, 0, q, :], QBD[:, 0, q, tau, :], start=True, stop=False), (S5T,), (ps_,))
                k.op('pe', lambda g, tau=tau, q=q, ps_=ps_, po=po: g.matmul(ps_[po:po + 32, tau * 32:(tau + 1) * 32], BBD[:, 1, q, :], QBD[:, 1, q, tau, :], start=False, stop=True), (S5T,), (ps_,))
            k.op('act', lambda g, ps_=ps_, po=po, tq=tq: g.copy(out=Kt[po:po + 32, tq, :, :], in_=ps_[po:po + 32, 0:256].rearrange("p (a b) -> p a b", b=32)), (ps_, S5T), (S5T,))
        sop(lambda g: g.tensor_copy(out=MUFr[:, :, 0], in_=pwr[:, 8, :])); sop(lambda g: g.tensor_copy(out=MUFi[:, :, 0], in_=pwi[:, 8, :]))
        for j in range(1, NJ):
            s_cmul(MUFr[:, :, j], MUFi[:, :, j], MUFr[:, :, j - 1], MUFi[:, :, j - 1], pwr[:, 8, :], pwi[:, 8, :], u1[:], u2[:])
        s_tt(M1[:], MUFr[:], MUFr[:], ALU.mult); s_tt(M2[:], MUFi[:], MUFi[:], ALU.mult); s_tt(M1[:], M1[:], M2[:], ALU.add)
        sop(lambda g: g.reciprocal(out=M1[:], in_=M1[:]))
        s_tt(MUIr[:], MUFr[:], M1[:], ALU.mult); s_tt(MUIi[:], MUFi[:], M1[:], ALU.mult)
        sop(lambda g: g.tensor_scalar(out=MUIi[:], in0=MUIi[:], scalar1=-1.0, scalar2=None, op0=ALU.mult))
        sop(lambda g: g.memset(Xr[:], 0.0), 'pool'); sop(lambda g: g.memset(Xi[:], 0.0), 'pool')

    def s5_tile_prompt(l, ntok):
        L, nj = 8, ntok // 8
        sop(lambda g: g.tensor_copy(out=Xr[:, :, 0], in_=Xr[:, :, nj])); sop(lambda g: g.tensor_copy(out=Xi[:, :, 0], in_=Xi[:, :, nj]))
        psS = [nps(), nps()]
        for q in range(16):
            po, tq = 32 * (q % 3), q // 3
            for ri in range(2):
                for sp in range(L):
                    k.op('pe', lambda g, q=q, ri=ri, sp=sp, po=po, tq=tq: g.matmul(psS[ri][:, q * nj:(q + 1) * nj], W1[po:po + 32, L - 1 - sp, ri, tq, :],
                                                                                uT[po:po + 32, tq, sp:ntok:L], start=(sp == 0), stop=(sp == L - 1)), (S5T, uT), (psS[ri],))
        Sr = psS[0][:, 0:16 * nj].rearrange("p (q j) -> p q j", j=nj); Si = psS[1][:, 0:16 * nj].rearrange("p (q j) -> p q j", j=nj)
        rd = (S5T, psS[0], psS[1])
        def mop(fn):
            k.op('dve', fn, rd, (S5T,))
        mop(lambda g: g.tensor_tensor(out=M1[:], in0=Sr, in1=MUIr[:], op=ALU.mult)); mop(lambda g: g.tensor_tensor(out=M2[:], in0=Si, in1=MUIi[:], op=ALU.mult))
        mop(lambda g: g.tensor_tensor(out=M1[:], in0=M1[:], in1=M2[:], op=ALU.subtract))
        mop(lambda g: g.tensor_tensor(out=M2[:], in0=Sr, in1=MUIi[:], op=ALU.mult)); mop(lambda g: g.tensor_tensor(out=M3[:], in0=Si, in1=MUIr[:], op=ALU.mult))
        mop(lambda g: g.tensor_tensor(out=M2[:], in0=M2[:], in1=M3[:], op=ALU.add))
        fl = lambda a: a[:, :, :].rearrange("p q j -> p (q j)")
        mop(lambda g: g.tensor_tensor_scan(out=fl(M3), data0=fl(rmask), data1=fl(M1), initial=0.0, op0=ALU.mult, op1=ALU.add))
        mop(lambda g: g.tensor_tensor_scan(out=fl(M4), data0=fl(rmask), data1=fl(M2), initial=0.0, op0=ALU.mult, op1=ALU.add))
        mop(lambda g: g.tensor_tensor(out=M3[:], in0=M3[:], in1=Xr[:, :, 0:1].to_broadcast([128, 16, nj]), op=ALU.add))
        mop(lambda g: g.tensor_tensor(out=M4[:], in0=M4[:], in1=Xi[:, :, 0:1].to_broadcast([128, 16, nj]), op=ALU.add))
        s_cmul(Xr[:, :, 1:nj + 1], Xi[:, :, 1:nj + 1], M3[:], M4[:], MUFr[:], MUFi[:], M1[:], M2[:])
        sop(lambda g: g.tensor_copy(out=Xbr[:], in_=Xr[:])); sop(lambda g: g.tensor_copy(out=Xbi[:], in_=Xi[:]))
        s5_outputs(l, ntok, L, nj, Xbr, Xbi, lambda q: (slice(0, nj)))

    def s5_outputs(l, ntok, L, nj, Xb_r, Xb_i, xsl, xtok=None):
        xrd = (S5T,) if xtok is None else (S5T, xtok)
        W_ = L * 32
        for tq in range(6):
            npair = 3 if tq < 5 else 1
            nr = 32 * npair
            psY = [nps(), nps()]
            for qq in range(npair):
                q = tq * 3 + qq
                po = 32 * qq
                bank = psY[qq // 2]
                c0 = (qq % 2) * 256
                k.op('pe', lambda g: g.matmul(bank[0:nj, c0:c0 + W_], Xb_r[:, q, 0:nj], QBD[:, 0, q, 1:1 + L, :].rearrange("p r c -> p (r c)"), start=True, stop=False), xrd, (bank,))
                k.op('pe', lambda g: g.matmul(bank[0:nj, c0:c0 + W_], Xb_i[:, q, 0:nj], QBD[:, 1, q, 1:1 + L, :].rearrange("p r c -> p (r c)"), start=False, stop=False), xrd, (bank,))
                for sp in range(L):
                    k.op('pe', lambda g: g.matmul(bank[0:nj, c0 + sp * 32:c0 + W_], uT[po:po + 32, tq, sp:ntok:L], Kt[po:po + 32, tq, 0:L - sp, :].rearrange("p a b -> p (a b)"),
                                                  start=False, stop=(sp == L - 1)), (S5T, uT), (bank,))
                k.cp(ysb[0:nj, 0:L, qq, :], bank[0:nj, c0:c0 + W_].rearrange("p (r c) -> p r c", c=32), (bank,), (ysb,), e='act')
            psy = npt()
            for r in range(L):
                k.tr(psy[0:nr, r * nj:(r + 1) * nj], ysb[0:nj, r, 0:npair, :].rearrange("p a c -> p (a c)"), identb[0:nj, 0:nj], (ysb, identb), (psy,))
            k.op('dve', lambda g, tq=tq, nr=nr, psy=psy: g.scalar_tensor_tensor(
                out=zt[0:nr, 0:ntok].rearrange("p (j r) -> p r j", r=L), in0=uT[0:nr, tq, 0:ntok].rearrange("p (j r) -> p r j", r=L),
                scalar=dcol[0:nr, tq:tq + 1], in1=psy[0:nr, 0:ntok].rearrange("p (r j) -> p r j", j=nj), op0=ALU.mult, op1=ALU.add), (uT, S5T, psy), (zt,))
            k.act(ygT[0:nr, tq, 0:ntok], zt[0:nr, 0:ntok], AF.Gelu, (zt,), (ygT,))
        wgl = wload96(W['s5_w_glu'][l], 0, 512)
        for to in range(6):
            nro = 96 if to < 5 else 32
            psg = nps()
            for ti_ in range(6):
                nri = 96 if ti_ < 5 else 32
                k.op('pe', lambda g, to=to, ti_=ti_, nro=nro, nri=nri, psg=psg: g.matmul(psg[0:nro, 0:ntok], wgl[0:nri, ti_, to * 96:to * 96 + nro], ygT[0:nri, ti_, 0:ntok],
                                                                                   start=(ti_ == 0), stop=(ti_ == 5)), (wgl, ygT), (psg,))
            k.op('act', lambda g, to=to, nro=nro, psg=psg: g.activation(out=zt[0:nro, 0:ntok], in_=psg[0:nro, 0:ntok], func=AF.Sigmoid, bias=bgcol[0:nro, to:to + 1]), (psg, S5T), (zt,))
            k.tt(ys5T[0:nro, to, 0:ntok], ygT[0:nro, to, 0:ntok], zt[0:nro, 0:ntok], ALU.mult, (ygT, zt), (ys5T,))

    def s5_final_state_out(l):
        k.dma('sp', lambda g: g.dma_start(out=O['ps5re'][l].rearrange("(q r) p -> (r p) q", r=2), in_=Xr[:, :, NJ], allow_slow_non_contiguous=True), (S5T,), (), True)
        k.dma('sp', lambda g: g.dma_start(out=O['ps5im'][l].rearrange("(q r) p -> (r p) q", r=2), in_=Xi[:, :, NJ], allow_slow_non_contiguous=True), (S5T,), (), True)

    NB = SEQ // 128
    Fcarry = k.sb([128, 8], F32, 'Fcarry')
    QTh = k.sb([96, 8, TT], BF16, 'QTh')
    AQ = k.sb([128, 512], BF16, 'AQ'); k.memset(AQ, AQ[:], 0.0)
    k.memset(AQ, AQ[:, :].rearrange("p (h c) -> p h c", c=64)[:, :, 2:4], 1.0)
    AQTh = k.sb([32, 8, NS], BF16, 'AQTh')
    kst = k.sb([96, 8, TT], BF16, 'kst')
    AK = k.sb([128, 512], BF16, 'AK'); k.memset(AK, AK[:], 0.0); k.memset(AK, AK[:, :].rearrange("p (h c) -> p h c", c=64)[:, :, 0:2], 1.0)
    vst = k.sb([128, 8, 65], BF16, 'vst'); k.memset(vst, vst[:], 1.0)
    kbuf = [k.sb([128, SEQ], BF16, 'kbuf%d' % i) for i in range(2)]
    vbuf = [k.sb([128, NB, 65], BF16, 'vbuf%d' % i) for i in range(2)]
    bfb = k.sb([128, 8], F32, 'bfb')
    lf8 = k.sb([128, 8], F32, 'lf8'); F8 = k.sb([128, 8], F32, 'F8'); t8 = k.sb([128, 8], F32, 't8')
    pTf = k.sb([128, 2, 4 * TT], BF16, 'pTf')
    oTf = k.sb([65, TT], F32, 'oTf')
    recf = k.sb([64, TT], F32, 'recf')
    yfx = k.sb([64, 8, TT], BF16, 'yfx')
    sel65 = k.sb([65, 64], F32, 'sel65'); k.memset(sel65, sel65[:], 0.0); k.memset(sel65, sel65[64:65, :], 1.0)

    def fox_gates(ps_, P, blk_negF, out_lf_ap):
        k.tt(t8[0:P, :], ps_[0:P, 0:8], bfb[0:P, :], ALU.add, (ps_, bfb), (t8,))
        k.act(t8[0:P, :], t8[0:P, :], AF.Exp, (t8,), (t8,), scale=-1.0)
        k.act(t8[0:P, :], t8[0:P, :], AF.Ln, (t8,), (t8,), bias=1.0)
        k.ts(lf8[0:P, :], t8[0:P, :], -1.0, None, ALU.mult, None, (t8,), (lf8,))
        k.dma('sp', lambda g: g.dma_start(out=out_lf_ap, in_=lf8[0:P, :]), (lf8,), (), True)

    def fox_prompt_tile(l, ti):
        P, ntok = 128, TT
        t0 = ti * TT
        Win = W['w_in'][l]

        def cons_ff(ps_, s, cb, nc_):
            fox_gates(ps_, P, None, O['pflf'][l, t0:t0 + P, :])
            p1 = nps()
            k.mm(p1[:, 0:8], trif[:], lf8[:], True, True, (trif, lf8), (p1,))
            k.tt(F8[:], p1[:, 0:8], Fcarry[:], ALU.add, (p1, Fcarry), (F8,))
            p2 = nps()
            k.mm(p2[:, 0:8], onesf[:], lf8[:], True, True, (onesf, lf8), (p2,))
            k.tt(Fcarry[:], Fcarry[:], p2[:, 0:8], ALU.add, (Fcarry, p2), (Fcarry,))
            aqv = AQ[:, :].rearrange("p (h c) -> p h c", c=64)
            akv = AK[:, :].rearrange("p (h c) -> p h c", c=64)
            k.cp(aqv[:, :, 0], F8[:], (F8,), (AQ,))
            k.cp(t8[:], aqv[:, :, 0], (AQ,), (t8,))
            k.tt(aqv[:, :, 1], F8[:], t8[:], ALU.subtract, (F8, t8), (AQ,))
            k.ts(akv[:, :, 2], aqv[:, :, 0], -1.0, None, ALU.mult, None, (AQ,), (AK,))
            k.ts(akv[:, :, 3], aqv[:, :, 1], -1.0, None, ALU.mult, None, (AQ,), (AK,))
            transpose_heads(AQ, P, QTh, 0, lambda c: AQ[0:P, c * 128:(c + 1) * 128], nrows=32, prow=64)
            transpose_heads(AK, P, kst, 0, lambda c: AK[0:P, c * 128:(c + 1) * 128], nrows=32, prow=64)
        proj_tok(hT, 8, P, 1, Win, 0, C_FF, 8, cons_ff)

        def cons_fq(ps_, s, cb, nc_):
            head_rms(ps_, P, 8, 64, gvb['fgq'], tmpb, tmpb[0:P, 0:512].rearrange("p (h d) -> p h d", d=64), 0.125)
            transpose_heads(tmpb, P, QTh, 0, lambda c: tmpb[0:P, c * 128:(c + 1) * 128])
        proj_tok(hT, 8, P, 1, Win, 0, C_FQ, 512, cons_fq)

        def cons_fk(ps_, s, cb, nc_):
            head_rms(ps_, P, 8, 64, gvb['fgk'], kout, kout[0:P, 0:512].rearrange("p (h d) -> p h d", d=64), 1.0)
            k.dma('sp', lambda g: g.dma_start(out=O['pfk'][l, t0:t0 + P, :], in_=kout[0:P, 0:512]), (kout,), (), True)
            k.cp(tmpb[0:P, 0:512], kout[0:P, 0:512], (kout,), (tmpb,))
            transpose_heads(tmpb, P, kst, 0, lambda c: tmpb[0:P, c * 128:(c + 1) * 128])
            k.dma('sp', lambda g: g.dma_start(out=ktscr.rearrange("h d t -> d h t")[:, :, t0:t0 + P], in_=kst[:, :, 0:P]), (kst,), (KSCR,))
        proj_tok(hT, 8, P, 1, Win, 0, C_FK, 512, cons_fk)

        def cons_fv(ps_, s, cb, nc_):
            k.cp(rowt[0:P, 0:512], ps_[0:P, 0:512], (ps_,), (rowt,), e='act')
            k.dma('sp', lambda g: g.dma_start(out=O['pfv'][l, t0:t0 + P, :], in_=rowt[0:P, 0:512]), (rowt,), (), True)
            k.cp(vst[:, :, 0:64], rowt[:, 0:512].rearrange("p (h d) -> p h d", d=64), (rowt,), (vst,))
            k.dma('sp', lambda g: g.dma_start(out=vscr.rearrange("h p b c -> p h b c")[:, :, ti, :], in_=vst[:, :, :]), (vst,), (VSCR,))
        proj_tok(hT, 8, P, 1, Win, 0, C_FV, 512, cons_fv)

        nkb = ti + 1
        for h in range(8):
            kb_, vb_ = kbuf[h % 2], vbuf[h % 2]
            k.dma('sp', lambda g: g.dma_start(out=kb_[0:96, 0:nkb * 128], in_=ktscr[h, :, 0:nkb * 128]), (KSCR,), (kb_,))
            k.dma('sp', lambda g: g.dma_start(out=vb_[:, 0:nkb, :], in_=vscr[h, :, 0:nkb, :]), (VSCR,), (vb_,))
            pso = ps_acc[h % 2]
            for g0 in range(0, nkb, 4):
                ng = min(4, nkb - g0)
                ps_ = nps()
                par = (g0 // 4) % 2
                for i in range(ng):
                    kb = g0 + i
                    k.mm(ps_[:, i * ntok:(i + 1) * ntok], kb_[0:96, kb * 128:(kb + 1) * 128], QTh[0:96, h, 0:ntok], True, True, (kb_, QTh), (ps_,))
                k.act(pTf[:, par, 0:ng * ntok], ps_[:, 0:ng * ntok], AF.Exp, (ps_,), (pTf,), bias=-8.0)
                if g0 + ng == nkb:
                    i = ng - 1
                    k.tt(pTf[:, par, i * ntok:(i + 1) * ntok], pTf[:, par, i * ntok:(i + 1) * ntok], trib[:, 0:ntok], ALU.mult, (pTf, trib), (pTf,))
                for i in range(ng):
                    kb = g0 + i
                    k.mm(pso[0:65, 0:ntok], vb_[:, kb, :], pTf[:, par, i * ntok:(i + 1) * ntok], kb == 0, kb == nkb - 1, (vb_, pTf), (pso,))
            fox_finish_head(h, pso, ntok)

    def fox_finish_head(h, pso, ntok):
        if pso is not None:
            k.cp(oTf[:, 0:ntok], pso[0:65, 0:ntok], (pso,), (oTf,), e='act')
        psr = nps()
        k.mm(psr[0:64, 0:ntok], sel65[:], oTf[:, 0:ntok], True, True, (sel65, oTf), (psr,))
        k.op('dve', lambda g: g.reciprocal(out=recf[:, 0:ntok], in_=psr[0:64, 0:ntok]), (psr,), (recf,))
        k.tt(yfx[:, h, 0:ntok], oTf[0:64, 0:ntok], recf[:, 0:ntok], ALU.mult, (oTf, recf), (yfx,))

    Cst = k.sb([128, 4, 129], F32, 'Cst'); CsTb = k.sb([128, 4, 128], BF16, 'CsTb'); nselb = k.sb([128, 4, 4], BF16, 'nselb')
    Fm = k.sb([4, 1], F32, 'Fm'); Mm = k.sb([4, 1], F32, 'Mm'); Mpe_bc = k.sb([128, 4], F32, 'Mpe_bc'); Me_bc = k.sb([128, 4], F32, 'Me_bc')
    G4 = {n_: k.sb([4, TT], F32, 'g4_' + n_) for n_ in ('mi', 'mf', 'lf', 'F', 'a', 'M', 'negM', 'emt', 'zero', 'one', 'rden')}
    k.memset(G4['zero'], G4['zero'][:], 0.0); k.memset(G4['one'], G4['one'][:], 1.0)
    negbf = k.sb([4, 1], F32, 'negbf'); bicol = k.sb([4, 1], F32, 'bicol'); gncol = k.sb([128, 1], F32, 'gncol')
    mqT = k.sb([128, 4, TT], BF16, 'mqT'); mkT = k.sb([128, 4, TT], BF16, 'mkT'); so4 = k.sb([128, 4, TT], BF16, 'so4')
    mk_tok = k.sb([128, 512], BF16, 'mk_tok'); VM = k.sb([128, 4, 129], BF16, 'VM'); k.memset(VM, VM[:], 1.0)
    atok = k.sb([128, 4], F32, 'atok'); diag4 = k.sb([4, 4], F32, 'diag4'); dec_bc = k.sb([128, 4], F32, 'dec_bc'); wend = k.sb([128, 4], F32, 'wend')
    iwb = k.sb([128, TT], F32, 'iwb'); QpT = k.sb([128, TT], BF16, 'QpT'); Et = k.sb([128, TT], F32, 'Et'); SWb = k.sb([128, TT], BF16, 'SWb')
    Kw = k.sb([128, 128], BF16, 'Kw'); hh = k.sb([128, TT], F32, 'hh'); sqb = k.sb([128, TT], BF16, 'sqb'); rst = k.sb([128, TT], F32, 'rst')

    def mlstm_setup(l):
        k.dma('sp', lambda g: g.dma_start(out=negbf[:], in_=W['ml_bf'][l].rearrange("(h o) -> h o", o=1)), (), (negbf,))
        k.ts(negbf[:], negbf[:], -1.0, None, ALU.mult, None, (negbf,), (negbf,))
        k.dma('sp', lambda g: g.dma_start(out=bicol[:], in_=W['ml_bi'][l].rearrange("(h o) -> h o", o=1)), (), (bicol,))
        k.dma('sp', lambda g: g.dma_start(out=gncol[:], in_=W['ml_gn'][l].rearrange("(p o) -> p o", o=1)), (), (gncol,))

    def mlstm_zero_state():
        k.memset(Cst, Cst[:], 0.0); k.memset(CsTb, CsTb[:], 0.0); k.memset(nselb, nselb[:], 0.0)
        k.memset(Fm, Fm[:], 0.0); k.memset(Mm, Mm[:], 0.0); k.memset(Mpe_bc, Mpe_bc[:], 0.0)

    def mlstm_inproj(l, P, ntok):
        Win = W['w_in'][l]

        def c_mq(ps_, j, m):
            k.cp(mqT[:, j, 0:ntok], ps_[:, 0:ntok], (ps_,), (mqT,), e='act')
        proj_feat(hT, 8, ntok, Win, 0, C_MQ, 512, c_mq)

        def c_mk(ps_, j, m):
            k.op('act', lambda g: g.mul(out=mkT[:, j, 0:ntok], in_=ps_[:, 0:ntok], mul=128 ** -0.5), (ps_,), (mkT,))
        proj_feat(hT, 8, ntok, Win, 0, C_MK, 512, c_mk)

        def c_mkt(ps_, s, cb, nc_):
            k.op('act', lambda g: g.mul(out=mk_tok[0:P, :], in_=ps_[0:P, 0:512], mul=128 ** -0.5), (ps_,), (mk_tok,))
        proj_tok(hT, 8, P, 1, Win, 0, C_MK, 512, c_mkt)

        def c_mvt(ps_, s, cb, nc_):
            k.cp(VM[0:P, :, 0:128], ps_[0:P, 0:512].rearrange("p (h d) -> p h d", d=128), (ps_,), (VM,))
        proj_tok(hT, 8, P, 1, Win, 0, C_MV, 512, c_mvt)
        wb = wload(Win, 0, 1024, C_MI, 8)
        for gi, nm in enumerate(('mi', 'mf')):
            ps_ = nps()
            for kc in range(8):
                k.mm(ps_[0:4, 0:ntok], wb[:, kc, 4 * gi:4 * gi + 4], hT[:, kc, 0:ntok], kc == 0, kc == 7, (hT, wb), (ps_,))
            k.cp(G4[nm][:, 0:ntok], ps_[0:4, 0:ntok], (ps_,), (G4[nm],))

        def c_mo(ps_, j, m):
            k.act(so4[:, j, 0:ntok], ps_[:, 0:ntok], AF.Sigmoid, (ps_,), (so4,))
        proj_feat(hT, 8, ntok, Win, 0, C_MO, 512, c_mo)

    def mlstm_gates(ntok, scan_mask=None, a_override=None):
        g = G4
        k.act(g['lf'][:, 0:ntok], g['mf'][:, 0:ntok], AF.Exp, (g['mf'], negbf), (g['lf'],), bias=negbf[:], scale=-1.0)
        k.act(g['lf'][:, 0:ntok], g['lf'][:, 0:ntok], AF.Ln, (g['lf'],), (g['lf'],), bias=1.0)
        k.ts(g['lf'][:, 0:ntok], g['lf'][:, 0:ntok], -1.0, None, ALU.mult, None, (g['lf'],), (g['lf'],))
        k.ts(g['mi'][:, 0:ntok], g['mi'][:, 0:ntok], bicol[:], None, ALU.add, None, (g['mi'], bicol), (g['mi'],))

    def mlstm_prompt_tile(l):
        ntok = TT
        g = G4
        mlstm_gates(ntok)
        k.op('dve', lambda e: e.tensor_tensor_scan(out=g['F'][:, 0:ntok], data0=g['one'][:, 0:ntok], data1=g['lf'][:, 0:ntok], initial=Fm[:, 0:1],
                                                   op0=ALU.mult, op1=ALU.add), (g['one'], g['lf'], Fm), (g['F'],))
        k.tt(g['a'][:, 0:ntok], g['mi'][:, 0:ntok], g['F'][:, 0:ntok], ALU.subtract, (g['mi'], g['F']), (g['a'],))
        k.op('dve', lambda e: e.tensor_tensor_scan(out=g['M'][:, 0:ntok], data0=g['zero'][:, 0:ntok], data1=g['a'][:, 0:ntok], initial=Mm[:, 0:1],
                                                   op0=ALU.add, op1=ALU.max), (g['zero'], g['a'], Mm), (g['M'],))
        k.tt(g['emt'][:, 0:ntok], g['F'][:, 0:ntok], g['M'][:, 0:ntok], ALU.add, (g['F'], g['M']), (g['emt'],))
        k.act(g['emt'][:, 0:ntok], g['emt'][:, 0:ntok], AF.Exp, (g['emt'],), (g['emt'],), scale=-1.0)
        k.ts(g['negM'][:, 0:ntok], g['M'][:, 0:ntok], -1.0, None, ALU.mult, None, (g['M'],), (g['negM'],))
        ps_ = nps()
        k.tr(ps_[:, 0:4], g['a'][0:4, 0:ntok], identf[0:4, 0:4], (g['a'], identf), (ps_,))
        k.cp(atok[:], ps_[:, 0:4], (ps_,), (atok,))
        k.ts(diag4[:], identf[0:4, 0:4], g['M'][:, ntok - 1:ntok], None, ALU.mult, None, (identf, g['M']), (diag4,))
        ps_ = nps()
        k.mm(ps_[:, 0:4], onesf[0:4, 0:128], diag4[:], True, True, (onesf, diag4), (ps_,))
        k.cp(Me_bc[:], ps_[:, 0:4], (ps_,), (Me_bc,))
        k.tt(dec_bc[:], Mpe_bc[:], Me_bc[:], ALU.subtract, (Mpe_bc, Me_bc), (dec_bc,))
        k.act(dec_bc[:], dec_bc[:], AF.Exp, (dec_bc,), (dec_bc,))
        k.tt(wend[:], atok[:], Me_bc[:], ALU.subtract, (atok, Me_bc), (wend,))
        k.act(wend[:], wend[:], AF.Exp, (wend,), (wend,))
        for h in range(4):
            psn = nps()
            k.mm(psn[:, 0:ntok], selh[h][:], g['negM'][:, 0:ntok], True, True, (selh[h], g['negM']), (psn,))
            k.act(iwb[:, 0:ntok], psn[:, 0:ntok], AF.Exp, (psn, Mpe_bc), (iwb,), bias=Mpe_bc[:, h:h + 1])
            k.tt(QpT[:, 0:ntok], mqT[:, h, 0:ntok], iwb[:, 0:ntok], ALU.mult, (mqT, iwb), (QpT,))
            k.ts(Et[:, 0:ntok], psn[:, 0:ntok], atok[:, h:h + 1], 0.0, ALU.add, ALU.min, (psn, atok), (Et,))
            k.act(Et[:, 0:ntok], Et[:, 0:ntok], AF.Exp, (Et,), (Et,))
            k.tt(Et[:, 0:ntok], Et[:, 0:ntok], trif[:, 0:ntok], ALU.mult, (Et, trif), (Et,))
            pss = nps()
            k.mm(pss[:, 0:ntok], mkT[:, h, 0:ntok], mqT[:, h, 0:ntok], True, True, (mkT, mqT), (pss,))
            k.tt(SWb[:, 0:ntok], pss[:, 0:ntok], Et[:, 0:ntok], ALU.mult, (pss, Et), (SWb,))
            psnum = nps()
            k.mm(psnum[:, 0:ntok], VM[:, h, 0:128], SWb[:, 0:ntok], True, False, (VM, SWb), (psnum,))
            k.mm(psnum[:, 0:ntok], CsTb[:, h, :], QpT[:, 0:ntok], False, True, (CsTb, QpT), (psnum,))
            k.cp(arena_f[:, h * TT:h * TT + ntok], psnum[:, 0:ntok], (psnum,), (arena,), e='act')
            k.mm(psden[0:4, 0:ntok], onesel[:, h, :], SWb[:, 0:ntok], h == 0, False, (onesel, SWb), (psden,))
            k.mm(psden[0:4, 0:ntok], nselb[:, h, :], QpT[:, 0:ntok], False, h == 3, (nselb, QpT), (psden,))
            k.ts(Kw[:], mk_tok[:, h * 128:(h + 1) * 128], wend[:, h:h + 1], None, ALU.mult, None, (mk_tok, wend), (Kw,))
            psd = nps()
            k.mm(psd[:, 0:129], Kw[:], VM[:, h, :], True, True, (Kw, VM), (psd,))
            k.stt(Cst[:, h, :], Cst[:, h, :], dec_bc[:, h:h + 1], psd[:, 0:129], ALU.mult, ALU.add, (Cst, dec_bc, psd), (Cst,))
            k.cp(CsTb[:, h, :], Cst[:, h, 0:128], (Cst,), (CsTb,))
            k.cp(nselb[:, h, h:h + 1], Cst[:, h, 128:129], (Cst,), (nselb,))
        mlstm_finish(ntok)
        k.cp(Fm[:], g['F'][:, ntok - 1:ntok], (g['F'],), (Fm,))
        k.cp(Mm[:], g['M'][:, ntok - 1:ntok], (g['M'],), (Mm,))
        k.cp(Mpe_bc[:], Me_bc[:], (Me_bc,), (Mpe_bc,))

    def mlstm_finish(ntok):
        g = G4
        k.act(g['rden'][:, 0:ntok], psden[0:4, 0:ntok], AF.Abs, (psden,), (g['rden'],))
        k.tt(g['rden'][:, 0:ntok], g['rden'][:, 0:ntok], g['emt'][:, 0:ntok], ALU.max, (g['rden'], g['emt']), (g['rden'],))
        k.op('dve', lambda e: e.reciprocal(out=g['rden'][:, 0:ntok], in_=g['rden'][:, 0:ntok]), (g['rden'],), (g['rden'],))
        for h in range(4):
            psr = nps()
            k.mm(psr[:, 0:ntok], selh[h][:], g['rden'][:, 0:ntok], True, True, (selh[h], g['rden']), (psr,))
            k.tt(hh[:, 0:ntok], arena_f[:, h * TT:h * TT + ntok], psr[:, 0:ntok], ALU.mult, (arena, psr), (hh,))
            k.act(sqb[:, 0:ntok], hh[:, 0:ntok], AF.Square, (hh,), (sqb,))
            ps2 = nps()
            k.mm(ps2[:, 0:ntok], onesb[:, :], sqb[:, 0:ntok], True, True, (onesb, sqb), (ps2,))
            k.ts(rst[:, 0:ntok], ps2[:, 0:ntok], 1.0 / 128, EPS, ALU.mult, ALU.add, (ps2,), (rst,))
            k.act(rst[:, 0:ntok], rst[:, 0:ntok], AF.Sqrt, (rst,), (rst,))
            k.op('dve', lambda e: e.reciprocal(out=rst[:, 0:ntok], in_=rst[:, 0:ntok]), (rst,), (rst,))
            k.tt(hh[:, 0:ntok], hh[:, 0:ntok], rst[:, 0:ntok], ALU.mult, (hh, rst), (hh,))
            k.stt(yT['ml'][:, h, 0:ntok], hh[:, 0:ntok], gncol[:, 0:1], so4[:, h, 0:ntok], ALU.mult, ALU.mult, (hh, gncol, so4), (yT['ml'],))

    def mlstm_prompt_out(l):
        for h in range(4):
            ps_ = nps()
            k.tr(ps_[:, 0:128], Cst[:, h, 0:128], identf[:, :], (Cst, identf), (ps_,))
            k.cp(rowt[:, 0:128], ps_[:, 0:128], (ps_,), (rowt,), e='act')
            k.dma('sp', lambda g: g.dma_start(out=O['pmlC'][l, h], in_=rowt[:, 0:128]), (rowt,), (), True)
            k.dma('sp', lambda g: g.dma_start(out=O['pmln'][l, h].rearrange("(p o) -> p o", o=1), in_=Cst[:, h, 128:129]), (Cst,), (), True)
        k.tt(G4['rden'][:, 0:1], Fm[:], Mm[:], ALU.add, (Fm, Mm), (G4['rden'],))
        k.dma('sp', lambda g: g.dma_start(out=O['pmlm'][l].rearrange("(h o) -> h o", o=1), in_=G4['rden'][:, 0:1]), (G4['rden'],), (), True)

    def merge_out(l, P, ntok):
        Win = W['w_in'][l]
        Gt = arena[:, :, :].rearrange("p a t -> p (a t)")

        def cons_g(ps_, s, cb, nc_):
            k.act(Gt[0:P, cb:cb + nc_], ps_[0:P, 0:nc_], AF.Sigmoid, (ps_,), (arena,))
        proj_tok(hT, 8, P, 1, Win, 0, C_G, 3072, cons_g)
        for cb in range(2):
            cs = slice(cb * 512, (cb + 1) * 512)
            wb = wload96(W['w_br_s5'][l], cb * 512, 512)
            ps_ = nps()
            for tq in range(6):
                nr = 96 if tq < 5 else 32
                k.mm(ps_[0:P, :], ys5T[0:nr, tq, 0:P], wb[0:nr, tq, :], tq == 0, tq == 5, (ys5T, wb), (ps_,))
            k.tt(mrgf[0:P, cs], ps_[0:P, :], Gt[0:P, cb * 512:(cb + 1) * 512], ALU.mult, (ps_, arena), (mrgf,))
            wb = wload(W['w_br_fox'][l], 0, 512, cb * 512, 512, p=64)
            ps_ = nps()
            for h in range(8):
                k.mm(ps_[0:P, :], yfx[:, h, 0:P], wb[0:64, h, :], h == 0, h == 7, (yfx, wb), (ps_,))
            k.tt(rowt[0:P, :], ps_[0:P, :], Gt[0:P, 1024 + cb * 512:1024 + (cb + 1) * 512], ALU.mult, (ps_, arena), (rowt,))
            k.tt(mrgf[0:P, cs], mrgf[0:P, cs], rowt[0:P, :], ALU.add, (mrgf, rowt), (mrgf,))
            wb = wload(W['w_br_ml'][l], 0, 512, cb * 512, 512)
            ps_ = nps()
            for h in range(4):
                k.mm(ps_[0:P, :], yT['ml'][:, h, 0:P], wb[:, h, :], h == 0, h == 3, (yT['ml'], wb), (ps_,))
            k.tt(rowt[0:P, :], ps_[0:P, :], Gt[0:P, 2048 + cb * 512:2048 + (cb + 1) * 512], ALU.mult, (ps_, arena), (rowt,))
            k.tt(mrgf[0:P, cs], mrgf[0:P, cs], rowt[0:P, :], ALU.add, (mrgf, rowt), (mrgf,))
        k.cp(tmpb[0:P, :], mrgf[0:P, :], (mrgf,), (tmpb,))
        transpose_to_T(tmpb, P, 8, hT, 0, lambda c: tmpb[0:P, c * 128:(c + 1) * 128])

        def cons_o(ps_, s, cb, nc_):
            k.tt(xres[0:P, 0, cb:cb + nc_], xres[0:P, 0, cb:cb + nc_], ps_[0:P, 0:nc_], ALU.add, (xres, ps_), (xres,))
        proj_tok(hT, 8, P, 1, W['w_out'][l], 0, 0, D, cons_o)

    def s5_inproj(l, ntok):
        def c_u(ps_, j, m):
            k.cp(uT[0:m, j, 0:ntok], ps_[0:m, 0:ntok], (ps_,), (uT,), e='act')
        proj_feat(hT, 8, ntok, W['w_in'][l], 0, C_S5, 512, c_u, chunk=96)

    BIG = 1.0e30

    def carve(parent, off, shape, dt):
        pt_ = parent.t
        pshape = list(pt_.shape)
        n0 = 1
        for d_ in pshape[1:]:
            n0 *= d_
        flat = pt_.reshape([pshape[0], n0]) if len(pshape) > 2 else pt_
        esz = mybir.dt.size(pt_.dtype)
        n = 1
        for d_ in shape[1:]:
            n *= d_
        nbytes = n * mybir.dt.size(dt)
        assert off % esz == 0 and nbytes % esz == 0 and (off + nbytes) <= n0 * esz and shape[0] <= pshape[0]
        ap = flat[0:shape[0], off // esz:(off + nbytes) // esz]
        if dt != pt_.dtype:
            ap = ap.bitcast(dt)
        if len(shape) == 3:
            ap = ap.rearrange("p (a b) -> p a b", b=shape[2])
        elif len(shape) == 4:
            ap = ap.rearrange("p (a b c) -> p a b c", b=shape[2], c=shape[3])
        return ap

    def sample_group():
        P = ntok = NS
        GB = []
        for par in range(2):
            GB.append(dict(tok=kbuf[par], gK=carve(kbuf[par], 0, [128, 512], F32), gV=carve(kbuf[par], 2048, [128, 512], F32),
                           gL=carve(kbuf[par], 4096, [128, 8], F32)))
        kpg = carve(vbuf[0], 0, [64, 8, 128], BF16); vpg = carve(vbuf[0], 2048, [128, 8, 65], BF16)
        k.op('pool', lambda g: g.memset(vpg, 1.0), (), (vbuf[0],))
        Et2 = carve(vbuf[1], 0, [128, 32], F32); PT2 = carve(vbuf[1], 128, [128, 8, 4], BF16)
        nb8 = carve(vbuf[1], 256, [128, 8], F32); Rc = carve(vbuf[1], 288, [128, 8], F32)
        Oacc = carve(Cst, 0, [65, 8, NS], F32)
        C0 = carve(CsTb, 0, [128, 128], F32); C0T = carve(CsTb, 512, [128, 128], BF16)
        X0r = carve(mrgf, 0, [128, 16, 16], F32); X0i = carve(mrgf, 1024, [128, 16, 16], F32)
        X0br = carve(mrgf, 2048, [128, 16, 16], BF16); X0bi = carve(mrgf, 2560, [128, 16, 16], BF16)
        n0tok = carve(junkb, 0, [16, 512], F32)
        ptb = carve(rowt, 0, [128, SB * NPAGE], I32)
        idx_all = k.sb([128, SB * NPAGE], I32, 'idx_all'); iot = k.sb([128, 1], I32, 'iot')
        E16 = k.sb([16, NS], F32, 'E16'); ETf = k.sb([NS, 16], F32, 'ETf'); blkF = k.sb([NS, NS], F32, 'blkF'); blkB = k.sb([NS, NS], BF16, 'blkB')
        ones8 = k.sb([8, 128], F32, 'ones8'); k.memset(ones8, ones8[:], 1.0)
        sel8buf = k.sb([8, 128], F32, 'sel8buf')
        eD = k.sb([8, NS], F32, 'eD'); negDk = k.sb([NS, 8], F32, 'negDk')
        S4 = {n_: k.sb([4, NS], F32, 's4_' + n_) for n_ in ('m0full', 'Mefull', 'iw4', 'a2')}
        m0T = k.sb([4, 16], F32, 'm0T'); dec4 = k.sb([4, 16], F32, 'dec4'); mnew4 = k.sb([4, 16], F32, 'mnew4'); dd4 = k.sb([4, 64], F32, 'dd4')
        Metok = k.sb([NS, 4], F32, 'Metok'); decT = k.sb([16, 4], F32, 'decT'); decbc = k.sb([128, 64], F32, 'decbc')
        n0T = k.sb([128, 64], F32, 'n0T'); n0sel = k.sb([128, 16, 4, 4], BF16, 'n0sel'); k.memset(n0sel, n0sel[:], 0.0)
        Wm = k.sb([NS, 16], F32, 'Wm'); Wmb = k.sb([NS, 16], BF16, 'Wmb'); Vw = k.sb([NS, 128], BF16, 'Vw')
        rmaskF, reset4 = G4['one'], G4['zero']

        k.memset(E16, E16[:], 1.0)
        k.op('pool', lambda g: g.affine_select(out=E16[:], in_=E16[:], pattern=[[1, NS]], compare_op=ALU.is_ge, fill=0.0, base=0, channel_multiplier=-ST), (E16,), (E16,))
        k.op('pool', lambda g: g.affine_select(out=E16[:], in_=E16[:], pattern=[[-1, NS]], compare_op=ALU.is_ge, fill=0.0, base=ST - 1, channel_multiplier=ST), (E16,), (E16,))
        ps_ = nps()
        k.mm(ps_[0:NS, 0:NS], E16[:, :], E16[:, :], True, True, (E16,), (ps_,))
        k.tt(blkF[:], ps_[0:NS, 0:NS], trif[0:NS, 0:NS], ALU.mult, (ps_, trif), (blkF,))
        k.cp(blkB[:], blkF[:], (blkF,), (blkB,))
        ps_ = nps()
        k.tr(ps_[0:NS, 0:16], E16[:, :], identf[0:16, 0:16], (E16, identf), (ps_,))
        k.cp(ETf[:], ps_[0:NS, 0:16], (ps_,), (ETf,))
        k.dma('sp', lambda g: g.dma_start(out=ptb, in_=I['pt'][0].partition_broadcast(128)), (), (rowt,))
        k.op('pool', lambda g: g.iota(iot[:], pattern=[[0, 1]], base=0, channel_multiplier=1), (), (iot,))
        k.ts(idx_all[:], ptb, PAGE, iot[:], ALU.mult, ALU.add, (rowt, iot), (idx_all,))
        k.memset(rmaskF, rmaskF[:], 1.0); k.memset(rmaskF, rmaskF[:, 0:NS].rearrange("p (b t) -> p b t", t=ST)[:, :, 0], 0.0)
        k.memset(reset4, reset4[:], BIG); k.memset(reset4, reset4[:, 0:NS].rearrange("p (b t) -> p b t", t=ST)[:, :, 0], -BIG)

        k.dma('sp', lambda g: g.dma_start(out=xres[0:NS, 0, :], in_=I['xs'][:, :]), (), (xres,))

        def fox_sample(l):
            Win = W['w_in'][l]

            def cons_ff(ps_, s, cb, nc_):
                fox_gates(ps_, P, None, O['sflf'][l, :, :])
                p1 = nps()
                k.mm(p1[0:P, 0:8], blkF[:, :], lf8[0:P, :], True, True, (blkF, lf8), (p1,))
                k.cp(F8[0:P, :], p1[0:P, 0:8], (p1,), (F8,))
                k.ts(negDk[:], F8[0:P, :], -1.0, -8.0, ALU.mult, ALU.add, (F8,), (negDk,))
                aqv = AQ[:, :].rearrange("p (h c) -> p h c", c=64)
                k.cp(aqv[0:P, :, 0], F8[0:P, :], (F8,), (AQ,))
                k.cp(t8[0:P, :], aqv[0:P, :, 0], (AQ,), (t8,))
                k.tt(aqv[0:P, :, 1], F8[0:P, :], t8[0:P, :], ALU.subtract, (F8, t8), (AQ,))
                transpose_heads(AQ, P, AQTh, 0, lambda c: AQ[0:P, c * 128:(c + 1) * 128], nrows=32)
                p2 = nps()
                k.tr(p2[0:8, 0:P], F8[0:P, 0:8], identf[0:P, 0:P], (F8, identf), (p2,))
                k.act(eD[:, 0:P], p2[0:8, 0:P], AF.Exp, (p2,), (eD,))
            proj_tok(hT, 8, P, 1, Win, 0, C_FF, 8, cons_ff)

            def cons_fq(ps_, s, cb, nc_):
                head_rms(ps_, P, 8, 64, gvb['fgq'], tmpb, tmpb[0:P, 0:512].rearrange("p (h d) -> p h d", d=64), 0.125)
                transpose_heads(tmpb, P, QTh, 0, lambda c: tmpb[0:P, c * 128:(c + 1) * 128])
            proj_tok(hT, 8, P, 1, Win, 0, C_FQ, 512, cons_fq)

            def cons_fk(ps_, s, cb, nc_):
                head_rms(ps_, P, 8, 64, gvb['fgk'], kout, kout[0:P, 0:512].rearrange("p (h d) -> p h d", d=64), 1.0)
                k.dma('sp', lambda g: g.dma_start(out=O['sfk'][l, :, :], in_=kout[0:P, 0:512]), (kout,), (), True)
                k.cp(tmpb[0:P, 0:512], kout[0:P, 0:512], (kout,), (tmpb,))
                transpose_heads(tmpb, P, kst, 0, lambda c: tmpb[0:P, c * 128:(c + 1) * 128])
            proj_tok(hT, 8, P, 1, Win, 0, C_FK, 512, cons_fk)

            def cons_fv(ps_, s, cb, nc_):
                k.cp(rowt[0:P, 0:512], ps_[0:P, 0:512], (ps_,), (rowt,), e='act')
                k.dma('sp', lambda g: g.dma_start(out=O['sfv'][l, :, :], in_=rowt[0:P, 0:512]), (rowt,), (), True)
                k.cp(vst[0:P, :, 0:64], rowt[0:P, 0:512].rearrange("p (h d) -> p h d", d=64), (rowt,), (vst,))
            proj_tok(hT, 8, P, 1, Win, 0, C_FV, 512, cons_fv)

            if l > 0:
                k.ts(idx_all[:], idx_all[:], NPOOL * PAGE, None, ALU.add, None, (idx_all,), (idx_all,))
            ckf = I['ck'].rearrange("l n c -> (l n) c"); cvf = I['cv'].rearrange("l n c -> (l n) c"); clff = I['clf'].rearrange("l n c -> (l n) c")
            k.op('pool', lambda g: g.memset(Oacc, 0.0), (), (Cst,))
            cnt = 0
            for b in range(SB):
                k.op('pool', lambda g: g.memset(Rc, 0.0), (), (vbuf[1],))
                for j in range(NPAGE - 1, -1, -1):
                    G = GB[cnt % 2]; cnt += 1
                    col = b * NPAGE + j
                    off = bass.IndirectOffsetOnAxis(ap=idx_all[:, col:col + 1], axis=0)
                    k.dma('pool', lambda g: g.indirect_dma_start(out=G['gK'], out_offset=None, in_=ckf, in_offset=off), (idx_all,), (G['tok'],))
                    k.dma('pool', lambda g: g.indirect_dma_start(out=G['gV'], out_offset=None, in_=cvf, in_offset=off), (idx_all,), (G['tok'],))
                    k.dma('pool', lambda g: g.indirect_dma_start(out=G['gL'], out_offset=None, in_=clff, in_offset=off), (idx_all,), (G['tok'],))
                    p1 = nps()
                    k.mm(p1[:, 0:8], triR[:, :], G['gL'], True, True, (triR, G['tok']), (p1,))
                    k.stt(nb8, p1[:, 0:8], 1.0, Rc, ALU.mult, ALU.add, (p1, vbuf[1]), (vbuf[1],))
                    k.ts(nb8, nb8, -8.0, None, ALU.add, None, (vbuf[1],), (vbuf[1],))
                    p2 = nps()
                    k.mm(p2[:, 0:8], onesf[:, :], G['gL'], True, True, (onesf, G['tok']), (p2,))
                    k.tt(Rc, Rc, p2[:, 0:8], ALU.add, (vbuf[1], p2), (vbuf[1],))
                    k.cp(tmpb[:, 0:512], G['gK'], (G['tok'],), (tmpb,), e='act')
                    transpose_heads(tmpb, 128, vbuf[0], 0, lambda c: tmpb[:, c * 128:(c + 1) * 128], dst_ap=kpg)
                    k.cp(vpg[:, :, 0:64], G['gV'].rearrange("p (h d) -> p h d", d=64), (G['tok'],), (vbuf[0],))
                    ps_ = nps()
                    for h in range(8):
                        k.mm(ps_[:, h * 4:(h + 1) * 4], kpg[0:64, h, :], QTh[0:64, h, ST * b:ST * b + ST], True, True, (vbuf[0], QTh), (ps_,))
                    k.tt(Et2.rearrange("p (h q) -> p h q", q=ST), ps_[:, 0:32].rearrange("p (h q) -> p h q", q=ST),
                         nb8.unsqueeze(2).to_broadcast([128, 8, ST]), ALU.add, (ps_, vbuf[1]), (vbuf[1],))
                    k.act(PT2.rearrange("p h q -> p (h q)"), Et2, AF.Exp, (vbuf[1],), (vbuf[1],))
                    pso = nps()
                    for h in range(8):
                        k.mm(pso[0:65, h * 4:(h + 1) * 4], vpg[:, h, :], PT2[:, h, :], True, True, (vbuf[0], vbuf[1]), (pso,))
                    k.tt(Oacc[:, :, ST * b:ST * b + ST], Oacc[:, :, ST * b:ST * b + ST], pso[0:65, 0:32].rearrange("p (h q) -> p h q", q=ST),
                         ALU.add, (Cst, pso), (Cst,))
            for h in range(8):
                ps_ = nps()
                k.mm(ps_[0:P, 0:P], kst[0:64, h, 0:P], QTh[0:64, h, 0:P], True, False, (kst, QTh), (ps_,))
                k.mm(ps_[0:P, 0:P], onesb[0:2, 0:P], AQTh[0:2, h, 0:P], False, True, (onesb, AQTh), (ps_,))
                k.act(pTf[0:P, 0, 0:P], ps_[0:P, 0:P], AF.Exp, (ps_, negDk), (pTf,), bias=negDk[:, h:h + 1])
                k.tt(pTf[0:P, 0, 0:P], pTf[0:P, 0, 0:P], blkB[:, :], ALU.mult, (pTf, blkB), (pTf,))
                pso = ps_acc[h % 2]
                k.mm(pso[0:65, 0:P], vst[0:P, h, :], pTf[0:P, 0, 0:P], True, True, (vst, pTf), (pso,))
                psr = nps()
                k.op('pool', lambda g: g.affine_select(out=sel8buf[:], in_=ones8[:], pattern=[[0, 128]], compare_op=ALU.is_equal,
                                                       fill=0.0, base=-h, channel_multiplier=1), (ones8,), (sel8buf,))
                k.mm(psr[0:65, 0:P], sel8buf[:, 0:65], eD[:, 0:P], True, True, (sel8buf, eD), (psr,))
                k.tt(oTf[:, 0:P], Oacc[:, h, :], psr[0:65, 0:P], ALU.mult, (Cst, psr), (oTf,))
                k.tt(oTf[:, 0:P], oTf[:, 0:P], pso[0:65, 0:P], ALU.add, (oTf, pso), (oTf,))
                fox_finish_head(h, None, P)

        def s5_sample(l):
            L, nj = ST, SB
            s5_inproj(l, ntok)
            stb = nwb()
            stg = carve(stb, 0, [16, 2048], F32)
            for (src_, X0x) in ((I['s5re'][l], X0r), (I['s5im'][l], X0i)):
                k.dma('sp', lambda g: g.dma_start(out=stg, in_=src_.rearrange("b g p -> b (g p)")), (), (stb,))
                for q4 in range(4):
                    pt_ = nps()
                    for i in range(4):
                        q = q4 * 4 + i
                        k.tr(pt_[:, i * 16:(i + 1) * 16], stg[0:16, q * 128:(q + 1) * 128], identf[0:16, 0:16], (stb, identf), (pt_,))
                    k.cp(X0x[:, q4 * 4:(q4 + 1) * 4, :], pt_[:, 0:64].rearrange("p (a b) -> p a b", b=16), (pt_,), (mrgf,))
            k.cp(X0br, X0r, (mrgf,), (mrgf,)); k.cp(X0bi, X0i, (mrgf,), (mrgf,))
            psS = [nps(), nps()]
            for q in range(16):
                po, tq = 32 * (q % 3), q // 3
                for ri in range(2):
                    for sp in range(L):
                        k.op('pe', lambda g: g.matmul(psS[ri][:, q * nj:(q + 1) * nj], W1[po:po + 32, L - 1 - sp, ri, tq, :],
                                                      uT[po:po + 32, tq, sp:ntok:L], start=(sp == 0), stop=(sp == L - 1)), (S5T, uT), (psS[ri],))
            bc16 = lambda a: a.unsqueeze(2).to_broadcast([128, 16, 16])
            rd = (S5T, mrgf, psS[0], psS[1])
            def mop(fn):
                k.op('dve', fn, rd, (S5T,))
            mop(lambda g: g.tensor_tensor(out=M1[:], in0=X0r, in1=bc16(pwr[:, 4, :]), op=ALU.mult))
            mop(lambda g: g.tensor_tensor(out=M3[:], in0=X0i, in1=bc16(pwi[:, 4, :]), op=ALU.mult))
            mop(lambda g: g.tensor_tensor(out=M1[:], in0=M1[:], in1=M3[:], op=ALU.subtract))
            mop(lambda g: g.tensor_tensor(out=M2[:], in0=X0r, in1=bc16(pwi[:, 4, :]), op=ALU.mult))
            mop(lambda g: g.tensor_tensor(out=M3[:], in0=X0i, in1=bc16(pwr[:, 4, :]), op=ALU.mult))
            mop(lambda g: g.tensor_tensor(out=M2[:], in0=M2[:], in1=M3[:], op=ALU.add))
            mop(lambda g: g.tensor_tensor(out=M1[:], in0=M1[:], in1=psS[0][:, 0:256].rearrange("p (q j) -> p q j", j=nj), op=ALU.add))
            mop(lambda g: g.tensor_tensor(out=M2[:], in0=M2[:], in1=psS[1][:, 0:256].rearrange("p (q j) -> p q j", j=nj), op=ALU.add))
            for (dst_, Mx) in ((O['ss5re'][l], M1), (O['ss5im'][l], M2)):
                for q4 in range(4):
                    pt_ = nps()
                    for i in range(4):
                        q = q4 * 4 + i
                        k.tr(pt_[0:16, i * 128:(i + 1) * 128], Mx[:, q, :], identf[:, :], (S5T, identf), (pt_,))
                    k.cp(stg[0:16, q4 * 512:(q4 + 1) * 512], pt_[0:16, 0:512], (pt_,), (stb,), e='act')
                k.dma('sp', lambda g: g.dma_start(out=dst_.rearrange("b g p -> b (g p)"), in_=stg), (stb,), (), True)
            s5_outputs(l, ntok, L, nj, X0br, X0bi, None, xtok=mrgf)

        def mlstm_sample(l):
            g = G4
            s4 = S4
            mlstm_inproj(l, P, ntok)
            mlstm_gates(ntok)
            k.dma('sp', lambda e: e.dma_start(out=m0T[:], in_=I['mlm'][l].rearrange("b h -> h b"), allow_slow_non_contiguous=True), (), (m0T,))
            v3 = lambda a: a[:, 0:ntok].rearrange("p (b t) -> p b t", t=ST)
            k.cp(v3(s4['m0full']), m0T[:, :].unsqueeze(2).to_broadcast([4, SB, ST]), (m0T,), (s4['m0full'],))
            k.op('dve', lambda e: e.tensor_tensor_scan(out=g['F'][:, 0:ntok], data0=rmaskF[:, 0:ntok], data1=g['lf'][:, 0:ntok], initial=0.0,
                                                       op0=ALU.mult, op1=ALU.add), (rmaskF, g['lf']), (g['F'],))
            k.tt(g['a'][:, 0:ntok], g['mi'][:, 0:ntok], g['F'][:, 0:ntok], ALU.subtract, (g['mi'], g['F']), (g['a'],))
            k.tt(s4['a2'][:, 0:ntok], g['a'][:, 0:ntok], s4['m0full'][:, 0:ntok], ALU.max, (g['a'], s4['m0full']), (s4['a2'],))
            k.op('dve', lambda e: e.tensor_tensor_scan(out=g['M'][:, 0:ntok], data0=reset4[:, 0:ntok], data1=s4['a2'][:, 0:ntok], initial=-BIG,
                                                       op0=ALU.min, op1=ALU.max), (reset4, s4['a2']), (g['M'],))
            k.tt(g['emt'][:, 0:ntok], g['F'][:, 0:ntok], g['M'][:, 0:ntok], ALU.add, (g['F'], g['M']), (g['emt'],))
            k.cp(mnew4[:], v3(g['emt'])[:, :, ST - 1], (g['emt'],), (mnew4,))
            k.dma('sp', lambda e: e.dma_start(out=O['smlm'][l].rearrange("b h -> h b"), in_=mnew4[:], allow_slow_non_contiguous=True), (mnew4,), (), True)
            k.act(g['emt'][:, 0:ntok], g['emt'][:, 0:ntok], AF.Exp, (g['emt'],), (g['emt'],), scale=-1.0)
            k.ts(g['negM'][:, 0:ntok], g['M'][:, 0:ntok], -1.0, None, ALU.mult, None, (g['M'],), (g['negM'],))
            k.cp(v3(s4['Mefull']), v3(g['M'])[:, :, ST - 1:ST].to_broadcast([4, SB, ST]), (g['M'],), (s4['Mefull'],))
            k.tt(s4['iw4'][:, 0:ntok], s4['m0full'][:, 0:ntok], g['M'][:, 0:ntok], ALU.subtract, (s4['m0full'], g['M']), (s4['iw4'],))
            k.act(s4['iw4'][:, 0:ntok], s4['iw4'][:, 0:ntok], AF.Exp, (s4['iw4'],), (s4['iw4'],))
            k.tt(dec4[:], m0T[:], v3(g['M'])[:, :, ST - 1], ALU.subtract, (m0T, g['M']), (dec4,))
            k.act(dec4[:], dec4[:], AF.Exp, (dec4,), (dec4,))
            ps_ = nps()
            k.tr(ps_[0:ntok, 0:4], g['a'][0:4, 0:ntok], identf[0:4, 0:4], (g['a'], identf), (ps_,))
            k.cp(atok[0:ntok, :], ps_[0:ntok, 0:4], (ps_,), (atok,))
            ps_ = nps()
            k.tr(ps_[0:ntok, 0:4], s4['Mefull'][0:4, 0:ntok], identf[0:4, 0:4], (s4['Mefull'], identf), (ps_,))
            k.cp(Metok[:], ps_[0:ntok, 0:4], (ps_,), (Metok,))
            k.tt(wend[0:ntok, :], atok[0:ntok, :], Metok[:], ALU.subtract, (atok, Metok), (wend,))
            k.act(wend[0:ntok, :], wend[0:ntok, :], AF.Exp, (wend,), (wend,))
            ps_ = nps()
            k.tr(ps_[0:16, 0:4], dec4[0:4, 0:16], identf[0:4, 0:4], (dec4, identf), (ps_,))
            k.cp(decT[:], ps_[0:16, 0:4], (ps_,), (decT,))
            k.tt(dd4[:, :].rearrange("p (a b) -> p a b", b=16), identf[0:4, 0:4].unsqueeze(2).to_broadcast([4, 4, 16]),
                 dec4[:, :].unsqueeze(1).to_broadcast([4, 4, 16]), ALU.mult, (identf, dec4), (dd4,))
            ps_ = nps()
            k.mm(ps_[:, 0:64], onesf[0:4, 0:128], dd4[:, :], True, True, (onesf, dd4), (ps_,))
            k.cp(decbc[:], ps_[:, 0:64], (ps_,), (decbc,))
            k.dma('sp', lambda e: e.dma_start(out=n0T[:], in_=I['mln'][l].rearrange("b h d -> d (b h)"), allow_slow_non_contiguous=True), (), (n0T,))
            for h in range(4):
                k.cp(n0sel[:, :, h, h], n0T[:, :].rearrange("p (b h) -> p b h", h=4)[:, :, h], (n0T,), (n0sel,))
            k.dma('sp', lambda e: e.dma_start(out=n0tok, in_=I['mln'][l].rearrange("b h d -> b (h d)")), (), (junkb,))
            for h in range(4):
                psn = nps()
                k.mm(psn[0:ntok, 0:ntok], selh[h][:, 0:ntok], g['negM'][:, 0:ntok], True, True, (selh[h], g['negM']), (psn,))
                psi = nps()
                k.mm(psi[:, 0:ntok], selh[h][:, :], s4['iw4'][:, 0:ntok], True, True, (selh[h], s4['iw4']), (psi,))
                k.tt(QpT[:, 0:ntok], mqT[:, h, 0:ntok], psi[:, 0:ntok], ALU.mult, (mqT, psi), (QpT,))
                k.ts(Et[0:ntok, 0:ntok], psn[0:ntok, 0:ntok], atok[0:ntok, h:h + 1], 0.0, ALU.add, ALU.min, (psn, atok), (Et,))
                k.act(Et[0:ntok, 0:ntok], Et[0:ntok, 0:ntok], AF.Exp, (Et,), (Et,))
                k.tt(Et[0:ntok, 0:ntok], Et[0:ntok, 0:ntok], blkF[:, :], ALU.mult, (Et, blkF), (Et,))
                pss = nps()
                k.mm(pss[0:ntok, 0:ntok], mkT[:, h, 0:ntok], mqT[:, h, 0:ntok], True, True, (mkT, mqT), (pss,))
                k.tt(SWb[0:ntok, 0:ntok], pss[0:ntok, 0:ntok], Et[0:ntok, 0:ntok], ALU.mult, (pss, Et), (SWb,))
                psnum = ps_acc[0]
                k.mm(psnum[:, 0:ntok], VM[0:ntok, h, 0:128], SWb[0:ntok, 0:ntok], True, False, (VM, SWb), (psnum,))
                k.mm(psden[0:4, 0:ntok], onesel[0:ntok, h, :], SWb[0:ntok, 0:ntok], h == 0, False, (onesel, SWb), (psden,))
                k.ts(Wm[:], ETf[:], wend[0:ntok, h:h + 1], None, ALU.mult, None, (ETf, wend), (Wm,))
                k.cp(Wmb[:], Wm[:], (Wm,), (Wmb,))
                psn2 = ps_acc[1]
                k.mm(psn2[0:16, 0:128], Wmb[:, :], mk_tok[0:ntok, h * 128:(h + 1) * 128], True, True, (Wmb, mk_tok), (psn2,))
                k.stt(n0tok[:, h * 128:(h + 1) * 128], n0tok[:, h * 128:(h + 1) * 128], decT[:, h:h + 1], psn2[0:16, 0:128], ALU.mult, ALU.add,
                      (junkb, decT, psn2), (junkb,))
                for b in range(SB):
                    k.dma('sp', lambda e: e.dma_start(out=C0, in_=I['mlC'][l, b, h]), (), (CsTb,))
                    pt2 = nps()
                    k.tr(pt2[:, 0:128], C0, identf[:, :], (CsTb, identf), (pt2,))
                    k.cp(C0T, pt2[:, 0:128], (pt2,), (CsTb,), e='act')
                    cs = slice(ST * b, ST * b + ST)
                    k.mm(psnum[:, cs], C0T, QpT[:, cs], False, b == SB - 1, (CsTb, QpT), (psnum,))
                    k.mm(psden[0:4, cs], n0sel[:, b, h, :], QpT[:, cs], False, (h == 3 and b == SB - 1), (n0sel, QpT), (psden,))
                    k.ts(Vw[:], VM[0:ntok, h, 0:128], Wm[:, b:b + 1], None, ALU.mult, None, (VM, Wm), (Vw,))
                    psd = nps()
                    k.mm(psd[:, 0:128], Vw[:, :], mk_tok[0:ntok, h * 128:(h + 1) * 128], True, True, (Vw, mk_tok), (psd,))
                    k.stt(C0, C0, decbc[:, h * 16 + b:h * 16 + b + 1], psd[:, 0:128], ALU.mult, ALU.add, (CsTb, decbc, psd), (CsTb,))
                    k.dma('sp', lambda e: e.dma_start(out=O['smlC'][l, b, h], in_=C0), (CsTb,), (), True)
                k.cp(arena_f[:, h * TT:h * TT + ntok], psnum[:, 0:ntok], (psnum,), (arena,), e='act')
            mlstm_finish(ntok)
            k.dma('sp', lambda e: e.dma_start(out=O['smln'][l].rearrange("b h d -> b (h d)"), in_=n0tok), (junkb,), (), True)

        def cross_attn_sample(l):
            qT = cross_q(l, P, ntok)
            for b in range(SB):
                G = GB[b % 2]
                for mc in range(2):
                    k.dma('sp', lambda e: e.dma_start(out=G['gK'], in_=I['cmk'][l, b, mc * 128:(mc + 1) * 128, :]), (), (G['tok'],))
                    k.cp(tmpb[:, 0:512], G['gK'], (G['tok'],), (tmpb,), e='act')
                    transpose_to_T(tmpb, 128, 4, memK, mc * 128, lambda c: tmpb[:, c * 128:(c + 1) * 128])
                    k.dma('sp', lambda e: e.dma_start(out=G['gV'], in_=I['cmv'][l, b, mc * 128:(mc + 1) * 128, :]), (), (G['tok'],))
                    k.cp(memV[:, mc, :], G['gV'], (G['tok'],), (memV,))
                cs = slice(ST * b, ST * b + ST)
                ps_ = nps()
                for h in range(4):
                    for mc in range(2):
                        c0 = (h * 2 + mc) * 4
                        k.mm(ps_[:, c0:c0 + 4], memK[:, h, mc * 128:(mc + 1) * 128], qT[:, h, cs], True, True, (memK, qT), (ps_,))
                k.act(PT2.rearrange("p h q -> p (h q)"), ps_[:, 0:32], AF.Exp, (ps_,), (vbuf[1],), bias=-8.0)
                pso = nps()
                for h in range(4):
                    for mc in range(2):
                        k.mm(pso[:, h * 4:(h + 1) * 4], memV[:, mc, h * 128:(h + 1) * 128], PT2[:, h * 2 + mc, :], mc == 0, mc == 1, (memV, vbuf[1]), (pso,))
                k.cp(arena_f[:, 0:4 * TT].rearrange("p (h t) -> p h t", t=TT)[:, :, cs], pso[:, 0:16].rearrange("p (h q) -> p h q", q=ST), (pso,), (arena,), e='act')
                for h in range(4):
                    for mc in range(2):
                        k.mm(psden[0:4, cs], onesel[:, h, :], PT2[:, h * 2 + mc, :], h == 0 and mc == 0, h == 3 and mc == 1, (onesel, vbuf[1]), (psden,))
            cross_finish(l, P, ntok)

        for l in range(DEPTH):
            bcast_load(gvb['fgq'], W['fox_gq'][l], 64)
            bcast_load(gvb['fgk'], W['fox_gk'][l], 64)
            bcast_load(bfb, W['fox_bf'][l], 8)
            s5_setup(l)
            mlstm_setup(l)
            bcast_load(gb, W['g_mix'][l], D)
            rmsnorm_T(xres, P, 1, gb, hT, tmpb)
            fox_sample(l)
            s5_sample(l)
            mlstm_sample(l)
            merge_out(l, P, ntok)
            cross_attn_sample(l)
            mlp(l, P, ntok)
        k.dma('sp', lambda g: g.dma_start(out=O['ys'][:, :], in_=xres[0:P, 0, :]), (xres,), (), True)
        k.memset(G4['zero'], G4['zero'][:], 0.0); k.memset(G4['one'], G4['one'][:], 1.0)

    def prompt_group():
        for l in range(DEPTH):
            memory_kv(l)
            bcast_load(gvb['fgq'], W['fox_gq'][l], 64)
            bcast_load(gvb['fgk'], W['fox_gk'][l], 64)
            bcast_load(bfb, W['fox_bf'][l], 8)
            k.memset(Fcarry, Fcarry[:], 0.0)
            s5_setup(l)
            mlstm_setup(l)
            mlstm_zero_state()
            for ti in range(NT):
                r0 = ti * TT
                if l == 0:
                    k.dma('sp', lambda g: g.dma_start(out=xres[:, 0, :], in_=I['xp'][r0:r0 + 128, :]), (), (xres,))
                else:
                    k.dma('sp', lambda g: g.dma_start(out=xres[:, 0, :], in_=xscr[r0:r0 + 128, :]), (XSCR,), (xres,))
                bcast_load(gb, W['g_mix'][l], D)
                rmsnorm_T(xres, 128, 1, gb, hT, tmpb)
                fox_prompt_tile(l, ti)
                s5_inproj(l, TT)
                s5_tile_prompt(l, TT)
                mlstm_inproj(l, 128, TT)
                mlstm_prompt_tile(l)
                merge_out(l, 128, TT)
                cross_attn_prompt(l)
                mlp(l, 128, TT)
                if l == 0:
                    k.dma('sp', lambda g: g.dma_start(out=xscr[r0:r0 + 128, :], in_=xres[:, 0, :]), (xres,), (XSCR,))
                else:
                    k.dma('sp', lambda g: g.dma_start(out=O['yp'][r0:r0 + 128, :], in_=xres[:, 0, :]), (xres,), (), True)
            s5_final_state_out(l)
            mlstm_prompt_out(l)

    if full:
        sample_group()
    prompt_group()
    k.finish()
    return nc


_NC_CACHE = {}


def kernel(**inp):
    f = lambda a: np.ascontiguousarray(np.asarray(a))
    if 'nc' not in _NC_CACHE:
        _NC_CACHE['nc'] = build()
    nc = _NC_CACHE['nc']
    wnames = ['g_mix', 'w_in', 's5_a_re', 's5_a_im', 's5_log_step', 's5_b_re', 's5_b_im', 's5_c_re', 's5_c_im', 's5_d',
              's5_w_glu', 's5_b_glu', 'fox_gq', 'fox_gk', 'fox_bf', 'ml_bi', 'ml_bf', 'ml_gn', 'w_br_s5', 'w_br_fox',
              'w_br_ml', 'w_out', 'g_cross', 'w_cq', 'cross_gq', 'g_mem', 'w_mk', 'w_mv', 'cross_gk', 'w_co', 'g_mlp',
              'w_up', 'w_down']
    shared = {n: f(inp[n]) for n in wnames}
    shared['ck'] = f(inp['cache_fox_k']).reshape(DEPTH, NPOOL * PAGE, 512)
    shared['cv'] = f(inp['cache_fox_v']).reshape(DEPTH, NPOOL * PAGE, 512)
    shared['clf'] = f(inp['cache_fox_logf']).reshape(DEPTH, NPOOL * PAGE, 8)
    in_maps = []
    for c in range(NCORE):
        b = c % 4
        sl = slice(SB * c, SB * (c + 1))
        m = dict(shared)
        m['xp'] = f(inp['x_prompt'][b]); m['xs'] = f(inp['x_sample'][sl]).reshape(NS, D); m['memp'] = f(inp['mem_prompt'][b])
        m['pt'] = f(inp['page_table'][sl]).reshape(1, SB * NPAGE).astype(np.int32)
        m['s5re'] = f(inp['state_s5_re'][:, sl]); m['s5im'] = f(inp['state_s5_im'][:, sl])
        m['mlC'] = f(inp['state_mlstm_C'][:, sl]); m['mln'] = f(inp['state_mlstm_n'][:, sl]); m['mlm'] = f(inp['state_mlstm_m'][:, sl])
        m['cmk'] = f(inp['cache_mem_k'][:, sl]).reshape(DEPTH, SB, 256, 512); m['cmv'] = f(inp['cache_mem_v'][:, sl]).reshape(DEPTH, SB, 256, 512)
        in_maps.append(m)
    res = run_bass_kernel_spmd(nc, in_maps, core_ids=list(range(NCORE))).results
    P4 = range(4)
    cat_p = lambda key, shp: np.stack([res[c][key] for c in P4], axis=1).reshape(shp)
    cat_s = lambda key, shp: np.concatenate([res[c][key] for c in range(NCORE)], axis=1).reshape(shp)
    yp = np.stack([res[c]['yp'] for c in P4], axis=0)
    ys = np.concatenate([res[c]['ys'].reshape(SB, ST, D) for c in range(NCORE)], axis=0)
    outs = (
        yp, ys,
        cat_p('pfk', (DEPTH, 4, SEQ, 8, 64)), cat_p('pfv', (DEPTH, 4, SEQ, 8, 64)), cat_p('pflf', (DEPTH, 4, SEQ, 8)),
        cat_p('ps5re', (DEPTH, 4, 32, 64)), cat_p('ps5im', (DEPTH, 4, 32, 64)),
        cat_p('pmlC', (DEPTH, 4, 4, 128, 128)), cat_p('pmln', (DEPTH, 4, 4, 128)), cat_p('pmlm', (DEPTH, 4, 4)),
        cat_p('pmemk', (DEPTH, 4, 256, 4, 128)), cat_p('pmemv', (DEPTH, 4, 256, 4, 128)),
        np.concatenate([res[c]['sfk'].reshape(DEPTH, SB, ST, 8, 64) for c in range(NCORE)], axis=1),
        np.concatenate([res[c]['sfv'].reshape(DEPTH, SB, ST, 8, 64) for c in range(NCORE)], axis=1),
        np.concatenate([res[c]['sflf'].reshape(DEPTH, SB, ST, 8) for c in range(NCORE)], axis=1),
        cat_s('ss5re', (DEPTH, 128, 32, 64)), cat_s('ss5im', (DEPTH, 128, 32, 64)),
        cat_s('smlC', (DEPTH, 128, 4, 128, 128)), cat_s('smln', (DEPTH, 128, 4, 128)), cat_s('smlm', (DEPTH, 128, 4)),
    )
    return tuple(np.ascontiguousarray(o, dtype=np.float32) for o in outs)
```
